# Optimizing a Trainium2 kernel written in Bass

```python
import math
import jax, jax.numpy as jnp
from jax import lax
import numpy as np

D_MODEL = 1024
BATCH = 8
SEQ = 4096
DEPTH = 2
DEC_BATCH = 32
DEC_SEQ = 1
PAST_LEN = 16384
PAGE_SIZE = 128

N_A_LAYERS = DEPTH // 2
N_B_LAYERS = DEPTH - N_A_LAYERS
POOL_WINDOWS = (2, 4, 8, 16)
N_POOL_GROUPS = len(POOL_WINDOWS)
POOL_GROUP = D_MODEL // N_POOL_GROUPS
POOL_BUF = max(POOL_WINDOWS) - 1
N_HEADS = 16
N_KV = 4
HPG = N_HEADS // N_KV
HEAD_DIM = D_MODEL // N_HEADS
ROT_DIM = HEAD_DIM // 4
ROPE_THETA = 500000.0
L_CMP = 32
STRIDE = 16
CMP_PARTS = L_CMP // STRIDE
CMP_HIDDEN = 2 * HEAD_DIM
L_SEL = 64
SEL_RATIO = L_SEL // STRIDE
N_SEL = 16
WINDOW = 512
N_BRANCH = 3
D_FF = 4 * D_MODEL
NSA_QBLOCK = 32
RMS_EPS = 1e-6
NEG_INF = -1e30
FORCE_SCORE = 1e9

kernel_name = 'yoco_pool_nsa_step'


def rmsnorm(x, g):
    xf = x.astype(jnp.float32)
    y = xf * lax.rsqrt(jnp.mean(xf * xf, axis=-1, keepdims=True) + RMS_EPS)
    return (y * g.astype(jnp.float32)).astype(x.dtype)


def rope(x, pos):
    half = ROT_DIM // 2
    inv = ROPE_THETA ** (-jnp.arange(0, ROT_DIM, 2, dtype=jnp.float32) / ROT_DIM)
    ang = pos.astype(jnp.float32)[:, None] * inv[None, :]
    shp = (pos.shape[0],) + (1,) * (x.ndim - 3) + (half,)
    cos = jnp.cos(ang).reshape(shp)
    sin = jnp.sin(ang).reshape(shp)
    xr = x[..., :ROT_DIM].astype(jnp.float32)
    x1, x2 = xr[..., :half], xr[..., half:]
    rot = jnp.concatenate([x1 * cos - x2 * sin, x2 * cos + x1 * sin], axis=-1)
    return jnp.concatenate([rot.astype(x.dtype), x[..., ROT_DIM:]], axis=-1)


def pool_mix(u_ext, n_hist, w_pool, scale):
    n_tot = u_ext.shape[1]
    cs0 = jnp.pad(jnp.cumsum(u_ext.astype(jnp.float32), axis=1), ((0, 0), (1, 0), (0, 0)))
    idx = jnp.arange(n_hist, n_tot)
    cur = u_ext[:, n_hist:].astype(jnp.float32)
    outs = []
    for gi, w in enumerate(POOL_WINDOWS):
        sl = slice(gi * POOL_GROUP, (gi + 1) * POOL_GROUP)
        c = cs0[..., sl]
        upper = c[:, n_hist + 1:]
        lower = jnp.pad(c, ((0, 0), (w, 0), (0, 0)))[:, n_hist + 1:n_tot + 1]
        cnt = jnp.minimum(idx + 1, w).astype(jnp.float32)[None, :, None]
        outs.append((upper - lower) / cnt - cur[..., sl])
    d = jnp.stack(outs, axis=2).astype(u_ext.dtype)
    z = jnp.einsum('btgc,gcd->btgd', d, w_pool).reshape(d.shape[0], d.shape[1], D_MODEL)
    return z * scale


def sq_relu_mlp(h, g, w_up, w_down):
    a = jax.nn.relu(rmsnorm(h, g) @ w_up)
    return (a * a) @ w_down


def kv_side(h, pos, g_kv, w_kv):
    B, T, _ = h.shape
    kv = (rmsnorm(h, g_kv) @ w_kv).reshape(B, T, N_BRANCH, 2, N_KV, HEAD_DIM)
    k = rope(kv[:, :, :, 0], pos)
    kv = jnp.stack([k, kv[:, :, :, 1]], axis=3)
    return kv[:, :, 0], kv[:, :, 1], kv[:, :, 2]


def cmp_parts(rows, cmp_w1):
    B, T = rows.shape[:2]
    n_seg = T // STRIDE
    seg = rows[:, :n_seg * STRIDE].reshape(B, n_seg, STRIDE, 2, N_KV, HEAD_DIM)
    w1 = cmp_w1.reshape(2, CMP_PARTS, STRIDE, HEAD_DIM, CMP_HIDDEN)
    return jnp.einsum('bsregh,eprhk->bspegk', seg, w1)


def cmp_finish(parts, cmp_pe, cmp_w1, cmp_w2):
    n_seg = parts.shape[1]
    nc = n_seg - CMP_PARTS + 1
    w1 = cmp_w1.reshape(2, CMP_PARTS, STRIDE, HEAD_DIM, CMP_HIDDEN)
    pe = cmp_pe.reshape(CMP_PARTS, STRIDE, 2, HEAD_DIM)
    pe_part = jnp.einsum('preh,eprhk->pek', pe, w1)
    hid = parts[:, 0:nc, 0] + pe_part[0][None, None, :, None, :]
    for p in range(1, CMP_PARTS):
        hid = hid + parts[:, p:p + nc, p] + pe_part[p][None, None, :, None, :]
    hid = jax.nn.gelu(hid)
    return jnp.einsum('bnegk,ekh->bnegh', hid, cmp_w2)


def cmp_end_positions(ckv):
    return jnp.arange(ckv.shape[1]) * STRIDE + (L_CMP - 1)


def cmp_to_sel(imp, ns):
    nc = imp.shape[-1]
    padded = jnp.pad(imp, ((0, 0),) * (imp.ndim - 1) + ((CMP_PARTS - 1, SEL_RATIO * ns - nc),))
    out = jnp.zeros(imp.shape[:-1] + (ns,), imp.dtype)
    for m in range(SEL_RATIO):
        for n in range(CMP_PARTS):
            off = m - n + CMP_PARTS - 1
            out = out + padded[..., off:off + SEL_RATIO * (ns - 1) + 1:SEL_RATIO]
    return out


def to_sel_blocks(rows, ns):
    B, T = rows.shape[:2]
    x = jnp.pad(rows, ((0, 0), (0, ns * L_SEL - T), (0, 0), (0, 0), (0, 0)))
    x = x.reshape(B, ns, L_SEL, 2, N_KV, HEAD_DIM).transpose(0, 4, 1, 2, 3, 5)
    return x.reshape(B, N_KV, ns, L_SEL * 2 * HEAD_DIM)


def block_gatherer(sblk):
    B, G = sblk.shape[:2]
    def gather(idx):
        Q, K = idx.shape[2:]
        bi = jnp.arange(B)[:, None, None]
        gi = jnp.arange(G)[None, :, None]
        g = sblk[bi, gi, idx.reshape(B, G, Q * K)]
        return g.reshape(B, G, Q, K * L_SEL, 2, HEAD_DIM)
    return gather


def paged_gatherer(pool, page_table, new_rows):
    DB, S = new_rows.shape[:2]
    bpp = PAGE_SIZE // L_SEL
    nb_past = PAST_LEN // L_SEL
    n_tail = -(-S // L_SEL)
    tail = jnp.pad(new_rows, ((0, 0), (0, n_tail * L_SEL - S), (0, 0), (0, 0), (0, 0)))
    tail = tail.reshape(DB, n_tail, L_SEL, 2, N_KV, HEAD_DIM)
    def gather(idx):
        Q, K = idx.shape[2:]
        bi = jnp.arange(DB)[:, None, None, None, None]
        gi = jnp.arange(N_KV)[None, :, None, None, None]
        r = jnp.arange(L_SEL)
        jp = jnp.minimum(idx, nb_past - 1)
        phys = page_table[bi[..., 0], jp // bpp]
        rows = (jp % bpp)[..., None] * L_SEL + r
        past = pool[phys[..., None], rows, :, gi, :]
        jt = jnp.clip(idx - nb_past, 0, n_tail - 1)
        tl = tail[bi, jt[..., None], r, :, gi, :]
        out = jnp.where((idx >= nb_past)[..., None, None, None], tl, past)
        return out.reshape(DB, N_KV, Q, K * L_SEL, 2, HEAD_DIM)
    return gather


def nsa_core(q, gate, qpos, ckv, c_end, ns, gather_sel, wkv, wpos):
    dt = q.dtype
    s = jnp.einsum('bqgjd,bngd->bgjqn', q, ckv[:, :, 0]).astype(jnp.float32)
    cmask = c_end[None, :] <= qpos[:, None]
    p_c = jax.nn.softmax(jnp.where(cmask, s, NEG_INF), axis=-1) * cmask
    o_c = jnp.einsum('bgjqn,bngd->bqgjd', p_c.astype(dt), ckv[:, :, 1])
    imp = cmp_to_sel(p_c.sum(axis=2), ns)
    blk = jnp.arange(ns)[None, :]
    cur = (qpos // L_SEL)[:, None]
    valid = blk <= cur
    forced = (blk == 0) | (blk == cur) | (blk == cur - 1)
    score = jnp.where(forced, FORCE_SCORE, jnp.where(valid, imp, -1.0))
    vals, idx = lax.top_k(score, min(N_SEL, ns))
    kv_sel = gather_sel(idx)
    kpos = idx[..., None] * L_SEL + jnp.arange(L_SEL)
    smask = ((vals >= 0)[..., None] & (kpos <= qpos[None, None, :, None, None])).reshape(idx.shape[:3] + (-1,))
    s = jnp.einsum('bqgjd,bgqkd->bgjqk', q, kv_sel[..., 0, :]).astype(jnp.float32)
    p_s = jax.nn.softmax(jnp.where(smask[:, :, None], s, NEG_INF), axis=-1)
    o_s = jnp.einsum('bgjqk,bgqkd->bqgjd', p_s.astype(dt), kv_sel[..., 1, :])
    s = jnp.einsum('bqgjd,bwgd->bgjqw', q, wkv[:, :, 0]).astype(jnp.float32)
    wmask = (wpos[None, :] >= 0) & (wpos[None, :] <= qpos[:, None]) & (wpos[None, :] >= qpos[:, None] - WINDOW)
    p_w = jax.nn.softmax(jnp.where(wmask, s, NEG_INF), axis=-1)
    o_w = jnp.einsum('bgjqw,bwgd->bqgjd', p_w.astype(dt), wkv[:, :, 1])
    return gate[..., 0:1] * o_c + gate[..., 1:2] * o_s + gate[..., 2:3] * o_w


def nsa_prompt_attend(q, gate, ckv, sblk, win_kv):
    B, T = q.shape[:2]
    ns = sblk.shape[2]
    c_end = cmp_end_positions(ckv)
    gather = block_gatherer(sblk)
    wpad = jnp.pad(win_kv, ((0, 0), (WINDOW, 0), (0, 0), (0, 0), (0, 0)))
    def step(i):
        s0 = i * NSA_QBLOCK
        qi = lax.dynamic_slice_in_dim(q, s0, NSA_QBLOCK, axis=1)
        gi = lax.dynamic_slice_in_dim(gate, s0, NSA_QBLOCK, axis=1)
        wi = lax.dynamic_slice_in_dim(wpad, s0, WINDOW + NSA_QBLOCK, axis=1)
        qpos = s0 + jnp.arange(NSA_QBLOCK)
        wpos = s0 - WINDOW + jnp.arange(WINDOW + NSA_QBLOCK)
        return nsa_core(qi, gi, qpos, ckv, c_end, ns, gather, wi, wpos)
    o = lax.map(step, jnp.arange(T // NSA_QBLOCK))
    return jnp.moveaxis(o, 0, 1).reshape(B, T, N_KV, HPG, HEAD_DIM)


def nsa_query(h, pos, g, w_qg):
    B, T, _ = h.shape
    a = rmsnorm(h, g) @ w_qg
    q = rope(a[..., :N_HEADS * HEAD_DIM].reshape(B, T, N_HEADS, HEAD_DIM), pos) * (HEAD_DIM ** -0.5)
    gate = jax.nn.sigmoid(a[..., N_HEADS * HEAD_DIM:].astype(jnp.float32)).astype(h.dtype)
    return q.reshape(B, T, N_KV, HPG, HEAD_DIM), gate.reshape(B, T, N_KV, HPG, N_BRANCH)


def setup_inputs(seed: int = 0) -> dict:
    key = jax.random.key(seed)
    ks = jax.random.split(key, 24)
    f32 = jnp.float32
    n_pages = PAST_LEN // PAGE_SIZE
    n_used = DEC_BATCH * n_pages
    n_pool = n_used + max(1, n_used // 4)
    win_buf = min(WINDOW, PAST_LEN)
    def nrm(k, shape, s=1.0):
        return jax.random.normal(k, shape, f32) * s
    page_table = jax.random.permutation(ks[6], n_pool)[:n_used].reshape(DEC_BATCH, n_pages).astype(jnp.int32)
    return {
        'x_prompt': nrm(ks[0], (BATCH, SEQ, D_MODEL)),
        'x_sample': nrm(ks[1], (DEC_BATCH, DEC_SEQ, D_MODEL)),
        'state_pool': nrm(ks[2], (N_A_LAYERS, DEC_BATCH, POOL_BUF, D_MODEL)),
        'cache_cmp_kv': nrm(ks[3], (n_pool, PAGE_SIZE, 2, N_KV, HEAD_DIM)),
        'cache_sel_kv': nrm(ks[4], (n_pool, PAGE_SIZE, 2, N_KV, HEAD_DIM)),
        'state_win_kv': nrm(ks[5], (DEC_BATCH, win_buf, 2, N_KV, HEAD_DIM)),
        'page_table': page_table,
        'norm_mix': 1.0 + nrm(ks[7], (DEPTH, D_MODEL), 0.05),
        'norm_ffn': 1.0 + nrm(ks[8], (DEPTH, D_MODEL), 0.05),
        'pool_w': nrm(ks[9], (N_A_LAYERS, N_POOL_GROUPS, POOL_GROUP, POOL_GROUP), POOL_GROUP ** -0.5),
        'pool_scale': 1.0 + nrm(ks[10], (N_A_LAYERS, D_MODEL), 0.1),
        'w_qg': nrm(ks[11], (N_B_LAYERS, D_MODEL, N_HEADS * HEAD_DIM + N_HEADS * N_BRANCH), D_MODEL ** -0.5),
        'w_o': nrm(ks[12], (N_B_LAYERS, N_HEADS * HEAD_DIM, D_MODEL), (N_HEADS * HEAD_DIM) ** -0.5),
        'norm_kv': 1.0 + nrm(ks[13], (D_MODEL,), 0.05),
        'w_kv': nrm(ks[14], (D_MODEL, N_BRANCH * 2 * N_KV * HEAD_DIM), D_MODEL ** -0.5),
        'cmp_pe': nrm(ks[15], (L_CMP, 2, HEAD_DIM), 0.1),
        'cmp_w1': nrm(ks[16], (2, L_CMP, HEAD_DIM, CMP_HIDDEN), (L_CMP * HEAD_DIM) ** -0.5),
        'cmp_w2': nrm(ks[17], (2, CMP_HIDDEN, HEAD_DIM), CMP_HIDDEN ** -0.5),
        'mlp_up': nrm(ks[18], (DEPTH, D_MODEL, D_FF), D_MODEL ** -0.5),
        'mlp_down': nrm(ks[19], (DEPTH, D_FF, D_MODEL), D_FF ** -0.5),
        'norm_final': 1.0 + nrm(ks[20], (D_MODEL,), 0.05),
    }


def reference(x_prompt, x_sample, state_pool, cache_cmp_kv, cache_sel_kv, state_win_kv, page_table,
              norm_mix, norm_ffn, pool_w, pool_scale, w_qg, w_o, norm_kv, w_kv, cmp_pe, cmp_w1, cmp_w2,
              mlp_up, mlp_down, norm_final):
    B, T, _ = x_prompt.shape
    DB, S, _ = x_sample.shape
    pos_p = jnp.arange(T)
    pos_s = PAST_LEN + jnp.arange(S)
    win_buf = state_win_kv.shape[1]
    hp, hs = x_prompt, x_sample
    pool_new_p, pool_new_s = [], []
    for l in range(DEPTH):
        if l < N_A_LAYERS:
            up = rmsnorm(hp, norm_mix[l])
            us = rmsnorm(hs, norm_mix[l])
            us_ext = jnp.concatenate([state_pool[l].astype(us.dtype), us], axis=1)
            hp = hp + pool_mix(up, 0, pool_w[l], pool_scale[l])
            hs = hs + pool_mix(us_ext, POOL_BUF, pool_w[l], pool_scale[l])
            pool_new_p.append(up[:, T - POOL_BUF:])
            pool_new_s.append(us_ext[:, S:])
        else:
            if l == N_A_LAYERS:
                cmp_p, sel_p, win_p = kv_side(hp, pos_p, norm_kv, w_kv)
                ckv_p = cmp_finish(cmp_parts(cmp_p, cmp_w1), cmp_pe, cmp_w1, cmp_w2)
                sblk_p = to_sel_blocks(sel_p, -(-T // L_SEL))
                cmp_s, sel_s, win_s = kv_side(hs, pos_s, norm_kv, w_kv)
                past_cmp = cache_cmp_kv[page_table].reshape(DB, PAST_LEN, 2, N_KV, HEAD_DIM)
                parts_s = jnp.concatenate([cmp_parts(past_cmp, cmp_w1), cmp_parts(cmp_s, cmp_w1)], axis=1)
                ckv_s = cmp_finish(parts_s, cmp_pe, cmp_w1, cmp_w2)
                c_end_s = cmp_end_positions(ckv_s)
                ns_s = PAST_LEN // L_SEL + -(-S // L_SEL)
                gather_s = paged_gatherer(cache_sel_kv, page_table, sel_s)
                win_ext = jnp.concatenate([state_win_kv.astype(win_s.dtype), win_s], axis=1)
                wpos_s = PAST_LEN - win_buf + jnp.arange(win_buf + S)
            b = l - N_A_LAYERS
            qp, gp = nsa_query(hp, pos_p, norm_mix[l], w_qg[b])
            op = nsa_prompt_attend(qp, gp, ckv_p, sblk_p, win_p)
            hp = hp + op.reshape(B, T, N_HEADS * HEAD_DIM) @ w_o[b]
            qs, gs = nsa_query(hs, pos_s, norm_mix[l], w_qg[b])
            o_s = nsa_core(qs, gs, pos_s, ckv_s, c_end_s, ns_s, gather_s, win_ext, wpos_s)
            hs = hs + o_s.reshape(DB, S, N_HEADS * HEAD_DIM) @ w_o[b]
        hp = hp + sq_relu_mlp(hp, norm_ffn[l], mlp_up[l], mlp_down[l])
        hs = hs + sq_relu_mlp(hs, norm_ffn[l], mlp_up[l], mlp_down[l])
    y_prompt = rmsnorm(hp, norm_final)
    y_sample = rmsnorm(hs, norm_final)
    cmp_kv_prompt = cmp_p
    sel_kv_prompt = sel_p
    win_kv_prompt = win_p[:, T - min(WINDOW, T):]
    pool_prompt = jnp.stack(pool_new_p)
    cmp_kv_sample = cmp_s
    sel_kv_sample = sel_s
    win_kv_sample = win_ext[:, S:]
    pool_sample = jnp.stack(pool_new_s)
    return (y_prompt, y_sample, cmp_kv_prompt, sel_kv_prompt, win_kv_prompt, pool_prompt, cmp_kv_sample, sel_kv_sample, win_kv_sample, pool_sample)
```

```python
import numpy as np
import ml_dtypes
from contextlib import ExitStack
import concourse.bass as bass
import concourse.mybir as mybir
from concourse.bass_utils import run_bass_kernel_spmd

F32 = mybir.dt.float32
BF16 = mybir.dt.bfloat16
I32 = mybir.dt.int32
U32 = mybir.dt.uint32
AF = mybir.ActivationFunctionType
ALU = mybir.AluOpType

NEG = -30000.0
T_SEQ = 4096
NSUB = 32
NTILE = 8
PAST = 16384
N_POOLPG = 5120
NC_USED = 4
PSEQ = 8 // NC_USED


class Sched:
    ENGS = ("pe", "dve", "act", "pool", "sp")

    def __init__(self, nc, stack, n_dma_sems=100):
        self.nc = nc
        self.sem = {e: stack.enter_context(nc.semaphore("s_" + e)) for e in self.ENGS}
        self.cnt = {e: 0 for e in self.ENGS}
        self.stack = stack
        self.dma_sem = {}
        self.dma_cnt = {}
        self.ops = {e: [] for e in self.ENGS}
        self.waited = {}
        self.last_w = {}
        self.readers = {}
        self.nops = 0

    def _dma_sem(self, key):
        if key not in self.dma_sem:
            self.dma_sem[key] = self.stack.enter_context(self.nc.semaphore("d%d" % len(self.dma_sem)))
            self.dma_cnt[key] = 0
        return self.dma_sem[key]

    def _deps(self, e, reads, writes):
        deps = {}

        def add(s, n):
            if deps.get(s, 0) < n:
                deps[s] = n
        for r in reads:
            d = self.last_w.get(r)
            if d is not None:
                add(*d)
        for w in writes:
            d = self.last_w.get(w)
            if d is not None:
                add(*d)
            for s, n in self.readers.get(w, {}).items():
                add(s, n)
        waits = []
        for s, n in deps.items():
            if e == "pe" and s == ("e", "pe"):
                continue
            k = (e, s)
            if self.waited.get(k, 0) < n:
                self.waited[k] = n
                waits.append((s, n))
        return waits

    def _semobj(self, s):
        return self.sem[s[1]] if s[0] == "e" else self.dma_sem[s[1]]

    def _record(self, me, reads, writes):
        s, n = me
        for r in reads:
            d = self.readers.setdefault(r, {})
            if d.get(s, 0) < n:
                d[s] = n
        for w in writes:
            self.last_w[w] = me
            self.readers[w] = {}

    def op(self, e, fn, reads=(), writes=()):
        waits = self._deps(e, reads, writes)
        self.cnt[e] += 1
        self._record((("e", e), self.cnt[e]), reads, writes)
        self.ops[e].append((waits, fn, self.sem[e], 1))
        self.nops += 1

    def dma(self, e, fn, key, reads=(), writes=()):
        sem = self._dma_sem(key)
        waits = self._deps(e, reads, writes)
        self.dma_cnt[key] += 16
        self._record((("d", key), self.dma_cnt[key]), reads, writes)
        self.ops[e].append((waits, fn, sem, 16))
        self.nops += 1

    def alias(self, old, new):
        deps = {}
        for k in old:
            d = self.last_w.get(k)
            if d is not None and deps.get(d[0], 0) < d[1]:
                deps[d[0]] = d[1]
            for s, n in self.readers.get(k, {}).items():
                if deps.get(s, 0) < n:
                    deps[s] = n
        for k in new:
            r = self.readers.setdefault(k, {})
            for s, n in deps.items():
                if r.get(s, 0) < n:
                    r[s] = n

    def barrier(self):
        targets = [(("e", e2), self.cnt[e2]) for e2 in self.ENGS if self.cnt[e2] > 0]
        targets += [(("d", k), n) for k, n in self.dma_cnt.items() if n > 0]
        for e in self.ENGS:
            waits = []
            for s_, n in targets:
                if e == "pe" and s_ == ("e", "pe"):
                    continue
                k = (e, s_)
                if self.waited.get(k, 0) < n:
                    self.waited[k] = n
                    waits.append((s_, n))
            self.ops[e].append((waits, None, None, 0))

    def final_waits(self, e, keys):
        waits = self._deps(e, keys, keys)
        self.ops[e].append((waits, None, None, 0))

    def emit(self, block):
        sched = self

        def mk(e):
            def body(engine):
                for waits, fn, sem, inc in sched.ops[e]:
                    for s, n in waits:
                        engine.wait_ge(sched._semobj(s), n)
                    if fn is not None:
                        fn(engine).then_inc(sem, inc)
            return body
        block.tensor(mk("pe"))
        block.vector(mk("dve"))
        block.scalar(mk("act"))
        block.gpsimd(mk("pool"))
        block.sync(mk("sp"))


POOL_WINDOWS = (2, 4, 8, 16)


def make_consts():
    bf = ml_dtypes.bfloat16
    c = {}
    c["c_ident"] = np.eye(128, dtype=np.float32).astype(bf)
    c["c_ident4"] = np.tile(np.eye(128, dtype=np.float32), (1, 4)).astype(bf)
    band = np.zeros((128, 12, 128), np.float32)
    i = np.arange(128)[:, None]
    j = np.arange(128)[None, :]
    for g, w in enumerate(POOL_WINDOWS):
        band[:, g * 3 + 0, :] = ((i <= j) & (i > j - w)) / w - (i == j)
        band[:, g * 3 + 1, :] = ((i - 128) > (j - w)) / w
        band[:, g * 3 + 2, :] = ((i <= j) & (i > j - w)) / np.minimum(j + 1, w) - (i == j)
    c["c_band"] = band.reshape(128, 12 * 128).astype(bf)
    tri = np.where(i <= j, 0.0, NEG).astype(np.float32)
    tri2 = np.where(i >= j, 0.0, NEG).astype(np.float32)
    c["c_tri"] = np.tile(tri, (1, 4)).astype(bf)
    c["c_tri2"] = np.tile(tri2, (1, 4)).astype(bf)
    key = np.arange(4096)[None, :]
    b = np.arange(64)[:, None]
    ind = np.zeros((128, 4096), np.float32)
    ind[64:128] = (key // 64 == b)
    c["c_ind"] = ind.astype(bf)
    ql = np.arange(128)[:, None]
    jj = np.arange(8)[None, :]
    c["c_cmq"] = np.where(ql >= 16 * jj + 15, 0.0, NEG).astype(np.float32)
    mpp = np.arange(504)[None, :] - 248
    c["c_cmpat"] = np.where(16 * mpp + 15 <= ql, 0.0, NEG).astype(np.float32).astype(bf)
    jp = np.arange(128)[None, :] - 64
    cc = (ql >= 64).astype(np.int64)
    G = np.where((jp == cc) | (jp == cc - 1), 100.0, np.where(jp > cc, -1000.0, 0.0))
    c["c_g"] = G.astype(np.float32)
    inv = 500000.0 ** (-np.arange(0, 16, 2, dtype=np.float32) / 16)
    pos = (np.arange(32)[None, :] * 128 + np.arange(128)[:, None]).astype(np.float32)
    ang = pos[:, :, None] * inv[None, None, :]
    c["c_cos"] = np.cos(ang).astype(np.float32).reshape(128, 256)
    c["c_sin"] = np.sin(ang).astype(np.float32).reshape(128, 256)
    angs = np.float32(PAST) * inv
    c["c_coss"] = np.tile(np.cos(angs).astype(np.float32)[None, :], (128, 1))
    c["c_sins"] = np.tile(np.sin(angs).astype(np.float32)[None, :], (128, 1))
    sband = np.zeros((128, 4, 4), np.float32)
    for b_ in range(4):
        for r_ in range(16):
            for g, w in enumerate(POOL_WINDOWS):
                sband[16 * b_ + r_, g, b_] = (1.0 / w if r_ >= 16 - w else 0.0) - (1.0 if r_ == 15 else 0.0)
    c["c_sband"] = sband.reshape(128, 16).astype(bf)
    rowsel = np.zeros((128, 4, 128), np.float32)
    for b_ in range(4):
        rowsel[b_, b_, :] = 1.0
    c["c_rowsel"] = rowsel.reshape(128, 512).astype(bf)
    pcol = np.zeros((128, 8), np.float32)
    pp = np.arange(128)
    pcol[:, 0] = pp < 64
    pcol[:, 1] = pp >= 64
    pcol[:, 2] = pp % 64
    for b_ in range(4):
        pcol[:, 3 + b_] = pp == b_
    pcol[:, 7] = pp
    c["c_pcol"] = pcol
    c["c_iota"] = np.tile(np.arange(128, dtype=np.float32)[None, :], (128, 1))
    return c


CONST_SPECS = {
    "c_sband": ([128, 16], BF16), "c_rowsel": ([128, 512], BF16), "c_pcol": ([128, 8], F32), "c_iota": ([128, 128], F32),
    "c_ident": ([128, 128], BF16), "c_ident4": ([128, 512], BF16), "c_band": ([128, 1536], BF16),
    "c_tri": ([128, 512], BF16), "c_tri2": ([128, 512], BF16), "c_ind": ([128, 4096], BF16),
    "c_cmq": ([128, 8], F32), "c_cmpat": ([128, 504], BF16), "c_g": ([128, 128], F32),
    "c_cos": ([128, 256], F32), "c_sin": ([128, 256], F32), "c_coss": ([128, 8], F32), "c_sins": ([128, 8], F32),
}

IN_SPECS = {
    "x": ([4096 * PSEQ, 1024], F32), "xs": ([4 * PSEQ, 1024], F32), "spool": ([60 * PSEQ, 1024], F32),
    "ccmp": ([N_POOLPG * 128, 512], F32), "csel": ([N_POOLPG * 128, 512], F32),
    "swin": ([2048 * PSEQ, 512], F32), "ptab": ([1, 512 * PSEQ], I32),
    "norm_mix": ([2, 1024], F32), "norm_ffn": ([2, 1024], F32), "pool_w": ([4, 256, 256], F32),
    "pool_scale": ([1, 1024], F32), "w_qg": ([1024, 1072], F32), "w_o": ([1024, 1024], F32),
    "norm_kv": ([1, 1024], F32), "w_kv": ([1024, 1536], F32), "cmp_pe": ([32, 2, 64], F32),
    "cmp_w1": ([2, 32, 64, 128], F32), "cmp_w2": ([2, 128, 64], F32),
    "mlp_up": ([2, 1024, 4096], F32), "mlp_down": ([2, 4096, 1024], F32), "norm_final": ([1, 1024], F32),
}
OUT_SPECS = {
    "y": [4096 * PSEQ, 1024], "ys": [4 * PSEQ, 1024], "cmp_p": [4096 * PSEQ, 512], "sel_p": [4096 * PSEQ, 512], "win_p": [512 * PSEQ, 512],
    "pool_p": [15 * PSEQ, 1024], "cmp_s": [4 * PSEQ, 512], "sel_s": [4 * PSEQ, 512], "win_s": [2048 * PSEQ, 512], "pool_s": [60 * PSEQ, 1024],
}

CH_UP = lambda l, fg: l * 8 + fg
CH_DN = lambda l, fg: 16 + l * 8 + fg
CH_KV = lambda j: 32 + j
CH_QG = lambda j: 35 + j
CH_O = lambda j: 38 + j
CH_C1 = lambda e: 40 + e
N_CH = 42


def build_program(n_tiles=NTILE, do_sample=True, stop=None, pool_pages=N_POOLPG, gather="allgather"):
    nc = bass.Bass("TRN2", target_bir_lowering=False)
    D = {}
    for k, (shp, dt) in IN_SPECS.items():
        if k in ("ccmp", "csel"):
            if not do_sample:
                continue
            if gather == "allgather":
                shp = [pool_pages * 128 // 8, 512]
            else:
                shp = [pool_pages * 128, 512]
        D[k] = nc.dram_tensor(k, shp, dt, kind="ExternalInput").ap()
    for k, (shp, dt) in CONST_SPECS.items():
        D[k] = nc.dram_tensor(k, shp, dt, kind="ExternalInput").ap()
    for k, shp in OUT_SPECS.items():
        D[k] = nc.dram_tensor(k, shp, F32, kind="ExternalOutput").ap()
    wscr = nc.dram_tensor("wscr", [N_CH, 128, 4096], BF16, kind="Internal").ap()
    if do_sample and gather == "allgather":
        cc_in = nc.dram_tensor("cc_in", [pool_pages * 128 // 8, 512], F32, kind="Internal").ap()
        cs_in = nc.dram_tensor("cs_in", [pool_pages * 128 // 8, 512], F32, kind="Internal").ap()
        CC = nc.dram_tensor("cc_full", [pool_pages * 128, 512], F32, kind="Internal").ap()
        CS = nc.dram_tensor("cs_full", [pool_pages * 128, 512], F32, kind="Internal").ap()
    elif do_sample:
        CC, CS = D["ccmp"], D["csel"]

    with ExitStack() as st:
        S = Sched(nc, st)
        total = [0]

        def sb(name, shape, dt):
            n = 1
            for s_ in shape[1:]:
                n *= s_
            total[0] += n * (2 if dt == BF16 else 4)
            return st.enter_context(nc.sbuf_tensor(name, shape, dt))

        selKT = sb("selKT", [128, 4, 4096], BF16)
        selV = sb("selV", [128, 32, 4, 65], BF16)
        winKT = sb("winKT", [128, 4, 1024], BF16)
        winV = sb("winV", [128, 8, 4, 65], BF16)
        cmpKT = sb("cmpKT", [128, 4, 288], BF16)
        cmpV = sb("cmpV", [128, 3, 4, 65], BF16)
        ident = sb("ident", [128, 128], BF16)
        ident4 = sb("ident4", [128, 512], BF16)
        band = sb("band", [128, 12, 128], BF16)
        tri = sb("tri", [128, 512], BF16)
        tri2 = sb("tri2", [128, 512], BF16)
        cmq = sb("cmq", [128, 8], F32)
        cmpat = sb("cmpat", [128, 504], BF16)
        gpat = sb("gpat", [128, 128], F32)
        cos_t = sb("cos_t", [128, 32, 8], F32)
        sin_t = sb("sin_t", [128, 32, 8], F32)
        g0bc = sb("g0bc", [128, 1024], F32)
        gfbc = sb("gfbc", [128, 1024], F32)
        poolW = sb("poolW", [128, 4, 2, 256], BF16)
        w2bf = sb("w2bf", [128, 2, 64], BF16)
        petot = sb("petot", [128, 2], F32)
        gcol = sb("gcol", [128, 5, 8], F32)
        wslot = [sb("wslot%d" % i, [128, 4096], BF16) for i in range(3)]
        h = sb("h", [128, 4, 1024], F32)
        nbf = [sb("nbf%d" % i, [128, 1024], BF16) for i in range(2)]
        ubf = [sb("ubf%d" % i, [128, 1024], BF16) for i in range(2)]
        nT = sb("nT", [128, 8, 512], BF16)
        big = sb("big", [128, 16384], BF16)
        junk = sb("junk", [128, 1024], BF16)
        ssq = sb("ssq", [128, 8], F32)
        rstd = sb("rstd", [128, 8], F32)
        relu_t = [sb("relu%d" % i, [128, 512], F32) for i in range(2)]
        kvf = sb("kvf", [128, 1536], F32)
        kvb = sb("kvb", [128, 1536], BF16)
        rt = sb("rt", [128, 4, 16, 8], F32)
        parts = sb("parts", [128, 2, 2, 4, 32], F32)
        carry = sb("carry", [128, 2, 4], F32)
        hid = sb("hid", [128, 2, 4, 32], F32)
        hid2 = sb("hid2", [128, 2, 4, 32], F32)
        hidb = sb("hidb", [128, 2, 4, 32], BF16)
        Ebuf = sb("Ebuf", [128, 4, 264], F32)
        esum = sb("esum", [128, 8], F32)
        imp = sb("imp", [128, 264], F32)
        score = sb("score", [128, 64], F32)
        score2 = sb("score2", [128, 64], F32)
        mx = sb("mx", [128, 16], F32)
        mbfull = sb("mbfull", [128, 128], BF16)
        gate = sb("gate", [128, 4, 48], F32)
        den = sb("den", [128, 3, 4], F32)
        wgt = sb("wgt", [128, 3, 4], F32)
        ocomb = sb("ocomb", [128, 3, 4, 64], F32)
        rowsel = sb("rowsel", [128, 4, 128], BF16)
        pcol = sb("pcol", [128, 8], F32)
        sband = sb("sband", [128, 4, 4], BF16)
        coss = sb("coss", [128, 8], F32)
        sins = sb("sins", [128, 8], F32)
        ps = st.enter_context(nc.psum_tensor("ps", [128, 8, 512], F32))

        aT = big[:, :].rearrange("p (f t) -> p f t", t=512)
        cmpXT = big[:, 0:4096].rearrange("p (e g t) -> p e g t", e=2, g=4)
        CX_KEYS = ["cmpXT%d" % s_ for s_ in range(4)]
        ysb = kvf[:, 0:1024]
        q_bf = big[:, 0:4096].rearrange("p (s c) -> p s c", c=1024)
        qT = [big[:, 4096 + i * 2048: 4096 + (i + 1) * 2048].rearrange("p (hh t) -> p hh t", t=128) for i in range(2)]
        oT = big[:, 8192:12288].rearrange("p (k t) -> p k t", t=512)
        o_bf = big[:, 12288:13312]
        PT = [big[:, 13312 + i * 512: 13312 + (i + 1) * 512] for i in range(2)]
        stage = [big[:, i * 8192:(i + 1) * 8192].bitcast(F32) for i in range(2)]
        AT_KEYS = ["aT%d" % f for f in range(32)]
        ATT_KEYS = ["q_bf%d" % s for s in range(4)] + ["qT0", "qT1", "qTm0", "qTm1", "o_bf", "PT0", "PT1"] + ["oT%d" % s for s in range(4)]
        STG_KEYS = ["stage0", "stage1"]
        print("SBUF bytes/partition:", total[0])

        blk = st.enter_context(nc.Block())

        def P(b):
            return ps[:, b, :]

        def Pb(b):
            return ps[:, b, :].bitcast(BF16)

        def act(out, in_, func, r, w, **kw):
            S.op("act", lambda e: e.activation(out=out, in_=in_, func=func, **kw), r, w)

        def acopy(out, in_, r, w):
            S.op("act", lambda e: e.copy(out=out, in_=in_), r, w)

        def cp(eng, out, in_, r, w):
            if eng == "act":
                return acopy(out, in_, r, w)
            S.op(eng, lambda e: e.tensor_copy(out=out, in_=in_), r, w)

        def tt(eng, out, in0, in1, op, r, w):
            S.op(eng, lambda e: e.tensor_tensor(out=out, in0=in0, in1=in1, op=op), r, w)

        def ts(eng, out, in0, s1, s2, op0, op1, r, w):
            if op1 is None:
                S.op(eng, lambda e: e.tensor_scalar(out=out, in0=in0, scalar1=s1, scalar2=None, op0=op0), r, w)
            else:
                S.op(eng, lambda e: e.tensor_scalar(out=out, in0=in0, scalar1=s1, scalar2=s2, op0=op0, op1=op1), r, w)

        def stt(out, in0, scalar, in1, op0, op1, r, w):
            S.op("dve", lambda e: e.scalar_tensor_tensor(out=out, in0=in0, scalar=scalar, in1=in1, op0=op0, op1=op1), r, w)

        def mm(out, lhsT, rhs, start, stop, r, w):
            S.op("pe", lambda e: e.matmul(out, lhsT=lhsT, rhs=rhs, start=start, stop=stop, skip_group_check=True), r, w)

        def tr(out, in_, r, w, idn=None):
            idn_ = ident[:] if idn is None else idn
            S.op("pe", lambda e: e.transpose(out=out, in_=in_, identity=idn_), list(r) + ["ident"], w)

        def memset(eng, ap, val, w):
            S.op(eng, lambda e: e.memset(ap, val), (), w)

        def ld(out, in_, key, w, r=(), q="sp", **kw):
            S.dma(q, lambda e: e.dma_start(out=out, in_=in_, **kw), key, r, w)

        def stq(out, in_, key, r, w, **kw):
            S.dma("pool", lambda e: e.dma_start(out=out, in_=in_, **kw), key, r, w)

        ld(ident[:], D["c_ident"], "c0", ["ident"])
        ld(ident4[:], D["c_ident4"], "c0", ["ident4"])
        ld(band[:].rearrange("p a b -> p (a b)"), D["c_band"], "c0", ["band"])
        ld(tri[:], D["c_tri"], "c0", ["tri"])
        ld(tri2[:], D["c_tri2"], "c0", ["tri2"])
        ld(cmq[:], D["c_cmq"], "c0", ["cmq"])
        ld(cmpat[:], D["c_cmpat"], "c0", ["cmpat"])
        ld(gpat[:], D["c_g"], "c0", ["gpat"])
        ld(cos_t[:].rearrange("p a b -> p (a b)"), D["c_cos"], "c0", ["cos"])
        ld(sin_t[:].rearrange("p a b -> p (a b)"), D["c_sin"], "c0", ["sin"])
        ld(rowsel[:].rearrange("p a b -> p (a b)"), D["c_rowsel"], "c0", ["rowsel"])
        ld(pcol[:], D["c_pcol"], "c0", ["pcol"])
        ld(sband[:].rearrange("p a b -> p (a b)"), D["c_sband"], "c0", ["sband"])
        ld(coss[:], D["c_coss"], "c0", ["coss"])
        ld(sins[:], D["c_sins"], "c0", ["sins"])
        if do_sample and gather == "allgather":
            rows_sh = pool_pages * 128 // 8
            nbig = rows_sh * 512 // 16384
            ld(cc_in.rearrange("(a b) c -> a (b c)", a=nbig), D["ccmp"].rearrange("(a b) c -> a (b c)", a=nbig), "ag_cp0", ["cc_in"])
            ld(cs_in.rearrange("(a b) c -> a (b c)", a=nbig), D["csel"].rearrange("(a b) c -> a (b c)", a=nbig), "ag_cp1", ["cs_in"])
            S.dma("pool", lambda e: e.collective_compute("AllGather", ALU.bypass, [list(range(8))], ins=[cc_in], outs=[CC]), "ag0", ["cc_in"], ["CC"])
            S.dma("pool", lambda e: e.collective_compute("AllGather", ALU.bypass, [list(range(8))], ins=[cs_in], outs=[CS]), "ag1", ["cs_in"], ["CS"])
        ld(g0bc[:], D["norm_mix"][0:1, :].partition_broadcast(128), "c0", ["g0bc"])
        ld(gfbc[:], D["norm_final"].partition_broadcast(128), "c0", ["gfbc"])
        for a in range(4):
            ld(selKT[64:128, a, :], D["c_ind"][64:128, :], "c0", ["selKT_ind"])
        ld(gcol[:, 0, :], D["norm_ffn"][0, :].rearrange("(kc p) -> p kc", p=128), "c0", ["gcol"], allow_slow_non_contiguous=True)
        ld(gcol[:, 1, :], D["norm_ffn"][1, :].rearrange("(kc p) -> p kc", p=128), "c0", ["gcol"], allow_slow_non_contiguous=True)
        ld(gcol[:, 2, :], D["norm_kv"][0, :].rearrange("(kc p) -> p kc", p=128), "c0", ["gcol"], allow_slow_non_contiguous=True)
        ld(gcol[:, 3, :], D["norm_mix"][1, :].rearrange("(kc p) -> p kc", p=128), "c0", ["gcol"], allow_slow_non_contiguous=True)
        ts("dve", gcol[:, 4, :], gcol[:, 3, :], 0.125, None, ALU.mult, None, ["gcol"], ["gcol"])
        memset("dve", selV[:, :, :, 64:65], 1.0, ["selV_ones"])
        memset("dve", winV[:, :, :, 64:65], 1.0, ["winV_ones"])
        memset("dve", cmpV[:], 0.0, ["cmpV_init"])
        memset("dve", cmpKT[:], 0.0, ["cmpKT_init"])
        memset("dve", imp[:], 0.0, ["imp"])
        memset("dve", Ebuf[:], 0.0, ["E"])
        memset("dve", carry[:], 0.0, ["carry"])
        memset("dve", mbfull[:], 0.0, ["mbfull"])

        conv_i = [0]

        def convert(chunk, pairs, scale_cols=None, nparts=128, width=4096, inner=512):
            i = conv_i[0] % 2
            conv_i[0] += 1
            sk, wk = "stage%d" % i, "wslot%d" % i
            stg = stage[i]
            for (dst, src) in pairs:
                ld(dst(stg), src, "stg%d" % i, [sk])
            eng = "dve" if (conv_i[0] % 2) else "pool"
            if scale_cols is None:
                cp(eng, wslot[i][0:nparts, 0:width], stg[0:nparts, 0:width], [sk], [wk])
            else:
                nk = width // inner
                for kc in range(nk):
                    ts(eng, wslot[i][0:nparts, kc * inner:(kc + 1) * inner], stg[0:nparts, kc * inner:(kc + 1) * inner],
                       gcol[:, scale_cols, kc:kc + 1], None, ALU.mult, None, [sk, "gcol"], [wk])
            stq(wscr[chunk, 0:nparts, 0:width], wslot[i][0:nparts, 0:width], "wst%d" % i, [wk], ["scr%d" % chunk])

        for l in range(2):
            for fg in range(8):
                convert(CH_UP(l, fg), [(lambda s_: s_[:, :].rearrange("p (kc n) -> p kc n", n=512),
                                        D["mlp_up"][l, :, fg * 512:(fg + 1) * 512].rearrange("(kc p) n -> p kc n", p=128))], scale_cols=l)
            for fg in range(8):
                convert(CH_DN(l, fg), [(lambda s_: s_[:, :].rearrange("p (fc n) -> p fc n", n=1024),
                                        D["mlp_down"][l, fg * 512:(fg + 1) * 512, :].rearrange("(fc p) n -> p fc n", p=128))])
        for j in range(3):
            convert(CH_KV(j), [(lambda s_: s_[:, :].rearrange("p (kc n) -> p kc n", n=512),
                                D["w_kv"][:, j * 512:(j + 1) * 512].rearrange("(kc p) n -> p kc n", p=128))], scale_cols=2)
        for j in range(2):
            convert(CH_QG(j), [(lambda s_: s_[:, :].rearrange("p (kc n) -> p kc n", n=512),
                                D["w_qg"][:, j * 512:(j + 1) * 512].rearrange("(kc p) n -> p kc n", p=128))], scale_cols=4)
        convert(CH_QG(2), [(lambda s_: s_[:, 0:384].rearrange("p (kc n) -> p kc n", n=48),
                            D["w_qg"][:, 1024:1072].rearrange("(kc p) n -> p kc n", p=128))], scale_cols=3, width=384, inner=48)
        for j in range(2):
            convert(CH_O(j), [(lambda s_: s_[:, :].rearrange("p (kc n) -> p kc n", n=512),
                               D["w_o"][:, j * 512:(j + 1) * 512].rearrange("(kc p) n -> p kc n", p=128))])
        for e_ in range(2):
            convert(CH_C1(e_), [(lambda s_: s_[0:64, :].rearrange("p (r k) -> p r k", k=128),
                                 D["cmp_w1"][e_].rearrange("r h k -> h r k"))], nparts=64)
        i_ = conv_i[0] % 2
        conv_i[0] += 1
        stg = stage[i_]
        ld(stg[:, 0:2048].rearrange("p (g kc d) -> p g kc d", g=4, kc=2), D["pool_w"].rearrange("g (kc p) d -> p g kc d", p=128), "stg%d" % i_, ["stage%d" % i_])
        ld(stg[:, 2048:3072], D["pool_scale"].partition_broadcast(128), "stg%d" % i_, ["stage%d" % i_])
        for g in range(4):
            for kc in range(2):
                tt("dve", poolW[:, g, kc, :], stg[:, (g * 2 + kc) * 256:(g * 2 + kc + 1) * 256], stg[:, 2048 + g * 256: 2048 + (g + 1) * 256],
                   ALU.mult, ["stage%d" % i_], ["poolW"])
        i_ = conv_i[0] % 2
        conv_i[0] += 1
        stg = stage[i_]
        ld(stg[:, 0:128].rearrange("p (e h) -> p e h", e=2), D["cmp_w2"].rearrange("e k h -> k e h"), "stg%d" % i_, ["stage%d" % i_])
        cp("dve", w2bf[:].rearrange("p e h -> p (e h)"), stg[:, 0:128], ["stage%d" % i_], ["w2bf"])
        wctr = [0]

        def wload(chunk, nparts=128, width=4096):
            i = wctr[0] % 3
            wctr[0] += 1
            ld(wslot[i][0:nparts, 0:width], wscr[chunk, 0:nparts, 0:width], "wl%d" % i, ["wslot%d" % i], r=["scr%d" % chunk])
            return wslot[i], "wslot%d" % i

        pe_b = sb("pe_b", [128, 32, 2], BF16)
        pe_f = sb("pe_f", [128, 32, 2], F32)
        ld(pe_f[0:64, :, :], D["cmp_pe"].rearrange("r e h -> h r e"), "c0", ["pe_f"], allow_slow_non_contiguous=True)
        cp("dve", pe_b[0:64].rearrange("p r e -> p (r e)"), pe_f[0:64].rearrange("p r e -> p (r e)"), ["pe_f"], ["pe_b"])
        for e_ in range(2):
            W, wk = wload(CH_C1(e_), nparts=64)
            for row in range(32):
                mm(ps[:, 7, e_:e_ + 1], W[0:64, row * 128:(row + 1) * 128], pe_b[0:64, row, e_:e_ + 1], row == 0, row == 31,
                   [wk, "pe_b"], ["ps7"])
        cp("dve", petot[:], ps[:, 7, 0:2], ["ps7"], ["petot"])

        S.alias(STG_KEYS, AT_KEYS + ATT_KEYS)

        def rms(src, src_keys, col):
            act(junk[:], src, AF.Square, src_keys, ["junk", "ssq%d" % col], accum_out=ssq[:, col:col + 1])
            act(rstd[:, col:col + 1], ssq[:, col:col + 1], AF.Sqrt, ["ssq%d" % col], ["rstd%d" % col], scale=1.0 / 1024, bias=1e-6)
            S.op("dve", lambda e: e.reciprocal(out=rstd[:, col:col + 1], in_=rstd[:, col:col + 1]), ["rstd%d" % col], ["rstd%d" % col])

        def norm_T(s, col, nb, ncols_valid=128):
            rms(h[:, s, :], ["h%d" % s], col)
            ts("dve", nbf[nb][:], h[:, s, :], rstd[:, col:col + 1], None, ALU.mult, None, ["h%d" % s, "rstd%d" % col], ["nbf%d" % nb])
            for kc in range(8):
                tr(Pb(4)[:, kc * 128:(kc + 1) * 128], nbf[nb][:, kc * 128:(kc + 1) * 128], ["nbf%d" % nb], ["ps4"])
            acopy(nT[:, :, s * 128:(s + 1) * 128], Pb(4).rearrange("p (k t) -> p k t", t=128), ["ps4"], ["nT%d" % s])

        def mlp(l, nsub):
            S.alias(ATT_KEYS, AT_KEYS)
            ntok = nsub * 128
            for s in range(nsub):
                norm_T(s, s, s % 2)
            for fg in range(8):
                W, wk = wload(CH_UP(l, fg))
                Wv = W[:, :].rearrange("p (kc n) -> p kc n", n=512)
                for fc in range(4):
                    f = fg * 4 + fc
                    b = f % 4
                    for kc in range(8):
                        mm(P(b)[:, 0:ntok], Wv[:, kc, fc * 128:(fc + 1) * 128], nT[:, kc, 0:ntok], kc == 0, kc == 7,
                           [wk] + ["nT%d" % s for s in range(nsub)], ["ps%d" % b])
                    rl = relu_t[f % 2]
                    act(rl[:, 0:ntok], P(b)[:, 0:ntok], AF.Relu, ["ps%d" % b], ["relu%d" % (f % 2)])
                    tt("pool", aT[:, f, 0:ntok], rl[:, 0:ntok], rl[:, 0:ntok], ALU.mult, ["relu%d" % (f % 2)], ["aT%d" % f])
            for fg in range(8):
                W, wk = wload(CH_DN(l, fg))
                Wv = W[:, :].rearrange("p (fc n) -> p fc n", n=1024)
                for fc in range(4):
                    f = fg * 4 + fc
                    for s in range(nsub):
                        for hf in range(2):
                            b = 2 * s + hf
                            mm(P(b), aT[:, f, s * 128:(s + 1) * 128], Wv[:, fc, hf * 512:(hf + 1) * 512], f == 0, f == 31,
                               [wk, "aT%d" % f], ["ps%d" % b])
            for s in range(nsub):
                tt("dve", h[:, s, :].rearrange("p (a n) -> p a n", n=512), ps[:, 2 * s:2 * s + 2, :], h[:, s, :].rearrange("p (a n) -> p a n", n=512),
                   ALU.add, ["ps%d" % (2 * s), "ps%d" % (2 * s + 1), "h%d" % s], ["h%d" % s])

        def rope(x1, x2, cosb, sinb, shape, keys_r, keys_w):
            a, b_ = shape
            t1 = rt[:, 0, 0:a * b_, :].rearrange("p (a b) d -> p a b d", a=a)
            t2 = rt[:, 1, 0:a * b_, :].rearrange("p (a b) d -> p a b d", a=a)
            t3 = rt[:, 2, 0:a * b_, :].rearrange("p (a b) d -> p a b d", a=a)
            t4 = rt[:, 3, 0:a * b_, :].rearrange("p (a b) d -> p a b d", a=a)
            tt("dve", t1, x1, cosb, ALU.mult, keys_r, ["rt0"])
            tt("dve", t2, x2, sinb, ALU.mult, keys_r, ["rt1"])
            tt("dve", t3, x2, cosb, ALU.mult, keys_r, ["rt2"])
            tt("dve", t4, x1, sinb, ALU.mult, keys_r, ["rt3"])
            tt("dve", x1, t1, t2, ALU.subtract, ["rt0", "rt1"], keys_w)
            tt("dve", x2, t3, t4, ALU.add, ["rt2", "rt3"], keys_w)

        class _Stop(Exception):
            pass

        def chk(name):
            if stop == name:
                raise _Stop()

        def main_loop(sq=0):
          nb_ctr = [0]
          DV = {k_: D[k_][sq * r_:(sq + 1) * r_] for k_, r_ in (("x", 4096), ("y", 4096), ("cmp_p", 4096), ("sel_p", 4096), ("win_p", 512), ("pool_p", 15))}
          if sq > 0:
              allk = ["cmpKT%d" % t_ for t_ in range(8)] + ["cmpV%d" % t_ for t_ in range(8)]
              memset("dve", cmpV[:], 0.0, ["cmpV_init"] + allk)
              memset("dve", cmpKT[:], 0.0, ["cmpKT_init"] + allk)
              memset("dve", imp[:], 0.0, ["imp"])
              memset("dve", carry[:], 0.0, ["carry"])
          for T in range(n_tiles if stop != "prologue" else 0):
              for s in range(4):
                  i = 4 * T + s
                  ld(h[:, s, :], DV["x"][i * 128:(i + 1) * 128, :], "xld%d" % s, ["h%d" % s])
              for s in range(4):
                  i = 4 * T + s
                  nb = nb_ctr[0] % 2
                  nb_ctr[0] += 1
                  rms(h[:, s, :], ["h%d" % s], s)
                  stt(ubf[nb][:], h[:, s, :], rstd[:, s:s + 1], g0bc[:], ALU.mult, ALU.mult, ["h%d" % s, "rstd%d" % s, "g0bc"], ["ubf%d" % nb])
                  if i == NSUB - 1:
                      stt(ysb, h[:, s, :], rstd[:, s:s + 1], g0bc[:], ALU.mult, ALU.mult, ["h%d" % s, "rstd%d" % s, "g0bc"], ["kvf0", "kvf1"])
                      stq(DV["pool_p"], kvf[113:128, 0:1024], "st_misc", ["kvf0", "kvf1"], ["out_pool_p"])
                  for c in range(8):
                      g = c // 2
                      b = c // 4
                      o = ps[:, b, (c % 4) * 128:(c % 4 + 1) * 128]
                      if i == 0:
                          mm(o, ubf[nb][:, c * 128:(c + 1) * 128], band[:, g * 3 + 2, :], True, True, ["ubf%d" % nb, "band"], ["ps%d" % b])
                      else:
                          mm(o, ubf[nb][:, c * 128:(c + 1) * 128], band[:, g * 3 + 0, :], True, False, ["ubf%d" % nb, "band"], ["ps%d" % b])
                          mm(o, ubf[1 - nb][:, c * 128:(c + 1) * 128], band[:, g * 3 + 1, :], False, True, ["ubf%d" % (1 - nb), "band"], ["ps%d" % b])
                  dTb = nT[:, :, 0:128]
                  acopy(dTb, ps[:, 0:2, :].rearrange("p b (c t) -> p (b c) t", t=128), ["ps0", "ps1"], ["nT0"])
                  for g in range(4):
                      b = 2 + g // 2
                      for kc in range(2):
                          mm(ps[:, b, (g % 2) * 256:(g % 2 + 1) * 256], dTb[:, 2 * g + kc, :], poolW[:, g, kc, :], kc == 0, kc == 1,
                             ["nT0", "poolW"], ["ps%d" % b])
                  tt("dve", h[:, s, :].rearrange("p (a n) -> p a n", n=512), ps[:, 2:4, :], h[:, s, :].rearrange("p (a n) -> p a n", n=512), ALU.add,
                     ["ps2", "ps3", "h%d" % s], ["h%d" % s])
              chk("l0")
              mlp(0, 4)
              chk("mlp0")
              S.alias(AT_KEYS, CX_KEYS)
              for s in range(4):
                  norm_T(s, s, s % 2)
              kvW = []
              for s in range(4):
                  i = 4 * T + s
                  for j in range(3):
                      if s == 0:
                          kvW.append(wload(CH_KV(j)))
                      W, wk = kvW[j]
                      Wv = W[:, :].rearrange("p (kc n) -> p kc n", n=512)
                      for kc in range(8):
                          mm(P(j), nT[:, kc, s * 128:(s + 1) * 128], Wv[:, kc, :], kc == 0, kc == 7, [wk, "nT%d" % s], ["ps%d" % j])
                      cp("act" if j != 1 else "dve", kvf[:, j * 512:(j + 1) * 512], P(j), ["ps%d" % j], ["kvf%d" % j])
                  kv5 = kvf[:, :].rearrange("p (br e g d) -> p br e g d", br=3, e=2, g=4)
                  x1 = kv5[:, :, 0, :, 0:8]
                  x2 = kv5[:, :, 0, :, 8:16]
                  cosb = cos_t[:, i, :].unsqueeze(1).unsqueeze(1).to_broadcast([128, 3, 4, 8])
                  sinb = sin_t[:, i, :].unsqueeze(1).unsqueeze(1).to_broadcast([128, 3, 4, 8])
                  rope(x1, x2, cosb, sinb, (3, 4), ["kvf0", "kvf1", "kvf2", "cos", "sin"], ["kvf0", "kvf1", "kvf2"])
                  chk("kv1")
                  stq(DV["cmp_p"][i * 128:(i + 1) * 128, :], kvf[:, 0:512], "st_kv0", ["kvf0"], ["out_cmp_p"])
                  stq(DV["sel_p"][i * 128:(i + 1) * 128, :], kvf[:, 512:1024], "st_kv1", ["kvf1"], ["out_sel_p"])
                  if i >= NSUB - 4:
                      ii = i - (NSUB - 4)
                      stq(DV["win_p"][ii * 128:(ii + 1) * 128, :], kvf[:, 1024:1536], "st_kv2", ["kvf2"], ["out_win_p"])
                  chk("kv2")
                  cp("pool", kvb[:], kvf[:], ["kvf0", "kvf1", "kvf2"], ["kvb"])
                  kb5 = kvb[:, :].rearrange("p (br e g d) -> p br e g d", br=3, e=2, g=4)
                  for g in range(4):
                      tr(Pb(5)[0:64, g * 128:(g + 1) * 128], kb5[:, 1, 0, g, :], ["kvb"], ["ps5"])
                  for g in range(4):
                      tr(Pb(5)[0:64, (4 + g) * 128:(5 + g) * 128], kb5[:, 2, 0, g, :], ["kvb"], ["ps5"])
                  chk("kv2b")
                  acopy(selKT[0:64, :, i * 128:(i + 1) * 128], Pb(5)[0:64, 0:512].rearrange("p (g t) -> p g t", t=128), ["ps5"], ["selKT%d" % i])
                  wsl = i % 8
                  cp("act", winKT[0:64, :, wsl * 128:(wsl + 1) * 128], Pb(5)[0:64, 512:1024].rearrange("p (g t) -> p g t", t=128), ["ps5"], ["winKT%d" % wsl])
                  chk("kv2c")
                  cp("act", selV[:, i, :, 0:64], kb5[:, 1, 1, :, :], ["kvb", "selV_ones"], ["selV%d" % i])
                  cp("act", winV[:, wsl, :, 0:64], kb5[:, 2, 1, :, :], ["kvb", "winV_ones"], ["winV%d" % wsl])
                  chk("kv3")
                  for e_ in range(2):
                      for g in range(4):
                          tr(Pb(6)[0:64, (e_ * 4 + g) * 128:(e_ * 4 + g + 1) * 128], kb5[:, 0, e_, g, :], ["kvb"], ["ps6"])
                  acopy(cmpXT[0:64, :, :, s * 128:(s + 1) * 128], Pb(6)[0:64, :].rearrange("p (e g t) -> p e g t", e=2, g=4), ["ps6"], ["cmpXT%d" % s])
              chk("kv4")
              for e_ in range(2):
                  W, wk = wload(CH_C1(e_), nparts=64)
                  for p_ in range(2):
                      for r_ in range(16):
                          row = p_ * 16 + r_
                          rhs = cmpXT[0:64, e_, :, :].rearrange("p g (sg r) -> p g sg r", r=16)[:, :, :, r_]
                          mm(ps[:, 7, (e_ * 2 + p_) * 128:(e_ * 2 + p_ + 1) * 128].rearrange("p (g sg) -> p g sg", g=4),
                             W[0:64, row * 128:(row + 1) * 128], rhs, r_ == 0, r_ == 15,
                             [wk] + ["cmpXT%d" % s for s in range(4)], ["ps7"])
              cp("dve", parts[:].rearrange("p e q g sg -> p (e q g sg)"), P(7), ["ps7"], ["parts"])
              chk("kv5")
              tt("dve", hid[:, :, :, 1:32], parts[:, :, 0, :, 0:31], parts[:, :, 1, :, 1:32], ALU.add, ["parts"], ["hid"])
              tt("dve", hid[:, :, :, 0:1], carry[:].unsqueeze(3), parts[:, :, 1, :, 0:1], ALU.add, ["parts", "carry"], ["hid"])
              cp("dve", carry[:].unsqueeze(3), parts[:, :, 0, :, 31:32], ["parts", "hid"], ["carry"])
              for e_ in range(2):
                  ts("dve", hid[:, e_], hid[:, e_], petot[:, e_:e_ + 1], None, ALU.add, None, ["hid", "petot"], ["hid"])
              hf_ = hid[:].rearrange("p e g s -> p (e g s)")
              h2_ = hid2[:].rearrange("p e g s -> p (e g s)")
              tt("dve", h2_, hf_, hf_, ALU.mult, ["hid"], ["hid2"])
              ts("dve", h2_, h2_, 0.044715, 1.0, ALU.mult, ALU.add, ["hid2"], ["hid2"])
              tt("dve", h2_, h2_, hf_, ALU.mult, ["hid2", "hid"], ["hid2"])
              act(h2_, h2_, AF.Exp, ["hid2"], ["hid2"], scale=-1.5957691216)
              ts("dve", h2_, h2_, 1.0, None, ALU.add, None, ["hid2"], ["hid2"])
              S.op("dve", lambda e, a=h2_: e.reciprocal(out=a, in_=a), ["hid2"], ["hid2"])
              tt("dve", hidb[:].rearrange("p e g s -> p (e g s)"), h2_, hf_, ALU.mult, ["hid2", "hid"], ["hidb"])
              chk("kv6")
              mm(ps[0:64, 7, 0:128], w2bf[:, 0, :], hidb[:, 0].rearrange("p g s -> p (g s)"), True, True, ["w2bf", "hidb"], ["ps7"])
              cp("dve", cmpKT[0:64, :, 32 * T:32 * T + 32], ps[0:64, 7, 0:128].rearrange("p (g s) -> p g s", g=4), ["ps7", "cmpKT_init"], ["cmpKT%d" % T])
              pq = 32 * (T % 3)
              for g in range(4):
                  mm(ps[pq:pq + 32, 7, 128 + g * 64:128 + (g + 1) * 64], hidb[:, 1, g, :], w2bf[:, 1, :], True, True, ["w2bf", "hidb"], ["ps7"])
              cp("dve", cmpV[pq:pq + 32, T // 3, :, 0:64], ps[pq:pq + 32, 7, 128:384].rearrange("p (g d) -> p g d", g=4), ["ps7", "cmpV_init"], ["cmpV%d" % T])
              memset("dve", cmpV[pq:pq + 32, T // 3, :, 64:65], 1.0, ["cmpV%d" % T])
              if T == 0:
                  memset("dve", cmpV[0:1, 0, :, :], 0.0, ["cmpV0"])

              chk("kv")
              S.alias(AT_KEYS + CX_KEYS, ATT_KEYS)
              for s in range(4):
                  norm_T(s, s, s % 2)
              qW = []
              for s in range(4):
                  i = 4 * T + s
                  for j in range(3):
                      if s == 0:
                          qW.append(wload(CH_QG(j), width=4096 if j < 2 else 384))
                      W, wk = qW[j]
                      if j < 2:
                          Wv = W[:, :].rearrange("p (kc n) -> p kc n", n=512)
                          for kc in range(8):
                              mm(P(j), nT[:, kc, s * 128:(s + 1) * 128], Wv[:, kc, :], kc == 0, kc == 7, [wk, "nT%d" % s], ["ps%d" % j])
                          cp("act" if j == 0 else "dve", kvf[:, j * 512:(j + 1) * 512], P(j), ["ps%d" % j], ["kvf%d" % j])
                      else:
                          Wv = W[:, 0:384].rearrange("p (kc n) -> p kc n", n=48)
                          for kc in range(8):
                              mm(P(2)[:, 0:48], nT[:, kc, s * 128:(s + 1) * 128], Wv[:, kc, :], kc == 0, kc == 7, [wk, "nT%d" % s], ["ps2"])
                          act(gate[:, s, :], P(2)[:, 0:48], AF.Exp, ["ps2"], ["gate%d" % s], scale=-1.0)
                          ts("dve", gate[:, s, :], gate[:, s, :], 1.0, None, ALU.add, None, ["gate%d" % s], ["gate%d" % s])
                          S.op("dve", lambda e, s=s: e.reciprocal(out=gate[:, s, :], in_=gate[:, s, :]), ["gate%d" % s], ["gate%d" % s])
                  q3 = kvf[:, 0:1024].rearrange("p (hh d) -> p hh d", d=64)
                  x1 = q3[:, :, 0:8].unsqueeze(1)
                  x2 = q3[:, :, 8:16].unsqueeze(1)
                  cosb = cos_t[:, i, :].unsqueeze(1).unsqueeze(1).to_broadcast([128, 1, 16, 8])
                  sinb = sin_t[:, i, :].unsqueeze(1).unsqueeze(1).to_broadcast([128, 1, 16, 8])
                  rope(x1, x2, cosb, sinb, (1, 16), ["kvf0", "kvf1", "cos", "sin"], ["kvf0", "kvf1"])
                  cp("pool", q_bf[:, s, :], kvf[:, 0:1024], ["kvf0", "kvf1"], ["q_bf%d" % s])

              chk("qg")
              for s in range(4):
                  i = 4 * T + s
                  qb = i % 2
                  qTt = qT[qb]
                  qk, qmk = "qT%d" % qb, "qTm%d" % qb
                  for hh in range(16):
                      bnk = 4 + hh // 8
                      tr(Pb(bnk)[0:64, (hh % 8) * 128:(hh % 8 + 1) * 128], q_bf[:, s, hh * 64:(hh + 1) * 64], ["q_bf%d" % s], ["ps%d" % bnk])
                  acopy(qTt[0:64, 0:8, :], Pb(4)[0:64, :].rearrange("p (hh t) -> p hh t", t=128), ["ps4"], [qk])
                  cp("act", qTt[0:64, 8:16, :], Pb(5)[0:64, :].rearrange("p (hh t) -> p hh t", t=128), ["ps5"], [qk])
                  ncm = 8 * i + 8
                  for g in range(4):
                      hs = slice(4 * g, 4 * g + 4)
                      if i >= 8:
                          for hh in range(4):
                              bnk = hh // 2
                              mm(ps[:, bnk, (hh % 2) * 256:(hh % 2) * 256 + ncm], qTt[0:64, 4 * g + hh, :], cmpKT[0:64, g, 0:ncm], True, True,
                                 [qk] + ["cmpKT%d" % t for t in range(T + 1)], ["ps%d" % bnk])
                          sc4 = ps[:, 0:2, :].rearrange("p b (hh m) -> p (b hh) m", m=256)
                          tt("dve", sc4[:, :, ncm - 8:ncm], sc4[:, :, ncm - 8:ncm], cmq[:, :].unsqueeze(1).to_broadcast([128, 4, 8]), ALU.add,
                             ["ps0", "ps1", "cmq"], ["ps0", "ps1"])
                          ts("dve", sc4[:, :, 0:1], sc4[:, :, 0:1], NEG, None, ALU.add, None, ["ps0", "ps1"], ["ps0", "ps1"])
                          for hh in range(4):
                              act(Ebuf[:, hh, 0:ncm], sc4[:, hh, 0:ncm], AF.Exp, ["ps0", "ps1"], ["E", "esum"], accum_out=esum[:, hh:hh + 1])
                          ts("dve", esum[:, 0:4], esum[:, 0:4], 1e-30, None, ALU.max, None, ["esum"], ["esum"])
                          S.op("dve", lambda e: e.reciprocal(out=esum[:, 4:8], in_=esum[:, 0:4]), ["esum"], ["esum"])
                          ts("dve", imp[:, 0:ncm], Ebuf[:, 0, 0:ncm], esum[:, 4:5], None, ALU.mult, None, ["E", "esum"], ["imp"])
                          for hh in range(1, 4):
                              stt(imp[:, 0:ncm], Ebuf[:, hh, 0:ncm], esum[:, 4 + hh:5 + hh], imp[:, 0:ncm], ALU.mult, ALU.add, ["E", "esum", "imp"], ["imp"])
                          A = imp[:, 0:256].rearrange("p (j t) -> p j t", t=4)
                          B = imp[:, 4:260].rearrange("p (j t) -> p j t", t=4)
                          tt("dve", score[:], A[:, :, 1], A[:, :, 2], ALU.add, ["imp"], ["score"])
                          tt("dve", score[:], score[:], A[:, :, 3], ALU.add, ["imp", "score"], ["score"])
                          stt(score[:], score[:], 2.0, A[:, :, 0], ALU.mult, ALU.add, ["imp", "score"], ["score"])
                          tt("dve", score[:], score[:], B[:, :, 0], ALU.add, ["imp", "score"], ["score"])
                          tt("dve", score[:], score[:], gpat[:, 64 - 2 * i:128 - 2 * i], ALU.add, ["score", "gpat"], ["score"])
                          memset("dve", score[:, 0:1], 100.0, ["score"])
                          S.op("dve", lambda e: e.max(out=mx[:, 0:8], in_=score[:]), ["score"], ["mx"])
                          S.op("dve", lambda e: e.match_replace(out=score2[:], in_to_replace=mx[:, 0:8], in_values=score[:], imm_value=-1e9), ["score", "mx"], ["score2"])
                          S.op("dve", lambda e: e.max(out=mx[:, 8:16], in_=score2[:]), ["score2"], ["mx"])
                          ts("dve", mbfull[:, 64:128], score[:], mx[:, 15:16], 1.0, ALU.is_ge, ALU.subtract, ["score", "mx"], ["mbfull"])
                          tr(Pb(7)[:, 0:128], mbfull[:], ["mbfull"], ["ps7"])
                          S.op("act", lambda e, qTt=qTt, hs=hs: e.mul(out=qTt[64:128, hs, :], in_=Pb(7)[64:128, 0:128].unsqueeze(1).to_broadcast([64, 4, 128]), mul=-NEG),
                               ["ps7"], [qmk + "_%d" % g])
                      else:
                          memset("dve", qTt[64:128, hs, :], 0.0, [qmk + "_%d" % g])
                      rhs_aug = qTt[:, hs, :]
                      rhs_q = qTt[0:64, hs, :]
                      rk = [qk, qmk + "_%d" % g]
                      pti = [0]

                      def branch(obank, ktiles, nk=128):
                          n = len(ktiles)
                          for idx, (lhsT, lk, mrhs, mlhs, Vap, vk, aug) in enumerate(ktiles):
                              sb_ = 2 + (pti[0] % 2)
                              pt_i = pti[0] % 2
                              pti[0] += 1
                              mm(P(sb_)[0:nk], lhsT, rhs_aug if aug else rhs_q, True, mrhs is None, lk + rk, ["ps%d" % sb_])
                              if mrhs is not None:
                                  mm(P(sb_)[0:nk], mlhs, mrhs, False, True, ["ident", "ident4", "tri", "tri2", "cmpat"], ["ps%d" % sb_])
                              act(PT[pt_i][0:nk], P(sb_)[0:nk], AF.Exp, ["ps%d" % sb_], ["PT%d" % pt_i])
                              for hh in range(4):
                                  mm(ps[:, obank, hh * 65:(hh + 1) * 65], PT[pt_i][0:nk, hh * 128:(hh + 1) * 128], Vap, (idx == 0 and hh == 0), (idx == n - 1),
                                     ["PT%d" % pt_i] + vk, ["ps%d" % obank])

                      kt = []
                      for t in range(3):
                          if 96 * t < ncm:
                              off = 96 * t - 8 * i + 248
                              kt.append((cmpKT[0:64, g, t * 96:(t + 1) * 96], ["cmpKT%d" % tt_ for tt_ in range(T + 1)] + ["cmpKT_init"],
                                         ident4[:], cmpat[:, off:off + 96], cmpV[0:96, t, g, :], ["cmpV%d" % tt_ for tt_ in range(T + 1)] + ["cmpV_init"], False))
                      branch(4, kt, nk=96)
                      kt = []
                      for j in range(i + 1):
                          kt.append((selKT[:, g, j * 128:(j + 1) * 128], ["selKT%d" % j, "selKT_ind"],
                                     tri[:] if j == i else None, ident[:], selV[:, j, g, :], ["selV%d" % j], True))
                      branch(5, kt)
                      kt = []
                      for j in range(max(0, i - 4), i + 1):
                          m_ = tri[:] if j == i else (tri2[:] if j == i - 4 else None)
                          kt.append((winKT[0:64, g, (j % 8) * 128:(j % 8 + 1) * 128], ["winKT%d" % (j % 8)],
                                     m_, ident[:], winV[:, j % 8, g, :], ["winV%d" % (j % 8)], False))
                      branch(6, kt)
                      for br in range(3):
                          cp("dve", den[:, br, :], ps[:, 4 + br, 0:260].rearrange("p (hh d) -> p hh d", d=65)[:, :, 64], ["ps%d" % (4 + br)], ["den"])
                      ts("dve", den[:], den[:], 1e-30, None, ALU.max, None, ["den"], ["den"])
                      S.op("dve", lambda e: e.reciprocal(out=den[:], in_=den[:]), ["den"], ["den"])
                      gv = gate[:, s, 12 * g:12 * g + 12].rearrange("p (hh br) -> p br hh", br=3)
                      tt("dve", wgt[:], den[:], gv, ALU.mult, ["den", "gate%d" % s], ["wgt"])
                      for br in range(3):
                          tt("dve", ocomb[:, br], ps[:, 4 + br, 0:260].rearrange("p (hh d) -> p hh d", d=65)[:, :, 0:64],
                             wgt[:, br, :].unsqueeze(2).to_broadcast([128, 4, 64]), ALU.mult, ["ps%d" % (4 + br), "wgt"], ["ocomb%d" % br])
                      tt("pool", ocomb[:, 0], ocomb[:, 0], ocomb[:, 1], ALU.add, ["ocomb0", "ocomb1"], ["ocomb0"])
                      tt("pool", o_bf[:, 256 * g:256 * (g + 1)].rearrange("p (hh d) -> p hh d", d=64), ocomb[:, 0], ocomb[:, 2], ALU.add,
                         ["ocomb0", "ocomb2"], ["o_bf"])
                  for kc in range(8):
                      tr(Pb(7)[:, kc * 128:(kc + 1) * 128], o_bf[:, kc * 128:(kc + 1) * 128], ["o_bf"], ["ps7"])
                  acopy(oT[:, :, s * 128:(s + 1) * 128], Pb(7).rearrange("p (k t) -> p k t", t=128), ["ps7"], ["oT%d" % s])
              chk("attn")
              oW = []
              for s in range(4):
                  for j in range(2):
                      if s == 0:
                          oW.append(wload(CH_O(j)))
                      W, wk = oW[j]
                      Wv = W[:, :].rearrange("p (kc n) -> p kc n", n=512)
                      b = (2 * s + j) % 4
                      for kc in range(8):
                          mm(P(b), oT[:, kc, s * 128:(s + 1) * 128], Wv[:, kc, :], kc == 0, kc == 7, [wk, "oT%d" % s], ["ps%d" % b])
                      tt("dve", h[:, s, j * 512:(j + 1) * 512], P(b), h[:, s, j * 512:(j + 1) * 512], ALU.add, ["ps%d" % b, "h%d" % s], ["h%d" % s])
              chk("wo")
              mlp(1, 4)
              chk("mlp1")
              for s in range(4):
                  i = 4 * T + s
                  rms(h[:, s, :], ["h%d" % s], s)
                  stt(ysb, h[:, s, :], rstd[:, s:s + 1], gfbc[:], ALU.mult, ALU.mult, ["h%d" % s, "rstd%d" % s, "gfbc"], ["kvf0", "kvf1"])
                  stq(DV["y"][i * 128:(i + 1) * 128, :], ysb, "st_y", ["kvf0", "kvf1"], ["out_y"])


        def sample_phase(ss=0):
            S.barrier()
            DS = {k_: D[k_][ss * r_:(ss + 1) * r_] for k_, r_ in (("xs", 4), ("spool", 60), ("swin", 2048), ("ys", 4), ("cmp_s", 4), ("sel_s", 4), ("win_s", 2048), ("pool_s", 60))}
            hflat = h[:].rearrange("p a b -> p (a b)")
            E1 = hflat[:, 1024:2048]
            imp_s = hflat[:, 2048:3076]
            misc = hflat[:, 3076:4096]
            ptb_i = misc[:, 0:128].bitcast(I32)
            idxp = misc[:, 128:256].bitcast(I32)
            PTf = misc[:, 256:384]
            iota_f = misc[:, 384:512]
            jl4 = misc[:, 512:544].bitcast(BF16).rearrange("p (g t) -> p g t", g=4)
            cxs = big[:, 8192:12288].rearrange("p (e g t) -> p e g t", e=2, g=4)
            jb = misc[:, 576:640]
            ji = misc[:, 640:704].bitcast(I32)
            pgi = misc[:, 704:768].bitcast(I32)
            pgf_ = misc[:, 768:832]
            parf = misc[:, 832:896]
            rb = misc[:, 896:960].rearrange("p (g t) -> p g t", g=4)
            idxs_f = misc[:, 960:992].rearrange("p (g t) -> p g t", g=4)
            idxs_i = misc[:, 992:1020].bitcast(I32)
            Ef = Ebuf[:].rearrange("p a b -> p (a b)")
            score_s = Ef[:, 0:256]
            score2_s = Ef[:, 256:512]
            mx_s = Ef[:, 512:528]
            ix_s = Ef[:, 528:544].bitcast(U32)
            phys = Ef[:, 544:608].rearrange("p (g t) -> p g t", g=4)
            idxs_i = Ef[:, 608:640].bitcast(I32).rearrange("p (g t) -> p g t", g=4)
            es_s = Ef[:, 640:648]
            cmpKT_s = selKT[:, 0, :].rearrange("p (g m) -> p g m", g=4)
            selKT_s = selKT[:, 1, :].rearrange("p (g t) -> p g t", g=4)
            winKT_s = selKT[:, 2, 0:2048].rearrange("p (g t) -> p g t", g=4)
            KTnew = selKT[:, 2, 2048:3072].rearrange("p (b g t) -> p b g t", b=2, g=4)
            sv = selV[:].rearrange("p a g d -> p (a g d)")
            cmpV_s = sv[:, 0:2860].rearrange("p (t g d) -> p t g d", g=4, d=65)
            selV_s = sv[:, 2860:4940].rearrange("p (t g d) -> p t g d", g=4, d=65)
            winV_s = sv[:, 4940:5980].rearrange("p (t g d) -> p t g d", g=4, d=65)
            Vnew = sv[:, 5980:6500].rearrange("p (t g d) -> p t g d", g=4, d=65)
            ring = nT[:].rearrange("p a b -> p (a b)").bitcast(F32)
            ohs = wslot[2][:, :].bitcast(F32).rearrange("p (t q) -> p t q", q=128)
            o_acc = kvf[:, 0:1024]
            ring_i = [0]

            def ring_slot():
                i = ring_i[0] % 4
                ring_i[0] += 1
                return ring[:, i * 512:(i + 1) * 512], "ring%d" % i

            memset("dve", h[:, 0, :], 0.0, ["h0"])
            ld(h[0:4, 0, :], DS["xs"], "xld0", ["h0"])
            ld(iota_f, D["c_iota"], "c0", ["iota_f"])
            rms(h[:, 0, :], ["h0"], 0)
            us_f = kvf[:, 0:1024]
            stt(us_f, h[:, 0, :], rstd[:, 0:1], g0bc[:], ALU.mult, ALU.mult, ["h0", "rstd0", "g0bc"], ["kvf0", "kvf1"])
            uext = hflat[0:64, 1024:2048]
            for b in range(4):
                ld(hflat[16 * b:16 * b + 15, 1024:2048], DS["spool"][15 * b:15 * b + 15, :], "smp0", ["uext"])
                ld(hflat[16 * b + 15:16 * b + 16, 1024:2048], kvf[b:b + 1, 0:1024], "smp0", ["uext"], r=["kvf0", "kvf1"])
                stq(DS["pool_s"][15 * b:15 * b + 14, :], DS["spool"][15 * b + 1:15 * b + 15, :], "st_s", [], ["out_s"])
                stq(DS["pool_s"][15 * b + 14:15 * b + 15, :], kvf[b:b + 1, 0:1024], "st_s", ["kvf0", "kvf1"], ["out_s"])
            cp("dve", ubf[0][0:64, :], uext, ["uext"], ["ubf0"])
            for c in range(8):
                g = c // 2
                b_ = c // 4
                mm(ps[:, b_, (c % 4) * 128:(c % 4) * 128 + 4], ubf[0][0:64, c * 128:(c + 1) * 128], sband[0:64, g, :], True, True, ["ubf0", "sband"], ["ps%d" % b_])
            dTb = nT[:, :, 0:128]
            memset("dve", dTb, 0.0, ["nT0"])
            acopy(dTb[:, :, 0:4], ps[:, 0:2, :].rearrange("p b (c t) -> p (b c) t", t=128)[:, :, 0:4], ["ps0", "ps1"], ["nT0"])
            for g in range(4):
                b_ = 2 + g // 2
                for kc in range(2):
                    mm(ps[:, b_, (g % 2) * 256:(g % 2 + 1) * 256], dTb[:, 2 * g + kc, :], poolW[:, g, kc, :], kc == 0, kc == 1, ["nT0", "poolW"], ["ps%d" % b_])
            tt("dve", h[:, 0, :].rearrange("p (a n) -> p a n", n=512), ps[:, 2:4, :], h[:, 0, :].rearrange("p (a n) -> p a n", n=512), ALU.add,
               ["ps2", "ps3", "h0"], ["h0"])
            chk("s_l0")
            mlp(0, 1)
            chk("s_mlp0")
            S.alias(AT_KEYS, ATT_KEYS + CX_KEYS)
            norm_T(0, 0, 0)
            for j in range(3):
                W, wk = wload(CH_KV(j))
                Wv = W[:, :].rearrange("p (kc n) -> p kc n", n=512)
                for kc in range(8):
                    mm(P(j), nT[:, kc, 0:128], Wv[:, kc, :], kc == 0, kc == 7, [wk, "nT0"], ["ps%d" % j])
                cp("act", kvf[:, j * 512:(j + 1) * 512], P(j), ["ps%d" % j], ["kvf%d" % j])
            kv5 = kvf[:, :].rearrange("p (br e g d) -> p br e g d", br=3, e=2, g=4)
            cosb = coss[:, :].unsqueeze(1).unsqueeze(1).to_broadcast([128, 3, 4, 8])
            sinb = sins[:, :].unsqueeze(1).unsqueeze(1).to_broadcast([128, 3, 4, 8])
            rope(kv5[:, :, 0, :, 0:8], kv5[:, :, 0, :, 8:16], cosb, sinb, (3, 4), ["kvf0", "kvf1", "kvf2", "coss", "sins"], ["kvf0", "kvf1", "kvf2"])
            stq(DS["cmp_s"], kvf[0:4, 0:512], "st_s", ["kvf0"], ["out_s"])
            stq(DS["sel_s"], kvf[0:4, 512:1024], "st_s", ["kvf1"], ["out_s"])
            stq(DS["win_s"].rearrange("(b r) c -> b r c", b=4)[:, 511, :], kvf[0:4, 1024:1536], "st_s", ["kvf2"], ["out_s"])
            stq(DS["win_s"].rearrange("(b r) c -> b r c", b=4)[:, 0:511, :], DS["swin"].rearrange("(b r) c -> b r c", b=4)[:, 1:512, :], "st_s", [], ["out_s"])
            cp("pool", kvb[:], kvf[:], ["kvf0", "kvf1", "kvf2"], ["kvb"])
            kb5 = kvb[:, :].rearrange("p (br e g d) -> p br e g d", br=3, e=2, g=4)
            for bi, br in enumerate((1, 2)):
                for g in range(4):
                    tr(Pb(5)[0:64, (bi * 4 + g) * 128:(bi * 4 + g + 1) * 128], kb5[:, br, 0, g, :], ["kvb"], ["ps5"])
            acopy(KTnew[0:64], Pb(5)[0:64, :].rearrange("p (b g t) -> p b g t", b=2, g=4), ["ps5"], ["KTnew"])
            vnew_all = ubf[1][:, 0:512].rearrange("p (b g d) -> p b g d", b=2, g=4)
            for bi, br in enumerate((1, 2)):
                cp("act", vnew_all[:, bi], kb5[:, br, 1, :, :], ["kvb"], ["vnew_all"])
            chk("s_kv")
            norm_T(0, 0, 0)
            for j in range(3):
                W, wk = wload(CH_QG(j), width=4096 if j < 2 else 384)
                if j < 2:
                    Wv = W[:, :].rearrange("p (kc n) -> p kc n", n=512)
                    for kc in range(8):
                        mm(P(j), nT[:, kc, 0:128], Wv[:, kc, :], kc == 0, kc == 7, [wk, "nT0"], ["ps%d" % j])
                    cp("act", kvf[:, j * 512:(j + 1) * 512], P(j), ["ps%d" % j], ["kvf%d" % j])
                else:
                    Wv = W[:, 0:384].rearrange("p (kc n) -> p kc n", n=48)
                    for kc in range(8):
                        mm(P(2)[:, 0:48], nT[:, kc, 0:128], Wv[:, kc, :], kc == 0, kc == 7, [wk, "nT0"], ["ps2"])
                    act(gate[:, 0, :], P(2)[:, 0:48], AF.Exp, ["ps2"], ["gate0"], scale=-1.0)
                    ts("dve", gate[:, 0, :], gate[:, 0, :], 1.0, None, ALU.add, None, ["gate0"], ["gate0"])
                    S.op("dve", lambda e: e.reciprocal(out=gate[:, 0, :], in_=gate[:, 0, :]), ["gate0"], ["gate0"])
            q3 = kvf[:, 0:1024].rearrange("p (hh d) -> p hh d", d=64)
            cosb = coss[:, :].unsqueeze(1).unsqueeze(1).to_broadcast([128, 1, 16, 8])
            sinb = sins[:, :].unsqueeze(1).unsqueeze(1).to_broadcast([128, 1, 16, 8])
            rope(q3[:, :, 0:8].unsqueeze(1), q3[:, :, 8:16].unsqueeze(1), cosb, sinb, (1, 16), ["kvf0", "kvf1", "coss", "sins"], ["kvf0", "kvf1"])
            cp("pool", q_bf[:, 0, :], kvf[:, 0:1024], ["kvf0", "kvf1"], ["q_bf0"])
            qTt = qT[0]
            for hh in range(16):
                bnk = 4 + hh // 8
                tr(Pb(bnk)[0:64, (hh % 8) * 128:(hh % 8 + 1) * 128], q_bf[:, 0, hh * 64:(hh + 1) * 64], ["q_bf0"], ["ps%d" % bnk])
            acopy(qTt[0:64, 0:8, :], Pb(4)[0:64, :].rearrange("p (hh t) -> p hh t", t=128), ["ps4"], ["qT0"])
            cp("act", qTt[0:64, 8:16, :], Pb(5)[0:64, :].rearrange("p (hh t) -> p hh t", t=128), ["ps5"], ["qT0"])
            chk("s_qg")
            memset("dve", o_acc, 0.0, ["kvf0", "kvf1"])
            memset("dve", sv[:, 0:6500], 0.0, ["sv"])
            memset("dve", selV_s[:, :, :, 64:65], 1.0, ["sv"])
            memset("dve", winV_s[:, :, :, 64:65], 1.0, ["sv"])
            memset("dve", imp_s, 0.0, ["imp_s"])
            pti = [0]

            def attend(obank, ktiles, rhs_q, rk):
                n = len(ktiles)
                for idx, (lhsT, lk, Vap, vk, nk) in enumerate(ktiles):
                    sb_ = 2 + (pti[0] % 2)
                    pt_i = pti[0] % 2
                    pti[0] += 1
                    mm(P(sb_)[0:nk], lhsT, rhs_q, True, True, lk + rk, ["ps%d" % sb_])
                    act(PT[pt_i][0:nk], P(sb_)[0:nk], AF.Exp, ["ps%d" % sb_], ["PT%d" % pt_i])
                    for hh in range(4):
                        mm(ps[:, obank, hh * 65:(hh + 1) * 65], PT[pt_i][0:nk, hh * 128:(hh + 1) * 128], Vap, (idx == 0 and hh == 0), (idx == n - 1),
                           ["PT%d" % pt_i] + vk, ["ps%d" % obank])

            for b in range(4):
                ld(ptb_i, D["ptab"][0:1, ss * 512 + b * 128: ss * 512 + (b + 1) * 128].partition_broadcast(128), "smp1", ["ptb"])
                ts("dve", idxp, ptb_i, 128, pcol[:, 7:8], ALU.mult, ALU.add, ["ptb", "pcol"], ["idxp"])
                cp("dve", PTf, ptb_i, ["ptb"], ["PTf"])
                memset("dve", carry[:], 0.0, ["carry"])
                for Tq in range(32):
                    for s_ in range(4):
                        pg = 4 * Tq + s_
                        slot, sk = ring_slot()
                        S.dma("pool", lambda e, slot=slot, pg=pg: e.indirect_dma_start(out=slot, out_offset=None, in_=CC,
                              in_offset=bass.IndirectOffsetOnAxis(ap=idxp[:, pg:pg + 1], axis=0)), "g_" + sk, ["idxp", "CC"], [sk])
                        cp("pool", kvb[:, 0:512], slot, [sk], ["kvb"])
                        kc5 = kvb[:, 0:512].rearrange("p (e g d) -> p e g d", e=2, g=4)
                        for e_ in range(2):
                            for g in range(4):
                                tr(Pb(6)[0:64, (e_ * 4 + g) * 128:(e_ * 4 + g + 1) * 128], kc5[:, e_, g, :], ["kvb"], ["ps6"])
                        acopy(cxs[0:64, :, :, s_ * 128:(s_ + 1) * 128], Pb(6)[0:64, :].rearrange("p (e g t) -> p e g t", e=2, g=4), ["ps6"], ["cxs%d" % s_])
                    for e_ in range(2):
                        W, wk = wload(CH_C1(e_), nparts=64)
                        for p_ in range(2):
                            for r_ in range(16):
                                row = p_ * 16 + r_
                                rhs = cxs[0:64, e_, :, :].rearrange("p g (sg r) -> p g sg r", r=16)[:, :, :, r_]
                                mm(ps[:, 7, (e_ * 2 + p_) * 128:(e_ * 2 + p_ + 1) * 128].rearrange("p (g sg) -> p g sg", g=4),
                                   W[0:64, row * 128:(row + 1) * 128], rhs, r_ == 0, r_ == 15, [wk] + ["cxs%d" % q for q in range(4)], ["ps7"])
                    cp("dve", parts[:].rearrange("p e q g sg -> p (e q g sg)"), P(7), ["ps7"], ["parts"])
                    tt("dve", hid[:, :, :, 1:32], parts[:, :, 0, :, 0:31], parts[:, :, 1, :, 1:32], ALU.add, ["parts"], ["hid"])
                    tt("dve", hid[:, :, :, 0:1], carry[:].unsqueeze(3), parts[:, :, 1, :, 0:1], ALU.add, ["parts", "carry"], ["hid"])
                    cp("dve", carry[:].unsqueeze(3), parts[:, :, 0, :, 31:32], ["parts", "hid"], ["carry"])
                    for e_ in range(2):
                        ts("dve", hid[:, e_], hid[:, e_], petot[:, e_:e_ + 1], None, ALU.add, None, ["hid", "petot"], ["hid"])
                    hf_ = hid[:].rearrange("p e g s -> p (e g s)")
                    h2_ = hid2[:].rearrange("p e g s -> p (e g s)")
                    tt("dve", h2_, hf_, hf_, ALU.mult, ["hid"], ["hid2"])
                    ts("dve", h2_, h2_, 0.044715, 1.0, ALU.mult, ALU.add, ["hid2"], ["hid2"])
                    tt("dve", h2_, h2_, hf_, ALU.mult, ["hid2", "hid"], ["hid2"])
                    act(h2_, h2_, AF.Exp, ["hid2"], ["hid2"], scale=-1.5957691216)
                    ts("dve", h2_, h2_, 1.0, None, ALU.add, None, ["hid2"], ["hid2"])
                    S.op("dve", lambda e, a=h2_: e.reciprocal(out=a, in_=a), ["hid2"], ["hid2"])
                    tt("dve", hidb[:].rearrange("p e g s -> p (e g s)"), h2_, hf_, ALU.mult, ["hid2", "hid"], ["hidb"])
                    mm(ps[0:64, 7, 0:128], w2bf[:, 0, :], hidb[:, 0].rearrange("p g s -> p (g s)"), True, True, ["w2bf", "hidb"], ["ps7"])
                    cp("dve", cmpKT_s[0:64, :, 32 * Tq:32 * Tq + 32], ps[0:64, 7, 0:128].rearrange("p (g s) -> p g s", g=4), ["ps7"], ["cmpKT_s"])
                    pq = 32 * (Tq % 3)
                    for g in range(4):
                        mm(ps[pq:pq + 32, 7, 128 + g * 64:128 + (g + 1) * 64], hidb[:, 1, g, :], w2bf[:, 1, :], True, True, ["w2bf", "hidb"], ["ps7"])
                    cp("dve", cmpV_s[pq:pq + 32, Tq // 3, :, 0:64], ps[pq:pq + 32, 7, 128:384].rearrange("p (g d) -> p g d", g=4), ["ps7", "sv"], ["cmpV_s"])
                    memset("dve", cmpV_s[pq:pq + 32, Tq // 3, :, 64:65], 1.0, ["cmpV_s"])
                    if Tq == 0:
                        memset("dve", cmpV_s[0:1, 0, :, :], 0.0, ["cmpV_s"])
                chk("s_cmp%d" % b)
                for t in range(4):
                    slot, sk = ring_slot()
                    ld(slot, DS["swin"][b * 512 + t * 128: b * 512 + (t + 1) * 128, :], "g_" + sk, [sk])
                    cp("pool", kvb[:, 512:1024], slot, [sk], ["kvb2"])
                    kw = kvb[:, 512:1024].rearrange("p (e g d) -> p e g d", e=2, g=4)
                    for g in range(4):
                        tr(Pb(5)[0:64, g * 128:(g + 1) * 128], kw[:, 0, g, :], ["kvb2"], ["ps5"])
                    acopy(winKT_s[0:64, :, t * 128:(t + 1) * 128], Pb(5)[0:64, 0:512].rearrange("p (g t) -> p g t", g=4), ["ps5"], ["winKT_s"])
                    cp("act", winV_s[:, t, :, 0:64], kw[:, 1, :, :], ["kvb2", "sv"], ["winV_s"])
                for bi in range(2):
                    ts("dve", Vnew[:, bi, :, 0:64], vnew_all[:, bi], pcol[:, 3 + b:4 + b], None, ALU.mult, None, ["vnew_all", "pcol", "sv"], ["Vnew"])
                    for g in range(4):
                        cp("dve", Vnew[:, bi, g, 64:65], pcol[:, 3 + b:4 + b], ["pcol", "sv"], ["Vnew"])
                for g in range(4):
                    rhs_q = qTt[0:64, 4 * g:4 * g + 4, :]
                    rk = ["qT0"]
                    for hh in range(4):
                        for half in range(2):
                            mm(P(half), qTt[0:64, 4 * g + hh, :], cmpKT_s[0:64, g, half * 512:(half + 1) * 512], True, True, ["qT0", "cmpKT_s"], ["ps%d" % half])
                        ts("dve", ps[:, 0, 0:1], ps[:, 0, 0:1], NEG, None, ALU.add, None, ["ps0"], ["ps0"])
                        act(E1.rearrange("p (a n) -> p a n", n=512), ps[:, 0:2, :], AF.Exp, ["ps0", "ps1"], ["E1", "es_s"], accum_out=es_s[:, 0:1])
                        S.op("dve", lambda e: e.reciprocal(out=es_s[:, 1:2], in_=es_s[:, 0:1]), ["es_s"], ["es_s"])
                        if hh == 0:
                            ts("dve", imp_s[:, 0:1024], E1, es_s[:, 1:2], None, ALU.mult, None, ["E1", "es_s"], ["imp_s"])
                        else:
                            stt(imp_s[:, 0:1024], E1, es_s[:, 1:2], imp_s[:, 0:1024], ALU.mult, ALU.add, ["E1", "es_s", "imp_s"], ["imp_s"])
                    A = imp_s[:, 0:1024].rearrange("p (j t) -> p j t", t=4)
                    B = imp_s[:, 4:1028].rearrange("p (j t) -> p j t", t=4)
                    tt("dve", score_s, A[:, :, 1], A[:, :, 2], ALU.add, ["imp_s"], ["score_s"])
                    tt("dve", score_s, score_s, A[:, :, 3], ALU.add, ["imp_s", "score_s"], ["score_s"])
                    stt(score_s, score_s, 2.0, A[:, :, 0], ALU.mult, ALU.add, ["imp_s", "score_s"], ["score_s"])
                    tt("dve", score_s, score_s, B[:, :, 0], ALU.add, ["imp_s", "score_s"], ["score_s"])
                    memset("dve", score_s[:, 0:1], -1.0, ["score_s"])
                    memset("dve", score_s[:, 255:256], -1.0, ["score_s"])
                    S.op("dve", lambda e: e.max(out=mx_s[:, 0:8], in_=score_s), ["score_s"], ["mx_s"])
                    S.op("dve", lambda e: e.max_index(out=ix_s[:, 0:8], in_max=mx_s[:, 0:8], in_values=score_s), ["score_s", "mx_s"], ["ix_s"])
                    S.op("dve", lambda e: e.match_replace(out=score2_s, in_to_replace=mx_s[:, 0:8], in_values=score_s, imm_value=-1e9), ["score_s", "mx_s"], ["score2_s"])
                    S.op("dve", lambda e: e.max(out=mx_s[:, 8:16], in_=score2_s), ["score2_s"], ["mx_s"])
                    S.op("dve", lambda e: e.max_index(out=ix_s[:, 8:16], in_max=mx_s[:, 8:16], in_values=score2_s), ["score2_s", "mx_s"], ["ix_s"])
                    cp("dve", jl4[:, g, 0:13], ix_s[:, 0:13], ["ix_s"], ["jl4"])
                    memset("dve", jl4[:, g, 13:14], 0.0, ["jl4"])
                    memset("dve", jl4[:, g, 14:15], 255.0, ["jl4"])
                    memset("dve", jl4[:, g, 15:16], 0.0, ["jl4"])
                mm(ps[:, 7, 0:64], rowsel[:, b, :], jl4.rearrange("p g t -> p (g t)"), True, True, ["rowsel", "jl4"], ["ps7"])
                cp("dve", jb, ps[:, 7, 0:64], ["ps7"], ["jb"])
                cp("dve", ji, jb, ["jb"], ["ji"])
                S.op("dve", lambda e: e.tensor_scalar(out=pgi, in0=ji, scalar1=1, scalar2=None, op0=ALU.arith_shift_right), ["ji"], ["pgi"])
                cp("dve", pgf_, pgi, ["pgi"], ["pgf"])
                stt(parf, pgf_, -2.0, jb, ALU.mult, ALU.add, ["pgf", "jb"], ["parf"])
                for g in range(4):
                    tt("dve", ohs, iota_f.unsqueeze(1).to_broadcast([128, 16, 128]),
                       pgf_[:, g * 16:(g + 1) * 16].unsqueeze(2).to_broadcast([128, 16, 128]), ALU.is_equal, ["iota_f", "pgf"], ["wslot2"])
                    tt("dve", ohs, ohs, PTf.unsqueeze(1).to_broadcast([128, 16, 128]), ALU.mult, ["wslot2", "PTf"], ["wslot2"])
                    S.op("dve", lambda e, g=g: e.tensor_reduce(out=phys[:, g, :], in_=ohs, axis=mybir.AxisListType.X, op=ALU.add), ["wslot2"], ["phys"])
                ts("dve", rb.rearrange("p g t -> p (g t)"), phys.rearrange("p g t -> p (g t)"), 128.0, None, ALU.mult, None, ["phys"], ["rb"])
                stt(rb.rearrange("p g t -> p (g t)"), parf, 64.0, rb.rearrange("p g t -> p (g t)"), ALU.mult, ALU.add, ["parf", "rb"], ["rb"])
                rbp = rb.rearrange("p g (pr two) -> p g pr two", two=2)
                ts("dve", idxs_f, rbp[:, :, :, 0], pcol[:, 0:1], pcol[:, 2:3], ALU.mult, ALU.add, ["rb", "pcol"], ["idxs_f"])
                stt(idxs_f, rbp[:, :, :, 1], pcol[:, 1:2], idxs_f, ALU.mult, ALU.add, ["rb", "pcol", "idxs_f"], ["idxs_f"])
                cp("dve", idxs_i, idxs_f, ["idxs_f"], ["idxs_i"])
                chk("s_idx%d" % b)
                for g in range(4):
                    for pr in range(8):
                        slot, sk = ring_slot()
                        S.dma("pool", lambda e, slot=slot, g=g, pr=pr: e.indirect_dma_start(out=slot, out_offset=None, in_=CS,
                              in_offset=bass.IndirectOffsetOnAxis(ap=idxs_i[:, g, pr:pr + 1], axis=0)), "g_" + sk, ["idxs_i", "CS"], [sk])
                        s5 = slot.rearrange("p (e g d) -> p e g d", e=2, g=4)
                        cp("pool", kvb[:, 1024:1152].rearrange("p (e d) -> p e d", e=2), s5[:, :, g, :], [sk], ["kvb3"])
                        tr(Pb(5)[0:64, 512 + (pr % 4) * 128: 512 + (pr % 4 + 1) * 128], kvb[:, 1024:1088], ["kvb3"], ["ps5"])
                        acopy(selKT_s[0:64, g, pr * 128:(pr + 1) * 128], Pb(5)[0:64, 512 + (pr % 4) * 128: 512 + (pr % 4 + 1) * 128], ["ps5"], ["selKT_s"])
                        cp("act", selV_s[:, pr, g, 0:64], kvb[:, 1088:1152], ["kvb3", "sv"], ["selV_s"])
                    memset("dve", selV_s[64:128, 7, g, :], 0.0, ["selV_s"])
                    rhs_q = qTt[0:64, 4 * g:4 * g + 4, :]
                    rk = ["qT0"]
                    kt = [(cmpKT_s[0:64, g, 96 * t:96 * t + (96 if t < 10 else 64)], ["cmpKT_s"], cmpV_s[0:(96 if t < 10 else 64), t, g, :], ["cmpV_s", "sv"],
                           (96 if t < 10 else 64)) for t in range(11)]
                    attend(4, kt, rhs_q, rk)
                    kt = [(selKT_s[0:64, g, pr * 128:(pr + 1) * 128], ["selKT_s"], selV_s[:, pr, g, :], ["selV_s", "sv"], 128) for pr in range(8)]
                    kt.append((KTnew[0:64, 0, g, :], ["KTnew"], Vnew[:, 0, g, :], ["Vnew"], 128))
                    attend(5, kt, rhs_q, rk)
                    kt = [(winKT_s[0:64, g, t * 128:(t + 1) * 128], ["winKT_s"], winV_s[:, t, g, :], ["winV_s", "sv"], 128) for t in range(4)]
                    kt.append((KTnew[0:64, 1, g, :], ["KTnew"], Vnew[:, 1, g, :], ["Vnew"], 128))
                    attend(6, kt, rhs_q, rk)
                    for br in range(3):
                        cp("dve", den[:, br, :], ps[:, 4 + br, 0:260].rearrange("p (hh d) -> p hh d", d=65)[:, :, 64], ["ps%d" % (4 + br)], ["den"])
                    ts("dve", den[:], den[:], 1e-30, None, ALU.max, None, ["den"], ["den"])
                    S.op("dve", lambda e: e.reciprocal(out=den[:], in_=den[:]), ["den"], ["den"])
                    gv = gate[:, 0, 12 * g:12 * g + 12].rearrange("p (hh br) -> p br hh", br=3)
                    tt("dve", wgt[:], den[:], gv, ALU.mult, ["den", "gate0"], ["wgt"])
                    ts("dve", wgt[:], wgt[:], pcol[:, 3 + b:4 + b], None, ALU.mult, None, ["wgt", "pcol"], ["wgt"])
                    for br in range(3):
                        tt("dve", ocomb[:, br], ps[:, 4 + br, 0:260].rearrange("p (hh d) -> p hh d", d=65)[:, :, 0:64],
                           wgt[:, br, :].unsqueeze(2).to_broadcast([128, 4, 64]), ALU.mult, ["ps%d" % (4 + br), "wgt"], ["ocomb%d" % br])
                    tt("pool", ocomb[:, 0], ocomb[:, 0], ocomb[:, 1], ALU.add, ["ocomb0", "ocomb1"], ["ocomb0"])
                    tt("pool", ocomb[:, 0], ocomb[:, 0], ocomb[:, 2], ALU.add, ["ocomb0", "ocomb2"], ["ocomb0"])
                    oa = o_acc[:, 256 * g:256 * (g + 1)].rearrange("p (hh d) -> p hh d", d=64)
                    tt("pool", oa, oa, ocomb[:, 0], ALU.add, ["ocomb0", "kvf0", "kvf1"], ["kvf0", "kvf1"])
                chk("s_att%d" % b)
            S.alias(["cxs%d" % q for q in range(4)], ["oT0", "oT1", "oT2", "oT3"])
            cp("pool", o_bf, o_acc, ["kvf0", "kvf1"], ["o_bf"])
            for kc in range(8):
                tr(Pb(7)[:, kc * 128:(kc + 1) * 128], o_bf[:, kc * 128:(kc + 1) * 128], ["o_bf"], ["ps7"])
            acopy(oT[:, :, 0:128], Pb(7).rearrange("p (k t) -> p k t", t=128), ["ps7"], ["oT0"])
            for j in range(2):
                W, wk = wload(CH_O(j))
                Wv = W[:, :].rearrange("p (kc n) -> p kc n", n=512)
                for kc in range(8):
                    mm(P(j), oT[:, kc, 0:128], Wv[:, kc, :], kc == 0, kc == 7, [wk, "oT0"], ["ps%d" % j])
                tt("dve", h[:, 0, j * 512:(j + 1) * 512], P(j), h[:, 0, j * 512:(j + 1) * 512], ALU.add, ["ps%d" % j, "h0"], ["h0"])
            mlp(1, 1)
            rms(h[:, 0, :], ["h0"], 0)
            stt(ysb, h[:, 0, :], rstd[:, 0:1], gfbc[:], ALU.mult, ALU.mult, ["h0", "rstd0", "gfbc"], ["kvf0", "kvf1"])
            stq(DS["ys"], kvf[0:4, 0:1024], "st_s", ["kvf0", "kvf1"], ["out_s"])

        try:
            for sq_ in range(PSEQ):
                main_loop(sq_)
            if do_sample:
                for ss_ in range(PSEQ):
                    sample_phase(ss_)
        except _Stop:
            pass
        S.final_waits("pool", ["out_y", "out_cmp_p", "out_sel_p", "out_win_p", "out_pool_p", "out_s"])
        print("ops:", S.nops, {e: len(v) for e, v in S.ops.items()})
        S.emit(blk)
    return nc


_CACHE = {}


def kernel(**inputs):
    return run(inputs, gather="direct")


def run(inputs, n_tiles=NTILE, do_sample=True, stop=None, ncores=NC_USED, pool_pages=N_POOLPG, gather="allgather", trace=False):
    n = NC_USED
    consts = make_consts()
    ck = (n_tiles, do_sample, stop, pool_pages, gather)
    if ck not in _CACHE:
        _CACHE[ck] = build_program(n_tiles, do_sample, stop, pool_pages, gather)
    nc = _CACHE[ck]
    f = lambda a: np.ascontiguousarray(a)
    shared = {
        "norm_mix": f(inputs["norm_mix"]), "norm_ffn": f(inputs["norm_ffn"]), "pool_w": f(inputs["pool_w"][0]),
        "pool_scale": f(inputs["pool_scale"]).reshape(1, 1024), "w_qg": f(inputs["w_qg"][0]), "w_o": f(inputs["w_o"][0]),
        "norm_kv": f(inputs["norm_kv"]).reshape(1, 1024), "w_kv": f(inputs["w_kv"]), "cmp_pe": f(inputs["cmp_pe"]),
        "cmp_w1": f(inputs["cmp_w1"]), "cmp_w2": f(inputs["cmp_w2"]), "mlp_up": f(inputs["mlp_up"]),
        "mlp_down": f(inputs["mlp_down"]), "norm_final": f(inputs["norm_final"]).reshape(1, 1024),
    }
    if do_sample and "ccmp_list" not in inputs:
        ccmp = f(inputs["cache_cmp_kv"]).reshape(pool_pages * 128, 512)
        csel = f(inputs["cache_sel_kv"]).reshape(pool_pages * 128, 512)
        if gather == "direct":
            shared["ccmp"] = ccmp
            shared["csel"] = csel
    shared.update(consts)
    in_maps = []
    for c in range(n):
        m = dict(shared)
        ns_ = 4 * PSEQ
        m["x"] = f(inputs["x_prompt"][PSEQ * c:PSEQ * (c + 1)]).reshape(PSEQ * 4096, 1024)
        m["xs"] = f(inputs["x_sample"][ns_ * c:ns_ * (c + 1), 0, :])
        m["spool"] = f(inputs["state_pool"][0, ns_ * c:ns_ * (c + 1)]).reshape(15 * ns_, 1024)
        m["swin"] = f(inputs["state_win_kv"][ns_ * c:ns_ * (c + 1)]).reshape(512 * ns_, 512)
        m["ptab"] = f(inputs["page_table"][ns_ * c:ns_ * (c + 1)]).reshape(1, 128 * ns_).astype(np.int32)
        if do_sample and "ccmp_list" in inputs:
            m["ccmp"] = inputs["ccmp_list"][c]
            m["csel"] = inputs["csel_list"][c]
            m["ptab"] = inputs["ptab_list"][c]
        elif do_sample and gather != "direct":
            rs_ = pool_pages * 128 // 8
            m["ccmp"] = ccmp[c * rs_:(c + 1) * rs_]
            m["csel"] = csel[c * rs_:(c + 1) * rs_]
        in_maps.append(m)
    if trace:
        res = run_bass_kernel_spmd(nc, in_maps[:ncores], core_ids=list(range(ncores)), trace=True)
        print("EXEC_TIME_NS", getattr(res, "exec_time_ns", None), flush=True)
    else:
        res = run_bass_kernel_spmd(nc, in_maps[:ncores], core_ids=list(range(ncores)))
    R = list(res.results)
    while len(R) < n:
        R.append(R[0])
    cat = lambda k: np.stack([R[c][k] for c in range(n)], 0)
    y_prompt = cat("y").reshape(8, 4096, 1024)
    y_sample = cat("ys").reshape(32, 1, 1024)
    cmp_p = cat("cmp_p").reshape(8, 4096, 2, 4, 64)
    sel_p = cat("sel_p").reshape(8, 4096, 2, 4, 64)
    win_p = cat("win_p").reshape(8, 512, 2, 4, 64)
    pool_p = cat("pool_p").reshape(1, 8, 15, 1024)
    cmp_s = cat("cmp_s").reshape(32, 1, 2, 4, 64)
    sel_s = cat("sel_s").reshape(32, 1, 2, 4, 64)
    win_s = cat("win_s").reshape(32, 512, 2, 4, 64)
    pool_s = cat("pool_s").reshape(1, 32, 15, 1024)
    return (y_prompt, y_sample, cmp_p, sel_p, win_p, pool_p, cmp_s, sel_s, win_s, pool_s)
```

```python
import numpy as np
import ml_dtypes
from contextlib import ExitStack
import concourse.bass as bass
import concourse.mybir as mybir
from concourse.bass_utils import run_bass_kernel_spmd

F32 = mybir.dt.float32
BF16 = mybir.dt.bfloat16
I32 = mybir.dt.int32
U32 = mybir.dt.uint32
AF = mybir.ActivationFunctionType
ALU = mybir.AluOpType

NEG = -30000.0
T_SEQ = 4096
NSUB = 32
NTILE = 8
PAST = 16384
N_POOLPG = 5120
NC_USED = 8
PSEQ = 8 // NC_USED


class Sched:
    ENGS = ("pe", "dve", "act", "pool", "sp")

    def __init__(self, nc, stack, n_dma_sems=100):
        self.nc = nc
        self.sem = {e: stack.enter_context(nc.semaphore("s_" + e)) for e in self.ENGS}
        self.cnt = {e: 0 for e in self.ENGS}
        self.stack = stack
        self.dma_sem = {}
        self.dma_cnt = {}
        self.ops = {e: [] for e in self.ENGS}
        self.waited = {}
        self.last_w = {}
        self.readers = {}
        self.nops = 0

    def _dma_sem(self, key):
        if key not in self.dma_sem:
            self.dma_sem[key] = self.stack.enter_context(self.nc.semaphore("d%d" % len(self.dma_sem)))
            self.dma_cnt[key] = 0
        return self.dma_sem[key]

    def _deps(self, e, reads, writes):
        deps = {}

        def add(s, n):
            if deps.get(s, 0) < n:
                deps[s] = n
        for r in reads:
            d = self.last_w.get(r)
            if d is not None:
                add(*d)
        for w in writes:
            d = self.last_w.get(w)
            if d is not None:
                add(*d)
            for s, n in self.readers.get(w, {}).items():
                add(s, n)
        waits = []
        for s, n in deps.items():
            if e == "pe" and s == ("e", "pe"):
                continue
            k = (e, s)
            if self.waited.get(k, 0) < n:
                self.waited[k] = n
                waits.append((s, n))
        return waits

    def _semobj(self, s):
        return self.sem[s[1]] if s[0] == "e" else self.dma_sem[s[1]]

    def _record(self, me, reads, writes):
        s, n = me
        for r in reads:
            d = self.readers.setdefault(r, {})
            if d.get(s, 0) < n:
                d[s] = n
        for w in writes:
            self.last_w[w] = me
            self.readers[w] = {}

    def op(self, e, fn, reads=(), writes=()):
        waits = self._deps(e, reads, writes)
        self.cnt[e] += 1
        self._record((("e", e), self.cnt[e]), reads, writes)
        self.ops[e].append((waits, fn, self.sem[e], 1))
        self.nops += 1

    def dma(self, e, fn, key, reads=(), writes=()):
        sem = self._dma_sem(key)
        waits = self._deps(e, reads, writes)
        self.dma_cnt[key] += 16
        self._record((("d", key), self.dma_cnt[key]), reads, writes)
        self.ops[e].append((waits, fn, sem, 16))
        self.nops += 1

    def alias(self, old, new):
        deps = {}
        for k in old:
            d = self.last_w.get(k)
            if d is not None and deps.get(d[0], 0) < d[1]:
                deps[d[0]] = d[1]
            for s, n in self.readers.get(k, {}).items():
                if deps.get(s, 0) < n:
                    deps[s] = n
        for k in new:
            r = self.readers.setdefault(k, {})
            for s, n in deps.items():
                if r.get(s, 0) < n:
                    r[s] = n

    def barrier(self):
        targets = [(("e", e2), self.cnt[e2]) for e2 in self.ENGS if self.cnt[e2] > 0]
        targets += [(("d", k), n) for k, n in self.dma_cnt.items() if n > 0]
        for e in self.ENGS:
            waits = []
            for s_, n in targets:
                if e == "pe" and s_ == ("e", "pe"):
                    continue
                k = (e, s_)
                if self.waited.get(k, 0) < n:
                    self.waited[k] = n
                    waits.append((s_, n))
            self.ops[e].append((waits, None, None, 0))

    def final_waits(self, e, keys):
        waits = self._deps(e, keys, keys)
        self.ops[e].append((waits, None, None, 0))

    def emit(self, block):
        sched = self

        def mk(e):
            def body(engine):
                for waits, fn, sem, inc in sched.ops[e]:
                    for s, n in waits:
                        engine.wait_ge(sched._semobj(s), n)
                    if fn is not None:
                        fn(engine).then_inc(sem, inc)
            return body
        block.tensor(mk("pe"))
        block.vector(mk("dve"))
        block.scalar(mk("act"))
        block.gpsimd(mk("pool"))
        block.sync(mk("sp"))


POOL_WINDOWS = (2, 4, 8, 16)


def make_consts():
    bf = ml_dtypes.bfloat16
    c = {}
    c["c_ident"] = np.eye(128, dtype=np.float32).astype(bf)
    c["c_ident4"] = np.tile(np.eye(128, dtype=np.float32), (1, 4)).astype(bf)
    band = np.zeros((128, 12, 128), np.float32)
    i = np.arange(128)[:, None]
    j = np.arange(128)[None, :]
    for g, w in enumerate(POOL_WINDOWS):
        band[:, g * 3 + 0, :] = ((i <= j) & (i > j - w)) / w - (i == j)
        band[:, g * 3 + 1, :] = ((i - 128) > (j - w)) / w
        band[:, g * 3 + 2, :] = ((i <= j) & (i > j - w)) / np.minimum(j + 1, w) - (i == j)
    c["c_band"] = band.reshape(128, 12 * 128).astype(bf)
    tri = np.where(i <= j, 0.0, NEG).astype(np.float32)
    tri2 = np.where(i >= j, 0.0, NEG).astype(np.float32)
    c["c_tri"] = np.tile(tri, (1, 4)).astype(bf)
    c["c_tri2"] = np.tile(tri2, (1, 4)).astype(bf)
    key = np.arange(4096)[None, :]
    b = np.arange(64)[:, None]
    ind = np.zeros((128, 4096), np.float32)
    ind[64:128] = (key // 64 == b)
    c["c_ind"] = ind.astype(bf)
    ql = np.arange(128)[:, None]
    jj = np.arange(8)[None, :]
    c["c_cmq"] = np.where(ql >= 16 * jj + 15, 0.0, NEG).astype(np.float32)
    mpp = np.arange(504)[None, :] - 248
    c["c_cmpat"] = np.where(16 * mpp + 15 <= ql, 0.0, NEG).astype(np.float32).astype(bf)
    jp = np.arange(128)[None, :] - 64
    cc = (ql >= 64).astype(np.int64)
    G = np.where((jp == cc) | (jp == cc - 1), 100.0, np.where(jp > cc, -1000.0, 0.0))
    c["c_g"] = G.astype(np.float32)
    inv = 500000.0 ** (-np.arange(0, 16, 2, dtype=np.float32) / 16)
    pos = (np.arange(32)[None, :] * 128 + np.arange(128)[:, None]).astype(np.float32)
    ang = pos[:, :, None] * inv[None, None, :]
    c["c_cos"] = np.cos(ang).astype(np.float32).reshape(128, 256)
    c["c_sin"] = np.sin(ang).astype(np.float32).reshape(128, 256)
    angs = np.float32(PAST) * inv
    c["c_coss"] = np.tile(np.cos(angs).astype(np.float32)[None, :], (128, 1))
    c["c_sins"] = np.tile(np.sin(angs).astype(np.float32)[None, :], (128, 1))
    sband = np.zeros((128, 4, 4), np.float32)
    for b_ in range(4):
        for r_ in range(16):
            for g, w in enumerate(POOL_WINDOWS):
                sband[16 * b_ + r_, g, b_] = (1.0 / w if r_ >= 16 - w else 0.0) - (1.0 if r_ == 15 else 0.0)
    c["c_sband"] = sband.reshape(128, 16).astype(bf)
    rowsel = np.zeros((128, 4, 128), np.float32)
    for b_ in range(4):
        rowsel[b_, b_, :] = 1.0
    c["c_rowsel"] = rowsel.reshape(128, 512).astype(bf)
    pcol = np.zeros((128, 8), np.float32)
    pp = np.arange(128)
    pcol[:, 0] = pp < 64
    pcol[:, 1] = pp >= 64
    pcol[:, 2] = pp % 64
    for b_ in range(4):
        pcol[:, 3 + b_] = pp == b_
    pcol[:, 7] = pp
    c["c_pcol"] = pcol
    c["c_iota"] = np.tile(np.arange(128, dtype=np.float32)[None, :], (128, 1))
    return c


CONST_SPECS = {
    "c_sband": ([128, 16], BF16), "c_rowsel": ([128, 512], BF16), "c_pcol": ([128, 8], F32), "c_iota": ([128, 128], F32),
    "c_ident": ([128, 128], BF16), "c_ident4": ([128, 512], BF16), "c_band": ([128, 1536], BF16),
    "c_tri": ([128, 512], BF16), "c_tri2": ([128, 512], BF16), "c_ind": ([128, 4096], BF16),
    "c_cmq": ([128, 8], F32), "c_cmpat": ([128, 504], BF16), "c_g": ([128, 128], F32),
    "c_cos": ([128, 256], F32), "c_sin": ([128, 256], F32), "c_coss": ([128, 8], F32), "c_sins": ([128, 8], F32),
}

IN_SPECS = {
    "x": ([4096 * PSEQ, 1024], F32), "xs": ([4 * PSEQ, 1024], F32), "spool": ([60 * PSEQ, 1024], F32),
    "ccmp": ([N_POOLPG * 128, 512], F32), "csel": ([N_POOLPG * 128, 512], F32),
    "swin": ([2048 * PSEQ, 512], F32), "ptab": ([1, 512 * PSEQ], I32),
    "norm_mix": ([2, 1024], F32), "norm_ffn": ([2, 1024], F32), "pool_w": ([4, 256, 256], F32),
    "pool_scale": ([1, 1024], F32), "w_qg": ([1024, 1072], F32), "w_o": ([1024, 1024], F32),
    "norm_kv": ([1, 1024], F32), "w_kv": ([1024, 1536], F32), "cmp_pe": ([32, 2, 64], F32),
    "cmp_w1": ([2, 32, 64, 128], F32), "cmp_w2": ([2, 128, 64], F32),
    "mlp_up": ([2, 1024, 4096], F32), "mlp_down": ([2, 4096, 1024], F32), "norm_final": ([1, 1024], F32),
}
OUT_SPECS = {
    "y": [4096 * PSEQ, 1024], "ys": [4 * PSEQ, 1024], "cmp_p": [4096 * PSEQ, 512], "sel_p": [4096 * PSEQ, 512], "win_p": [512 * PSEQ, 512],
    "pool_p": [15 * PSEQ, 1024], "cmp_s": [4 * PSEQ, 512], "sel_s": [4 * PSEQ, 512], "win_s": [2048 * PSEQ, 512], "pool_s": [60 * PSEQ, 1024],
}

CH_UP = lambda l, fg: l * 8 + fg
CH_DN = lambda l, fg: 16 + l * 8 + fg
CH_KV = lambda j: 32 + j
CH_QG = lambda j: 35 + j
CH_O = lambda j: 38 + j
CH_C1 = lambda e: 40 + e
N_CH = 42


def build_program(n_tiles=NTILE, do_sample=True, stop=None, pool_pages=N_POOLPG, gather="allgather"):
    nc = bass.Bass("TRN2", target_bir_lowering=False)
    D = {}
    for k, (shp, dt) in IN_SPECS.items():
        if k in ("ccmp", "csel"):
            if not do_sample:
                continue
            if gather == "allgather":
                shp = [pool_pages * 128 // 8, 512]
            else:
                shp = [pool_pages * 128, 512]
        D[k] = nc.dram_tensor(k, shp, dt, kind="ExternalInput").ap()
    for k, (shp, dt) in CONST_SPECS.items():
        D[k] = nc.dram_tensor(k, shp, dt, kind="ExternalInput").ap()
    for k, shp in OUT_SPECS.items():
        D[k] = nc.dram_tensor(k, shp, F32, kind="ExternalOutput").ap()
    wscr = nc.dram_tensor("wscr", [N_CH, 128, 4096], BF16, kind="Internal").ap()
    if do_sample and gather == "allgather":
        cc_in = nc.dram_tensor("cc_in", [pool_pages * 128 // 8, 512], F32, kind="Internal").ap()
        cs_in = nc.dram_tensor("cs_in", [pool_pages * 128 // 8, 512], F32, kind="Internal").ap()
        CC = nc.dram_tensor("cc_full", [pool_pages * 128, 512], F32, kind="Internal").ap()
        CS = nc.dram_tensor("cs_full", [pool_pages * 128, 512], F32, kind="Internal").ap()
    elif do_sample:
        CC, CS = D["ccmp"], D["csel"]

    with ExitStack() as st:
        S = Sched(nc, st)
        total = [0]

        def sb(name, shape, dt):
            n = 1
            for s_ in shape[1:]:
                n *= s_
            total[0] += n * (2 if dt == BF16 else 4)
            return st.enter_context(nc.sbuf_tensor(name, shape, dt))

        selKT = sb("selKT", [128, 4, 4096], BF16)
        selV = sb("selV", [128, 32, 4, 65], BF16)
        winKT = sb("winKT", [128, 4, 1024], BF16)
        winV = sb("winV", [128, 8, 4, 65], BF16)
        cmpKT = sb("cmpKT", [128, 4, 288], BF16)
        cmpV = sb("cmpV", [128, 3, 4, 65], BF16)
        ident = sb("ident", [128, 128], BF16)
        ident4 = sb("ident4", [128, 512], BF16)
        band = sb("band", [128, 12, 128], BF16)
        tri = sb("tri", [128, 512], BF16)
        tri2 = sb("tri2", [128, 512], BF16)
        cmq = sb("cmq", [128, 8], F32)
        cmpat = sb("cmpat", [128, 504], BF16)
        gpat = sb("gpat", [128, 128], F32)
        cos_t = sb("cos_t", [128, 32, 8], F32)
        sin_t = sb("sin_t", [128, 32, 8], F32)
        g0bc = sb("g0bc", [128, 1024], F32)
        gfbc = sb("gfbc", [128, 1024], F32)
        poolW = sb("poolW", [128, 4, 2, 256], BF16)
        w2bf = sb("w2bf", [128, 2, 64], BF16)
        petot = sb("petot", [128, 2], F32)
        gcol = sb("gcol", [128, 5, 8], F32)
        wslot = [sb("wslot%d" % i, [128, 4096], BF16) for i in range(3)]
        h = sb("h", [128, 4, 1024], F32)
        nbf = [sb("nbf%d" % i, [128, 1024], BF16) for i in range(2)]
        ubf = [sb("ubf%d" % i, [128, 1024], BF16) for i in range(2)]
        nT = sb("nT", [128, 8, 512], BF16)
        big = sb("big", [128, 16384], BF16)
        junk = sb("junk", [128, 1024], BF16)
        ssq = sb("ssq", [128, 8], F32)
        rstd = sb("rstd", [128, 8], F32)
        relu_t = [sb("relu%d" % i, [128, 512], F32) for i in range(2)]
        kvf = sb("kvf", [128, 1536], F32)
        kvb = sb("kvb", [128, 1536], BF16)
        rt = sb("rt", [128, 4, 16, 8], F32)
        parts = sb("parts", [128, 2, 2, 4, 32], F32)
        carry = sb("carry", [128, 2, 4], F32)
        hid = sb("hid", [128, 2, 4, 32], F32)
        hid2 = sb("hid2", [128, 2, 4, 32], F32)
        hidb = sb("hidb", [128, 2, 4, 32], BF16)
        Ebuf = sb("Ebuf", [128, 4, 264], F32)
        esum = sb("esum", [128, 8], F32)
        imp = sb("imp", [128, 264], F32)
        score = sb("score", [128, 64], F32)
        score2 = sb("score2", [128, 64], F32)
        mx = sb("mx", [128, 16], F32)
        mbfull = sb("mbfull", [128, 128], BF16)
        gate = sb("gate", [128, 4, 48], F32)
        den = sb("den", [128, 3, 4], F32)
        wgt = sb("wgt", [128, 3, 4], F32)
        ocomb = sb("ocomb", [128, 3, 4, 64], F32)
        rowsel = sb("rowsel", [128, 4, 128], BF16)
        pcol = sb("pcol", [128, 8], F32)
        sband = sb("sband", [128, 4, 4], BF16)
        coss = sb("coss", [128, 8], F32)
        sins = sb("sins", [128, 8], F32)
        ps = st.enter_context(nc.psum_tensor("ps", [128, 8, 512], F32))

        aT = big[:, :].rearrange("p (f t) -> p f t", t=512)
        cmpXT = big[:, 0:4096].rearrange("p (e g t) -> p e g t", e=2, g=4)
        CX_KEYS = ["cmpXT%d" % s_ for s_ in range(4)]
        ysb = kvf[:, 0:1024]
        q_bf = big[:, 0:4096].rearrange("p (s c) -> p s c", c=1024)
        qT = [big[:, 4096 + i * 2048: 4096 + (i + 1) * 2048].rearrange("p (hh t) -> p hh t", t=128) for i in range(2)]
        oT = big[:, 8192:12288].rearrange("p (k t) -> p k t", t=512)
        o_bf = big[:, 12288:13312]
        PT = [big[:, 13312 + i * 512: 13312 + (i + 1) * 512] for i in range(2)]
        stage = [big[:, i * 8192:(i + 1) * 8192].bitcast(F32) for i in range(2)]
        AT_KEYS = ["aT%d" % f for f in range(32)]
        ATT_KEYS = ["q_bf%d" % s for s in range(4)] + ["qT0", "qT1", "qTm0", "qTm1", "o_bf", "PT0", "PT1"] + ["oT%d" % s for s in range(4)]
        STG_KEYS = ["stage0", "stage1"]
        print("SBUF bytes/partition:", total[0])

        blk = st.enter_context(nc.Block())

        def P(b):
            return ps[:, b, :]

        def Pb(b):
            return ps[:, b, :].bitcast(BF16)

        def act(out, in_, func, r, w, **kw):
            S.op("act", lambda e: e.activation(out=out, in_=in_, func=func, **kw), r, w)

        def acopy(out, in_, r, w):
            S.op("act", lambda e: e.copy(out=out, in_=in_), r, w)

        def cp(eng, out, in_, r, w):
            if eng == "act":
                return acopy(out, in_, r, w)
            S.op(eng, lambda e: e.tensor_copy(out=out, in_=in_), r, w)

        def tt(eng, out, in0, in1, op, r, w):
            S.op(eng, lambda e: e.tensor_tensor(out=out, in0=in0, in1=in1, op=op), r, w)

        def ts(eng, out, in0, s1, s2, op0, op1, r, w):
            if op1 is None:
                S.op(eng, lambda e: e.tensor_scalar(out=out, in0=in0, scalar1=s1, scalar2=None, op0=op0), r, w)
            else:
                S.op(eng, lambda e: e.tensor_scalar(out=out, in0=in0, scalar1=s1, scalar2=s2, op0=op0, op1=op1), r, w)

        def stt(out, in0, scalar, in1, op0, op1, r, w):
            S.op("dve", lambda e: e.scalar_tensor_tensor(out=out, in0=in0, scalar=scalar, in1=in1, op0=op0, op1=op1), r, w)

        def mm(out, lhsT, rhs, start, stop, r, w):
            S.op("pe", lambda e: e.matmul(out, lhsT=lhsT, rhs=rhs, start=start, stop=stop, skip_group_check=True), r, w)

        def tr(out, in_, r, w, idn=None):
            idn_ = ident[:] if idn is None else idn
            S.op("pe", lambda e: e.transpose(out=out, in_=in_, identity=idn_), list(r) + ["ident"], w)

        def memset(eng, ap, val, w):
            S.op(eng, lambda e: e.memset(ap, val), (), w)

        def ld(out, in_, key, w, r=(), q="sp", **kw):
            S.dma(q, lambda e: e.dma_start(out=out, in_=in_, **kw), key, r, w)

        def stq(out, in_, key, r, w, **kw):
            S.dma("pool", lambda e: e.dma_start(out=out, in_=in_, **kw), key, r, w)

        ld(ident[:], D["c_ident"], "c0", ["ident"])
        ld(ident4[:], D["c_ident4"], "c0", ["ident4"])
        ld(band[:].rearrange("p a b -> p (a b)"), D["c_band"], "c0", ["band"])
        ld(tri[:], D["c_tri"], "c0", ["tri"])
        ld(tri2[:], D["c_tri2"], "c0", ["tri2"])
        ld(cmq[:], D["c_cmq"], "c0", ["cmq"])
        ld(cmpat[:], D["c_cmpat"], "c0", ["cmpat"])
        ld(gpat[:], D["c_g"], "c0", ["gpat"])
        ld(cos_t[:].rearrange("p a b -> p (a b)"), D["c_cos"], "c0", ["cos"])
        ld(sin_t[:].rearrange("p a b -> p (a b)"), D["c_sin"], "c0", ["sin"])
        ld(rowsel[:].rearrange("p a b -> p (a b)"), D["c_rowsel"], "c0", ["rowsel"])
        ld(pcol[:], D["c_pcol"], "c0", ["pcol"])
        ld(sband[:].rearrange("p a b -> p (a b)"), D["c_sband"], "c0", ["sband"])
        ld(coss[:], D["c_coss"], "c0", ["coss"])
        ld(sins[:], D["c_sins"], "c0", ["sins"])
        if do_sample and gather == "allgather":
            rows_sh = pool_pages * 128 // 8
            nbig = rows_sh * 512 // 16384
            ld(cc_in.rearrange("(a b) c -> a (b c)", a=nbig), D["ccmp"].rearrange("(a b) c -> a (b c)", a=nbig), "ag_cp0", ["cc_in"])
            ld(cs_in.rearrange("(a b) c -> a (b c)", a=nbig), D["csel"].rearrange("(a b) c -> a (b c)", a=nbig), "ag_cp1", ["cs_in"])
            S.dma("pool", lambda e: e.collective_compute("AllGather", ALU.bypass, [list(range(8))], ins=[cc_in], outs=[CC]), "ag0", ["cc_in"], ["CC"])
            S.dma("pool", lambda e: e.collective_compute("AllGather", ALU.bypass, [list(range(8))], ins=[cs_in], outs=[CS]), "ag1", ["cs_in"], ["CS"])
        ld(g0bc[:], D["norm_mix"][0:1, :].partition_broadcast(128), "c0", ["g0bc"])
        ld(gfbc[:], D["norm_final"].partition_broadcast(128), "c0", ["gfbc"])
        for a in range(4):
            ld(selKT[64:128, a, :], D["c_ind"][64:128, :], "c0", ["selKT_ind"])
        ld(gcol[:, 0, :], D["norm_ffn"][0, :].rearrange("(kc p) -> p kc", p=128), "c0", ["gcol"], allow_slow_non_contiguous=True)
        ld(gcol[:, 1, :], D["norm_ffn"][1, :].rearrange("(kc p) -> p kc", p=128), "c0", ["gcol"], allow_slow_non_contiguous=True)
        ld(gcol[:, 2, :], D["norm_kv"][0, :].rearrange("(kc p) -> p kc", p=128), "c0", ["gcol"], allow_slow_non_contiguous=True)
        ld(gcol[:, 3, :], D["norm_mix"][1, :].rearrange("(kc p) -> p kc", p=128), "c0", ["gcol"], allow_slow_non_contiguous=True)
        ts("dve", gcol[:, 4, :], gcol[:, 3, :], 0.125, None, ALU.mult, None, ["gcol"], ["gcol"])
        memset("dve", selV[:, :, :, 64:65], 1.0, ["selV_ones"])
        memset("dve", winV[:, :, :, 64:65], 1.0, ["winV_ones"])
        memset("dve", cmpV[:], 0.0, ["cmpV_init"])
        memset("dve", cmpKT[:], 0.0, ["cmpKT_init"])
        memset("dve", imp[:], 0.0, ["imp"])
        memset("dve", Ebuf[:], 0.0, ["E"])
        memset("dve", carry[:], 0.0, ["carry"])
        memset("dve", mbfull[:], 0.0, ["mbfull"])

        conv_i = [0]

        def convert(chunk, pairs, scale_cols=None, nparts=128, width=4096, inner=512):
            i = conv_i[0] % 2
            conv_i[0] += 1
            sk, wk = "stage%d" % i, "wslot%d" % i
            stg = stage[i]
            for (dst, src) in pairs:
                ld(dst(stg), src, "stg%d" % i, [sk])
            eng = "dve" if (conv_i[0] % 2) else "pool"
            if scale_cols is None:
                cp(eng, wslot[i][0:nparts, 0:width], stg[0:nparts, 0:width], [sk], [wk])
            else:
                nk = width // inner
                for kc in range(nk):
                    ts(eng, wslot[i][0:nparts, kc * inner:(kc + 1) * inner], stg[0:nparts, kc * inner:(kc + 1) * inner],
                       gcol[:, scale_cols, kc:kc + 1], None, ALU.mult, None, [sk, "gcol"], [wk])
            stq(wscr[chunk, 0:nparts, 0:width], wslot[i][0:nparts, 0:width], "wst%d" % i, [wk], ["scr%d" % chunk])

        for l in range(2):
            for fg in range(8):
                convert(CH_UP(l, fg), [(lambda s_: s_[:, :].rearrange("p (kc n) -> p kc n", n=512),
                                        D["mlp_up"][l, :, fg * 512:(fg + 1) * 512].rearrange("(kc p) n -> p kc n", p=128))], scale_cols=l)
            for fg in range(8):
                convert(CH_DN(l, fg), [(lambda s_: s_[:, :].rearrange("p (fc n) -> p fc n", n=1024),
                                        D["mlp_down"][l, fg * 512:(fg + 1) * 512, :].rearrange("(fc p) n -> p fc n", p=128))])
        for j in range(3):
            convert(CH_KV(j), [(lambda s_: s_[:, :].rearrange("p (kc n) -> p kc n", n=512),
                                D["w_kv"][:, j * 512:(j + 1) * 512].rearrange("(kc p) n -> p kc n", p=128))], scale_cols=2)
        for j in range(2):
            convert(CH_QG(j), [(lambda s_: s_[:, :].rearrange("p (kc n) -> p kc n", n=512),
                                D["w_qg"][:, j * 512:(j + 1) * 512].rearrange("(kc p) n -> p kc n", p=128))], scale_cols=4)
        convert(CH_QG(2), [(lambda s_: s_[:, 0:384].rearrange("p (kc n) -> p kc n", n=48),
                            D["w_qg"][:, 1024:1072].rearrange("(kc p) n -> p kc n", p=128))], scale_cols=3, width=384, inner=48)
        for j in range(2):
            convert(CH_O(j), [(lambda s_: s_[:, :].rearrange("p (kc n) -> p kc n", n=512),
                               D["w_o"][:, j * 512:(j + 1) * 512].rearrange("(kc p) n -> p kc n", p=128))])
        for e_ in range(2):
            convert(CH_C1(e_), [(lambda s_: s_[0:64, :].rearrange("p (r k) -> p r k", k=128),
                                 D["cmp_w1"][e_].rearrange("r h k -> h r k"))], nparts=64)
        i_ = conv_i[0] % 2
        conv_i[0] += 1
        stg = stage[i_]
        ld(stg[:, 0:2048].rearrange("p (g kc d) -> p g kc d", g=4, kc=2), D["pool_w"].rearrange("g (kc p) d -> p g kc d", p=128), "stg%d" % i_, ["stage%d" % i_])
        ld(stg[:, 2048:3072], D["pool_scale"].partition_broadcast(128), "stg%d" % i_, ["stage%d" % i_])
        for g in range(4):
            for kc in range(2):
                tt("dve", poolW[:, g, kc, :], stg[:, (g * 2 + kc) * 256:(g * 2 + kc + 1) * 256], stg[:, 2048 + g * 256: 2048 + (g + 1) * 256],
                   ALU.mult, ["stage%d" % i_], ["poolW"])
        i_ = conv_i[0] % 2
        conv_i[0] += 1
        stg = stage[i_]
        ld(stg[:, 0:128].rearrange("p (e h) -> p e h", e=2), D["cmp_w2"].rearrange("e k h -> k e h"), "stg%d" % i_, ["stage%d" % i_])
        cp("dve", w2bf[:].rearrange("p e h -> p (e h)"), stg[:, 0:128], ["stage%d" % i_], ["w2bf"])
        wctr = [0]

        def wload(chunk, nparts=128, width=4096):
            i = wctr[0] % 3
            wctr[0] += 1
            ld(wslot[i][0:nparts, 0:width], wscr[chunk, 0:nparts, 0:width], "wl%d" % i, ["wslot%d" % i], r=["scr%d" % chunk])
            return wslot[i], "wslot%d" % i

        pe_b = sb("pe_b", [128, 32, 2], BF16)
        pe_f = sb("pe_f", [128, 32, 2], F32)
        ld(pe_f[0:64, :, :], D["cmp_pe"].rearrange("r e h -> h r e"), "c0", ["pe_f"], allow_slow_non_contiguous=True)
        cp("dve", pe_b[0:64].rearrange("p r e -> p (r e)"), pe_f[0:64].rearrange("p r e -> p (r e)"), ["pe_f"], ["pe_b"])
        for e_ in range(2):
            W, wk = wload(CH_C1(e_), nparts=64)
            for row in range(32):
                mm(ps[:, 7, e_:e_ + 1], W[0:64, row * 128:(row + 1) * 128], pe_b[0:64, row, e_:e_ + 1], row == 0, row == 31,
                   [wk, "pe_b"], ["ps7"])
        cp("dve", petot[:], ps[:, 7, 0:2], ["ps7"], ["petot"])

        S.alias(STG_KEYS, AT_KEYS + ATT_KEYS)

        def rms(src, src_keys, col):
            act(junk[:], src, AF.Square, src_keys, ["junk", "ssq%d" % col], accum_out=ssq[:, col:col + 1])
            act(rstd[:, col:col + 1], ssq[:, col:col + 1], AF.Sqrt, ["ssq%d" % col], ["rstd%d" % col], scale=1.0 / 1024, bias=1e-6)
            S.op("dve", lambda e: e.reciprocal(out=rstd[:, col:col + 1], in_=rstd[:, col:col + 1]), ["rstd%d" % col], ["rstd%d" % col])

        def norm_T(s, col, nb, ncols_valid=128):
            rms(h[:, s, :], ["h%d" % s], col)
            ts("dve", nbf[nb][:], h[:, s, :], rstd[:, col:col + 1], None, ALU.mult, None, ["h%d" % s, "rstd%d" % col], ["nbf%d" % nb])
            for kc in range(8):
                tr(Pb(4)[:, kc * 128:(kc + 1) * 128], nbf[nb][:, kc * 128:(kc + 1) * 128], ["nbf%d" % nb], ["ps4"])
            acopy(nT[:, :, s * 128:(s + 1) * 128], Pb(4).rearrange("p (k t) -> p k t", t=128), ["ps4"], ["nT%d" % s])

        def mlp(l, nsub):
            S.alias(ATT_KEYS, AT_KEYS)
            ntok = nsub * 128
            for s in range(nsub):
                norm_T(s, s, s % 2)
            for fg in range(8):
                W, wk = wload(CH_UP(l, fg))
                Wv = W[:, :].rearrange("p (kc n) -> p kc n", n=512)
                for fc in range(4):
                    f = fg * 4 + fc
                    b = f % 4
                    for kc in range(8):
                        mm(P(b)[:, 0:ntok], Wv[:, kc, fc * 128:(fc + 1) * 128], nT[:, kc, 0:ntok], kc == 0, kc == 7,
                           [wk] + ["nT%d" % s for s in range(nsub)], ["ps%d" % b])
                    rl = relu_t[f % 2]
                    act(rl[:, 0:ntok], P(b)[:, 0:ntok], AF.Relu, ["ps%d" % b], ["relu%d" % (f % 2)])
                    tt("pool", aT[:, f, 0:ntok], rl[:, 0:ntok], rl[:, 0:ntok], ALU.mult, ["relu%d" % (f % 2)], ["aT%d" % f])
            for fg in range(8):
                W, wk = wload(CH_DN(l, fg))
                Wv = W[:, :].rearrange("p (fc n) -> p fc n", n=1024)
                for fc in range(4):
                    f = fg * 4 + fc
                    for s in range(nsub):
                        for hf in range(2):
                            b = 2 * s + hf
                            mm(P(b), aT[:, f, s * 128:(s + 1) * 128], Wv[:, fc, hf * 512:(hf + 1) * 512], f == 0, f == 31,
                               [wk, "aT%d" % f], ["ps%d" % b])
            for s in range(nsub):
                tt("dve", h[:, s, :].rearrange("p (a n) -> p a n", n=512), ps[:, 2 * s:2 * s + 2, :], h[:, s, :].rearrange("p (a n) -> p a n", n=512),
                   ALU.add, ["ps%d" % (2 * s), "ps%d" % (2 * s + 1), "h%d" % s], ["h%d" % s])

        def rope(x1, x2, cosb, sinb, shape, keys_r, keys_w):
            a, b_ = shape
            t1 = rt[:, 0, 0:a * b_, :].rearrange("p (a b) d -> p a b d", a=a)
            t2 = rt[:, 1, 0:a * b_, :].rearrange("p (a b) d -> p a b d", a=a)
            t3 = rt[:, 2, 0:a * b_, :].rearrange("p (a b) d -> p a b d", a=a)
            t4 = rt[:, 3, 0:a * b_, :].rearrange("p (a b) d -> p a b d", a=a)
            tt("dve", t1, x1, cosb, ALU.mult, keys_r, ["rt0"])
            tt("dve", t2, x2, sinb, ALU.mult, keys_r, ["rt1"])
            tt("dve", t3, x2, cosb, ALU.mult, keys_r, ["rt2"])
            tt("dve", t4, x1, sinb, ALU.mult, keys_r, ["rt3"])
            tt("dve", x1, t1, t2, ALU.subtract, ["rt0", "rt1"], keys_w)
            tt("dve", x2, t3, t4, ALU.add, ["rt2", "rt3"], keys_w)

        class _Stop(Exception):
            pass

        def chk(name):
            if stop == name:
                raise _Stop()

        def main_loop(sq=0):
          nb_ctr = [0]
          DV = {k_: D[k_][sq * r_:(sq + 1) * r_] for k_, r_ in (("x", 4096), ("y", 4096), ("cmp_p", 4096), ("sel_p", 4096), ("win_p", 512), ("pool_p", 15))}
          if sq > 0:
              allk = ["cmpKT%d" % t_ for t_ in range(8)] + ["cmpV%d" % t_ for t_ in range(8)]
              memset("dve", cmpV[:], 0.0, ["cmpV_init"] + allk)
              memset("dve", cmpKT[:], 0.0, ["cmpKT_init"] + allk)
              memset("dve", imp[:], 0.0, ["imp"])
              memset("dve", carry[:], 0.0, ["carry"])
          for T in range(n_tiles if stop != "prologue" else 0):
              for s in range(4):
                  i = 4 * T + s
                  ld(h[:, s, :], DV["x"][i * 128:(i + 1) * 128, :], "xld%d" % s, ["h%d" % s])
              for s in range(4):
                  i = 4 * T + s
                  nb = nb_ctr[0] % 2
                  nb_ctr[0] += 1
                  rms(h[:, s, :], ["h%d" % s], s)
                  stt(ubf[nb][:], h[:, s, :], rstd[:, s:s + 1], g0bc[:], ALU.mult, ALU.mult, ["h%d" % s, "rstd%d" % s, "g0bc"], ["ubf%d" % nb])
                  if i == NSUB - 1:
                      stt(ysb, h[:, s, :], rstd[:, s:s + 1], g0bc[:], ALU.mult, ALU.mult, ["h%d" % s, "rstd%d" % s, "g0bc"], ["kvf0", "kvf1"])
                      stq(DV["pool_p"], kvf[113:128, 0:1024], "st_misc", ["kvf0", "kvf1"], ["out_pool_p"])
                  for c in range(8):
                      g = c // 2
                      b = c // 4
                      o = ps[:, b, (c % 4) * 128:(c % 4 + 1) * 128]
                      if i == 0:
                          mm(o, ubf[nb][:, c * 128:(c + 1) * 128], band[:, g * 3 + 2, :], True, True, ["ubf%d" % nb, "band"], ["ps%d" % b])
                      else:
                          mm(o, ubf[nb][:, c * 128:(c + 1) * 128], band[:, g * 3 + 0, :], True, False, ["ubf%d" % nb, "band"], ["ps%d" % b])
                          mm(o, ubf[1 - nb][:, c * 128:(c + 1) * 128], band[:, g * 3 + 1, :], False, True, ["ubf%d" % (1 - nb), "band"], ["ps%d" % b])
                  dTb = nT[:, :, 0:128]
                  acopy(dTb, ps[:, 0:2, :].rearrange("p b (c t) -> p (b c) t", t=128), ["ps0", "ps1"], ["nT0"])
                  for g in range(4):
                      b = 2 + g // 2
                      for kc in range(2):
                          mm(ps[:, b, (g % 2) * 256:(g % 2 + 1) * 256], dTb[:, 2 * g + kc, :], poolW[:, g, kc, :], kc == 0, kc == 1,
                             ["nT0", "poolW"], ["ps%d" % b])
                  tt("dve", h[:, s, :].rearrange("p (a n) -> p a n", n=512), ps[:, 2:4, :], h[:, s, :].rearrange("p (a n) -> p a n", n=512), ALU.add,
                     ["ps2", "ps3", "h%d" % s], ["h%d" % s])
              chk("l0")
              mlp(0, 4)
              chk("mlp0")
              S.alias(AT_KEYS, CX_KEYS)
              for s in range(4):
                  norm_T(s, s, s % 2)
              kvW = []
              for s in range(4):
                  i = 4 * T + s
                  for j in range(3):
                      if s == 0:
                          kvW.append(wload(CH_KV(j)))
                      W, wk = kvW[j]
                      Wv = W[:, :].rearrange("p (kc n) -> p kc n", n=512)
                      for kc in range(8):
                          mm(P(j), nT[:, kc, s * 128:(s + 1) * 128], Wv[:, kc, :], kc == 0, kc == 7, [wk, "nT%d" % s], ["ps%d" % j])
                      cp("act" if j != 1 else "dve", kvf[:, j * 512:(j + 1) * 512], P(j), ["ps%d" % j], ["kvf%d" % j])
                  kv5 = kvf[:, :].rearrange("p (br e g d) -> p br e g d", br=3, e=2, g=4)
                  x1 = kv5[:, :, 0, :, 0:8]
                  x2 = kv5[:, :, 0, :, 8:16]
                  cosb = cos_t[:, i, :].unsqueeze(1).unsqueeze(1).to_broadcast([128, 3, 4, 8])
                  sinb = sin_t[:, i, :].unsqueeze(1).unsqueeze(1).to_broadcast([128, 3, 4, 8])
                  rope(x1, x2, cosb, sinb, (3, 4), ["kvf0", "kvf1", "kvf2", "cos", "sin"], ["kvf0", "kvf1", "kvf2"])
                  chk("kv1")
                  stq(DV["cmp_p"][i * 128:(i + 1) * 128, :], kvf[:, 0:512], "st_kv0", ["kvf0"], ["out_cmp_p"])
                  stq(DV["sel_p"][i * 128:(i + 1) * 128, :], kvf[:, 512:1024], "st_kv1", ["kvf1"], ["out_sel_p"])
                  if i >= NSUB - 4:
                      ii = i - (NSUB - 4)
                      stq(DV["win_p"][ii * 128:(ii + 1) * 128, :], kvf[:, 1024:1536], "st_kv2", ["kvf2"], ["out_win_p"])
                  chk("kv2")
                  cp("pool", kvb[:], kvf[:], ["kvf0", "kvf1", "kvf2"], ["kvb"])
                  kb5 = kvb[:, :].rearrange("p (br e g d) -> p br e g d", br=3, e=2, g=4)
                  for g in range(4):
                      tr(Pb(5)[0:64, g * 128:(g + 1) * 128], kb5[:, 1, 0, g, :], ["kvb"], ["ps5"])
                  for g in range(4):
                      tr(Pb(5)[0:64, (4 + g) * 128:(5 + g) * 128], kb5[:, 2, 0, g, :], ["kvb"], ["ps5"])
                  chk("kv2b")
                  acopy(selKT[0:64, :, i * 128:(i + 1) * 128], Pb(5)[0:64, 0:512].rearrange("p (g t) -> p g t", t=128), ["ps5"], ["selKT%d" % i])
                  wsl = i % 8
                  cp("act", winKT[0:64, :, wsl * 128:(wsl + 1) * 128], Pb(5)[0:64, 512:1024].rearrange("p (g t) -> p g t", t=128), ["ps5"], ["winKT%d" % wsl])
                  chk("kv2c")
                  cp("act", selV[:, i, :, 0:64], kb5[:, 1, 1, :, :], ["kvb", "selV_ones"], ["selV%d" % i])
                  cp("act", winV[:, wsl, :, 0:64], kb5[:, 2, 1, :, :], ["kvb", "winV_ones"], ["winV%d" % wsl])
                  chk("kv3")
                  for e_ in range(2):
                      for g in range(4):
                          tr(Pb(6)[0:64, (e_ * 4 + g) * 128:(e_ * 4 + g + 1) * 128], kb5[:, 0, e_, g, :], ["kvb"], ["ps6"])
                  acopy(cmpXT[0:64, :, :, s * 128:(s + 1) * 128], Pb(6)[0:64, :].rearrange("p (e g t) -> p e g t", e=2, g=4), ["ps6"], ["cmpXT%d" % s])
              chk("kv4")
              for e_ in range(2):
                  W, wk = wload(CH_C1(e_), nparts=64)
                  for p_ in range(2):
                      for r_ in range(16):
                          row = p_ * 16 + r_
                          rhs = cmpXT[0:64, e_, :, :].rearrange("p g (sg r) -> p g sg r", r=16)[:, :, :, r_]
                          mm(ps[:, 7, (e_ * 2 + p_) * 128:(e_ * 2 + p_ + 1) * 128].rearrange("p (g sg) -> p g sg", g=4),
                             W[0:64, row * 128:(row + 1) * 128], rhs, r_ == 0, r_ == 15,
                             [wk] + ["cmpXT%d" % s for s in range(4)], ["ps7"])
              cp("dve", parts[:].rearrange("p e q g sg -> p (e q g sg)"), P(7), ["ps7"], ["parts"])
              chk("kv5")
              tt("dve", hid[:, :, :, 1:32], parts[:, :, 0, :, 0:31], parts[:, :, 1, :, 1:32], ALU.add, ["parts"], ["hid"])
              tt("dve", hid[:, :, :, 0:1], carry[:].unsqueeze(3), parts[:, :, 1, :, 0:1], ALU.add, ["parts", "carry"], ["hid"])
              cp("dve", carry[:].unsqueeze(3), parts[:, :, 0, :, 31:32], ["parts", "hid"], ["carry"])
              for e_ in range(2):
                  ts("dve", hid[:, e_], hid[:, e_], petot[:, e_:e_ + 1], None, ALU.add, None, ["hid", "petot"], ["hid"])
              hf_ = hid[:].rearrange("p e g s -> p (e g s)")
              h2_ = hid2[:].rearrange("p e g s -> p (e g s)")
              tt("dve", h2_, hf_, hf_, ALU.mult, ["hid"], ["hid2"])
              ts("dve", h2_, h2_, 0.044715, 1.0, ALU.mult, ALU.add, ["hid2"], ["hid2"])
              tt("dve", h2_, h2_, hf_, ALU.mult, ["hid2", "hid"], ["hid2"])
              act(h2_, h2_, AF.Exp, ["hid2"], ["hid2"], scale=-1.5957691216)
              ts("dve", h2_, h2_, 1.0, None, ALU.add, None, ["hid2"], ["hid2"])
              S.op("dve", lambda e, a=h2_: e.reciprocal(out=a, in_=a), ["hid2"], ["hid2"])
              tt("dve", hidb[:].rearrange("p e g s -> p (e g s)"), h2_, hf_, ALU.mult, ["hid2", "hid"], ["hidb"])
              chk("kv6")
              mm(ps[0:64, 7, 0:128], w2bf[:, 0, :], hidb[:, 0].rearrange("p g s -> p (g s)"), True, True, ["w2bf", "hidb"], ["ps7"])
              cp("dve", cmpKT[0:64, :, 32 * T:32 * T + 32], ps[0:64, 7, 0:128].rearrange("p (g s) -> p g s", g=4), ["ps7", "cmpKT_init"], ["cmpKT%d" % T])
              pq = 32 * (T % 3)
              for g in range(4):
                  mm(ps[pq:pq + 32, 7, 128 + g * 64:128 + (g + 1) * 64], hidb[:, 1, g, :], w2bf[:, 1, :], True, True, ["w2bf", "hidb"], ["ps7"])
              cp("dve", cmpV[pq:pq + 32, T // 3, :, 0:64], ps[pq:pq + 32, 7, 128:384].rearrange("p (g d) -> p g d", g=4), ["ps7", "cmpV_init"], ["cmpV%d" % T])
              memset("dve", cmpV[pq:pq + 32, T // 3, :, 64:65], 1.0, ["cmpV%d" % T])
              if T == 0:
                  memset("dve", cmpV[0:1, 0, :, :], 0.0, ["cmpV0"])

              chk("kv")
              S.alias(AT_KEYS + CX_KEYS, ATT_KEYS)
              for s in range(4):
                  norm_T(s, s, s % 2)
              qW = []
              for s in range(4):
                  i = 4 * T + s
                  for j in range(3):
                      if s == 0:
                          qW.append(wload(CH_QG(j), width=4096 if j < 2 else 384))
                      W, wk = qW[j]
                      if j < 2:
                          Wv = W[:, :].rearrange("p (kc n) -> p kc n", n=512)
                          for kc in range(8):
                              mm(P(j), nT[:, kc, s * 128:(s + 1) * 128], Wv[:, kc, :], kc == 0, kc == 7, [wk, "nT%d" % s], ["ps%d" % j])
                          cp("act" if j == 0 else "dve", kvf[:, j * 512:(j + 1) * 512], P(j), ["ps%d" % j], ["kvf%d" % j])
                      else:
                          Wv = W[:, 0:384].rearrange("p (kc n) -> p kc n", n=48)
                          for kc in range(8):
                              mm(P(2)[:, 0:48], nT[:, kc, s * 128:(s + 1) * 128], Wv[:, kc, :], kc == 0, kc == 7, [wk, "nT%d" % s], ["ps2"])
                          act(gate[:, s, :], P(2)[:, 0:48], AF.Exp, ["ps2"], ["gate%d" % s], scale=-1.0)
                          ts("dve", gate[:, s, :], gate[:, s, :], 1.0, None, ALU.add, None, ["gate%d" % s], ["gate%d" % s])
                          S.op("dve", lambda e, s=s: e.reciprocal(out=gate[:, s, :], in_=gate[:, s, :]), ["gate%d" % s], ["gate%d" % s])
                  q3 = kvf[:, 0:1024].rearrange("p (hh d) -> p hh d", d=64)
                  x1 = q3[:, :, 0:8].unsqueeze(1)
                  x2 = q3[:, :, 8:16].unsqueeze(1)
                  cosb = cos_t[:, i, :].unsqueeze(1).unsqueeze(1).to_broadcast([128, 1, 16, 8])
                  sinb = sin_t[:, i, :].unsqueeze(1).unsqueeze(1).to_broadcast([128, 1, 16, 8])
                  rope(x1, x2, cosb, sinb, (1, 16), ["kvf0", "kvf1", "cos", "sin"], ["kvf0", "kvf1"])
                  cp("pool", q_bf[:, s, :], kvf[:, 0:1024], ["kvf0", "kvf1"], ["q_bf%d" % s])

              chk("qg")
              for s in range(4):
                  i = 4 * T + s
                  qb = i % 2
                  qTt = qT[qb]
                  qk, qmk = "qT%d" % qb, "qTm%d" % qb
                  for hh in range(16):
                      bnk = 4 + hh // 8
                      tr(Pb(bnk)[0:64, (hh % 8) * 128:(hh % 8 + 1) * 128], q_bf[:, s, hh * 64:(hh + 1) * 64], ["q_bf%d" % s], ["ps%d" % bnk])
                  acopy(qTt[0:64, 0:8, :], Pb(4)[0:64, :].rearrange("p (hh t) -> p hh t", t=128), ["ps4"], [qk])
                  cp("act", qTt[0:64, 8:16, :], Pb(5)[0:64, :].rearrange("p (hh t) -> p hh t", t=128), ["ps5"], [qk])
                  ncm = 8 * i + 8
                  for g in range(4):
                      hs = slice(4 * g, 4 * g + 4)
                      if i >= 8:
                          for hh in range(4):
                              bnk = hh // 2
                              mm(ps[:, bnk, (hh % 2) * 256:(hh % 2) * 256 + ncm], qTt[0:64, 4 * g + hh, :], cmpKT[0:64, g, 0:ncm], True, True,
                                 [qk] + ["cmpKT%d" % t for t in range(T + 1)], ["ps%d" % bnk])
                          sc4 = ps[:, 0:2, :].rearrange("p b (hh m) -> p (b hh) m", m=256)
                          tt("dve", sc4[:, :, ncm - 8:ncm], sc4[:, :, ncm - 8:ncm], cmq[:, :].unsqueeze(1).to_broadcast([128, 4, 8]), ALU.add,
                             ["ps0", "ps1", "cmq"], ["ps0", "ps1"])
                          ts("dve", sc4[:, :, 0:1], sc4[:, :, 0:1], NEG, None, ALU.add, None, ["ps0", "ps1"], ["ps0", "ps1"])
                          for hh in range(4):
                              act(Ebuf[:, hh, 0:ncm], sc4[:, hh, 0:ncm], AF.Exp, ["ps0", "ps1"], ["E", "esum"], accum_out=esum[:, hh:hh + 1])
                          ts("dve", esum[:, 0:4], esum[:, 0:4], 1e-30, None, ALU.max, None, ["esum"], ["esum"])
                          S.op("dve", lambda e: e.reciprocal(out=esum[:, 4:8], in_=esum[:, 0:4]), ["esum"], ["esum"])
                          ts("dve", imp[:, 0:ncm], Ebuf[:, 0, 0:ncm], esum[:, 4:5], None, ALU.mult, None, ["E", "esum"], ["imp"])
                          for hh in range(1, 4):
                              stt(imp[:, 0:ncm], Ebuf[:, hh, 0:ncm], esum[:, 4 + hh:5 + hh], imp[:, 0:ncm], ALU.mult, ALU.add, ["E", "esum", "imp"], ["imp"])
                          A = imp[:, 0:256].rearrange("p (j t) -> p j t", t=4)
                          B = imp[:, 4:260].rearrange("p (j t) -> p j t", t=4)
                          tt("dve", score[:], A[:, :, 1], A[:, :, 2], ALU.add, ["imp"], ["score"])
                          tt("dve", score[:], score[:], A[:, :, 3], ALU.add, ["imp", "score"], ["score"])
                          stt(score[:], score[:], 2.0, A[:, :, 0], ALU.mult, ALU.add, ["imp", "score"], ["score"])
                          tt("dve", score[:], score[:], B[:, :, 0], ALU.add, ["imp", "score"], ["score"])
                          tt("dve", score[:], score[:], gpat[:, 64 - 2 * i:128 - 2 * i], ALU.add, ["score", "gpat"], ["score"])
                          memset("dve", score[:, 0:1], 100.0, ["score"])
                          S.op("dve", lambda e: e.max(out=mx[:, 0:8], in_=score[:]), ["score"], ["mx"])
                          S.op("dve", lambda e: e.match_replace(out=score2[:], in_to_replace=mx[:, 0:8], in_values=score[:], imm_value=-1e9), ["score", "mx"], ["score2"])
                          S.op("dve", lambda e: e.max(out=mx[:, 8:16], in_=score2[:]), ["score2"], ["mx"])
                          ts("dve", mbfull[:, 64:128], score[:], mx[:, 15:16], 1.0, ALU.is_ge, ALU.subtract, ["score", "mx"], ["mbfull"])
                          tr(Pb(7)[:, 0:128], mbfull[:], ["mbfull"], ["ps7"])
                          S.op("act", lambda e, qTt=qTt, hs=hs: e.mul(out=qTt[64:128, hs, :], in_=Pb(7)[64:128, 0:128].unsqueeze(1).to_broadcast([64, 4, 128]), mul=-NEG),
                               ["ps7"], [qmk + "_%d" % g])
                      else:
                          memset("dve", qTt[64:128, hs, :], 0.0, [qmk + "_%d" % g])
                      rhs_aug = qTt[:, hs, :]
                      rhs_q = qTt[0:64, hs, :]
                      rk = [qk, qmk + "_%d" % g]
                      pti = [0]

                      def branch(obank, ktiles, nk=128):
                          n = len(ktiles)
                          for idx, (lhsT, lk, mrhs, mlhs, Vap, vk, aug) in enumerate(ktiles):
                              sb_ = 2 + (pti[0] % 2)
                              pt_i = pti[0] % 2
                              pti[0] += 1
                              mm(P(sb_)[0:nk], lhsT, rhs_aug if aug else rhs_q, True, mrhs is None, lk + rk, ["ps%d" % sb_])
                              if mrhs is not None:
                                  mm(P(sb_)[0:nk], mlhs, mrhs, False, True, ["ident", "ident4", "tri", "tri2", "cmpat"], ["ps%d" % sb_])
                              act(PT[pt_i][0:nk], P(sb_)[0:nk], AF.Exp, ["ps%d" % sb_], ["PT%d" % pt_i])
                              for hh in range(4):
                                  mm(ps[:, obank, hh * 65:(hh + 1) * 65], PT[pt_i][0:nk, hh * 128:(hh + 1) * 128], Vap, (idx == 0 and hh == 0), (idx == n - 1),
                                     ["PT%d" % pt_i] + vk, ["ps%d" % obank])

                      kt = []
                      for t in range(3):
                          if 96 * t < ncm:
                              off = 96 * t - 8 * i + 248
                              kt.append((cmpKT[0:64, g, t * 96:(t + 1) * 96], ["cmpKT%d" % tt_ for tt_ in range(T + 1)] + ["cmpKT_init"],
                                         ident4[:], cmpat[:, off:off + 96], cmpV[0:96, t, g, :], ["cmpV%d" % tt_ for tt_ in range(T + 1)] + ["cmpV_init"], False))
                      branch(4, kt, nk=96)
                      kt = []
                      for j in range(i + 1):
                          kt.append((selKT[:, g, j * 128:(j + 1) * 128], ["selKT%d" % j, "selKT_ind"],
                                     tri[:] if j == i else None, ident[:], selV[:, j, g, :], ["selV%d" % j], True))
                      branch(5, kt)
                      kt = []
                      for j in range(max(0, i - 4), i + 1):
                          m_ = tri[:] if j == i else (tri2[:] if j == i - 4 else None)
                          kt.append((winKT[0:64, g, (j % 8) * 128:(j % 8 + 1) * 128], ["winKT%d" % (j % 8)],
                                     m_, ident[:], winV[:, j % 8, g, :], ["winV%d" % (j % 8)], False))
                      branch(6, kt)
                      for br in range(3):
                          cp("dve", den[:, br, :], ps[:, 4 + br, 0:260].rearrange("p (hh d) -> p hh d", d=65)[:, :, 64], ["ps%d" % (4 + br)], ["den"])
                      ts("dve", den[:], den[:], 1e-30, None, ALU.max, None, ["den"], ["den"])
                      S.op("dve", lambda e: e.reciprocal(out=den[:], in_=den[:]), ["den"], ["den"])
                      gv = gate[:, s, 12 * g:12 * g + 12].rearrange("p (hh br) -> p br hh", br=3)
                      tt("dve", wgt[:], den[:], gv, ALU.mult, ["den", "gate%d" % s], ["wgt"])
                      for br in range(3):
                          tt("dve", ocomb[:, br], ps[:, 4 + br, 0:260].rearrange("p (hh d) -> p hh d", d=65)[:, :, 0:64],
                             wgt[:, br, :].unsqueeze(2).to_broadcast([128, 4, 64]), ALU.mult, ["ps%d" % (4 + br), "wgt"], ["ocomb%d" % br])
                      tt("pool", ocomb[:, 0], ocomb[:, 0], ocomb[:, 1], ALU.add, ["ocomb0", "ocomb1"], ["ocomb0"])
                      tt("pool", o_bf[:, 256 * g:256 * (g + 1)].rearrange("p (hh d) -> p hh d", d=64), ocomb[:, 0], ocomb[:, 2], ALU.add,
                         ["ocomb0", "ocomb2"], ["o_bf"])
                  for kc in range(8):
                      tr(Pb(7)[:, kc * 128:(kc + 1) * 128], o_bf[:, kc * 128:(kc + 1) * 128], ["o_bf"], ["ps7"])
                  acopy(oT[:, :, s * 128:(s + 1) * 128], Pb(7).rearrange("p (k t) -> p k t", t=128), ["ps7"], ["oT%d" % s])
              chk("attn")
              oW = []
              for s in range(4):
                  for j in range(2):
                      if s == 0:
                          oW.append(wload(CH_O(j)))
                      W, wk = oW[j]
                      Wv = W[:, :].rearrange("p (kc n) -> p kc n", n=512)
                      b = (2 * s + j) % 4
                      for kc in range(8):
                          mm(P(b), oT[:, kc, s * 128:(s + 1) * 128], Wv[:, kc, :], kc == 0, kc == 7, [wk, "oT%d" % s], ["ps%d" % b])
                      tt("dve", h[:, s, j * 512:(j + 1) * 512], P(b), h[:, s, j * 512:(j + 1) * 512], ALU.add, ["ps%d" % b, "h%d" % s], ["h%d" % s])
              chk("wo")
              mlp(1, 4)
              chk("mlp1")
              for s in range(4):
                  i = 4 * T + s
                  rms(h[:, s, :], ["h%d" % s], s)
                  stt(ysb, h[:, s, :], rstd[:, s:s + 1], gfbc[:], ALU.mult, ALU.mult, ["h%d" % s, "rstd%d" % s, "gfbc"], ["kvf0", "kvf1"])
                  stq(DV["y"][i * 128:(i + 1) * 128, :], ysb, "st_y", ["kvf0", "kvf1"], ["out_y"])


        def sample_phase(ss=0):
            S.barrier()
            DS = {k_: D[k_][ss * r_:(ss + 1) * r_] for k_, r_ in (("xs", 4), ("spool", 60), ("swin", 2048), ("ys", 4), ("cmp_s", 4), ("sel_s", 4), ("win_s", 2048), ("pool_s", 60))}
            hflat = h[:].rearrange("p a b -> p (a b)")
            E1 = hflat[:, 1024:2048]
            imp_s = hflat[:, 2048:3076]
            misc = hflat[:, 3076:4096]
            ptb_i = misc[:, 0:128].bitcast(I32)
            idxp = misc[:, 128:256].bitcast(I32)
            PTf = misc[:, 256:384]
            iota_f = misc[:, 384:512]
            jl4 = misc[:, 512:544].bitcast(BF16).rearrange("p (g t) -> p g t", g=4)
            cxs = big[:, 8192:12288].rearrange("p (e g t) -> p e g t", e=2, g=4)
            jb = misc[:, 576:640]
            ji = misc[:, 640:704].bitcast(I32)
            pgi = misc[:, 704:768].bitcast(I32)
            pgf_ = misc[:, 768:832]
            parf = misc[:, 832:896]
            rb = misc[:, 896:960].rearrange("p (g t) -> p g t", g=4)
            idxs_f = misc[:, 960:992].rearrange("p (g t) -> p g t", g=4)
            idxs_i = misc[:, 992:1020].bitcast(I32)
            Ef = Ebuf[:].rearrange("p a b -> p (a b)")
            score_s = Ef[:, 0:256]
            score2_s = Ef[:, 256:512]
            mx_s = Ef[:, 512:528]
            ix_s = Ef[:, 528:544].bitcast(U32)
            phys = Ef[:, 544:608].rearrange("p (g t) -> p g t", g=4)
            idxs_i = Ef[:, 608:640].bitcast(I32).rearrange("p (g t) -> p g t", g=4)
            es_s = Ef[:, 640:648]
            cmpKT_s = selKT[:, 0, :].rearrange("p (g m) -> p g m", g=4)
            selKT_s = selKT[:, 1, :].rearrange("p (g t) -> p g t", g=4)
            winKT_s = selKT[:, 2, 0:2048].rearrange("p (g t) -> p g t", g=4)
            KTnew = selKT[:, 2, 2048:3072].rearrange("p (b g t) -> p b g t", b=2, g=4)
            sv = selV[:].rearrange("p a g d -> p (a g d)")
            cmpV_s = sv[:, 0:2860].rearrange("p (t g d) -> p t g d", g=4, d=65)
            selV_s = sv[:, 2860:4940].rearrange("p (t g d) -> p t g d", g=4, d=65)
            winV_s = sv[:, 4940:5980].rearrange("p (t g d) -> p t g d", g=4, d=65)
            Vnew = sv[:, 5980:6500].rearrange("p (t g d) -> p t g d", g=4, d=65)
            ring = nT[:].rearrange("p a b -> p (a b)").bitcast(F32)
            ohs = wslot[2][:, :].bitcast(F32).rearrange("p (t q) -> p t q", q=128)
            o_acc = kvf[:, 0:1024]
            ring_i = [0]

            def ring_slot():
                i = ring_i[0] % 4
                ring_i[0] += 1
                return ring[:, i * 512:(i + 1) * 512], "ring%d" % i

            memset("dve", h[:, 0, :], 0.0, ["h0"])
            ld(h[0:4, 0, :], DS["xs"], "xld0", ["h0"])
            ld(iota_f, D["c_iota"], "c0", ["iota_f"])
            rms(h[:, 0, :], ["h0"], 0)
            us_f = kvf[:, 0:1024]
            stt(us_f, h[:, 0, :], rstd[:, 0:1], g0bc[:], ALU.mult, ALU.mult, ["h0", "rstd0", "g0bc"], ["kvf0", "kvf1"])
            uext = hflat[0:64, 1024:2048]
            for b in range(4):
                ld(hflat[16 * b:16 * b + 15, 1024:2048], DS["spool"][15 * b:15 * b + 15, :], "smp0", ["uext"])
                ld(hflat[16 * b + 15:16 * b + 16, 1024:2048], kvf[b:b + 1, 0:1024], "smp0", ["uext"], r=["kvf0", "kvf1"])
                stq(DS["pool_s"][15 * b:15 * b + 14, :], DS["spool"][15 * b + 1:15 * b + 15, :], "st_s", [], ["out_s"])
                stq(DS["pool_s"][15 * b + 14:15 * b + 15, :], kvf[b:b + 1, 0:1024], "st_s", ["kvf0", "kvf1"], ["out_s"])
            cp("dve", ubf[0][0:64, :], uext, ["uext"], ["ubf0"])
            for c in range(8):
                g = c // 2
                b_ = c // 4
                mm(ps[:, b_, (c % 4) * 128:(c % 4) * 128 + 4], ubf[0][0:64, c * 128:(c + 1) * 128], sband[0:64, g, :], True, True, ["ubf0", "sband"], ["ps%d" % b_])
            dTb = nT[:, :, 0:128]
            memset("dve", dTb, 0.0, ["nT0"])
            acopy(dTb[:, :, 0:4], ps[:, 0:2, :].rearrange("p b (c t) -> p (b c) t", t=128)[:, :, 0:4], ["ps0", "ps1"], ["nT0"])
            for g in range(4):
                b_ = 2 + g // 2
                for kc in range(2):
                    mm(ps[:, b_, (g % 2) * 256:(g % 2 + 1) * 256], dTb[:, 2 * g + kc, :], poolW[:, g, kc, :], kc == 0, kc == 1, ["nT0", "poolW"], ["ps%d" % b_])
            tt("dve", h[:, 0, :].rearrange("p (a n) -> p a n", n=512), ps[:, 2:4, :], h[:, 0, :].rearrange("p (a n) -> p a n", n=512), ALU.add,
               ["ps2", "ps3", "h0"], ["h0"])
            chk("s_l0")
            mlp(0, 1)
            chk("s_mlp0")
            S.alias(AT_KEYS, ATT_KEYS + CX_KEYS)
            norm_T(0, 0, 0)
            for j in range(3):
                W, wk = wload(CH_KV(j))
                Wv = W[:, :].rearrange("p (kc n) -> p kc n", n=512)
                for kc in range(8):
                    mm(P(j), nT[:, kc, 0:128], Wv[:, kc, :], kc == 0, kc == 7, [wk, "nT0"], ["ps%d" % j])
                cp("act", kvf[:, j * 512:(j + 1) * 512], P(j), ["ps%d" % j], ["kvf%d" % j])
            kv5 = kvf[:, :].rearrange("p (br e g d) -> p br e g d", br=3, e=2, g=4)
            cosb = coss[:, :].unsqueeze(1).unsqueeze(1).to_broadcast([128, 3, 4, 8])
            sinb = sins[:, :].unsqueeze(1).unsqueeze(1).to_broadcast([128, 3, 4, 8])
            rope(kv5[:, :, 0, :, 0:8], kv5[:, :, 0, :, 8:16], cosb, sinb, (3, 4), ["kvf0", "kvf1", "kvf2", "coss", "sins"], ["kvf0", "kvf1", "kvf2"])
            stq(DS["cmp_s"], kvf[0:4, 0:512], "st_s", ["kvf0"], ["out_s"])
            stq(DS["sel_s"], kvf[0:4, 512:1024], "st_s", ["kvf1"], ["out_s"])
            stq(DS["win_s"].rearrange("(b r) c -> b r c", b=4)[:, 511, :], kvf[0:4, 1024:1536], "st_s", ["kvf2"], ["out_s"])
            stq(DS["win_s"].rearrange("(b r) c -> b r c", b=4)[:, 0:511, :], DS["swin"].rearrange("(b r) c -> b r c", b=4)[:, 1:512, :], "st_s", [], ["out_s"])
            cp("pool", kvb[:], kvf[:], ["kvf0", "kvf1", "kvf2"], ["kvb"])
            kb5 = kvb[:, :].rearrange("p (br e g d) -> p br e g d", br=3, e=2, g=4)
            for bi, br in enumerate((1, 2)):
                for g in range(4):
                    tr(Pb(5)[0:64, (bi * 4 + g) * 128:(bi * 4 + g + 1) * 128], kb5[:, br, 0, g, :], ["kvb"], ["ps5"])
            acopy(KTnew[0:64], Pb(5)[0:64, :].rearrange("p (b g t) -> p b g t", b=2, g=4), ["ps5"], ["KTnew"])
            vnew_all = ubf[1][:, 0:512].rearrange("p (b g d) -> p b g d", b=2, g=4)
            for bi, br in enumerate((1, 2)):
                cp("act", vnew_all[:, bi], kb5[:, br, 1, :, :], ["kvb"], ["vnew_all"])
            chk("s_kv")
            norm_T(0, 0, 0)
            for j in range(3):
                W, wk = wload(CH_QG(j), width=4096 if j < 2 else 384)
                if j < 2:
                    Wv = W[:, :].rearrange("p (kc n) -> p kc n", n=512)
                    for kc in range(8):
                        mm(P(j), nT[:, kc, 0:128], Wv[:, kc, :], kc == 0, kc == 7, [wk, "nT0"], ["ps%d" % j])
                    cp("act", kvf[:, j * 512:(j + 1) * 512], P(j), ["ps%d" % j], ["kvf%d" % j])
                else:
                    Wv = W[:, 0:384].rearrange("p (kc n) -> p kc n", n=48)
                    for kc in range(8):
                        mm(P(2)[:, 0:48], nT[:, kc, 0:128], Wv[:, kc, :], kc == 0, kc == 7, [wk, "nT0"], ["ps2"])
                    act(gate[:, 0, :], P(2)[:, 0:48], AF.Exp, ["ps2"], ["gate0"], scale=-1.0)
                    ts("dve", gate[:, 0, :], gate[:, 0, :], 1.0, None, ALU.add, None, ["gate0"], ["gate0"])
                    S.op("dve", lambda e: e.reciprocal(out=gate[:, 0, :], in_=gate[:, 0, :]), ["gate0"], ["gate0"])
            q3 = kvf[:, 0:1024].rearrange("p (hh d) -> p hh d", d=64)
            cosb = coss[:, :].unsqueeze(1).unsqueeze(1).to_broadcast([128, 1, 16, 8])
            sinb = sins[:, :].unsqueeze(1).unsqueeze(1).to_broadcast([128, 1, 16, 8])
            rope(q3[:, :, 0:8].unsqueeze(1), q3[:, :, 8:16].unsqueeze(1), cosb, sinb, (1, 16), ["kvf0", "kvf1", "coss", "sins"], ["kvf0", "kvf1"])
            cp("pool", q_bf[:, 0, :], kvf[:, 0:1024], ["kvf0", "kvf1"], ["q_bf0"])
            qTt = qT[0]
            for hh in range(16):
                bnk = 4 + hh // 8
                tr(Pb(bnk)[0:64, (hh % 8) * 128:(hh % 8 + 1) * 128], q_bf[:, 0, hh * 64:(hh + 1) * 64], ["q_bf0"], ["ps%d" % bnk])
            acopy(qTt[0:64, 0:8, :], Pb(4)[0:64, :].rearrange("p (hh t) -> p hh t", t=128), ["ps4"], ["qT0"])
            cp("act", qTt[0:64, 8:16, :], Pb(5)[0:64, :].rearrange("p (hh t) -> p hh t", t=128), ["ps5"], ["qT0"])
            chk("s_qg")
            memset("dve", o_acc, 0.0, ["kvf0", "kvf1"])
            memset("dve", sv[:, 0:6500], 0.0, ["sv"])
            memset("dve", selV_s[:, :, :, 64:65], 1.0, ["sv"])
            memset("dve", winV_s[:, :, :, 64:65], 1.0, ["sv"])
            memset("dve", imp_s, 0.0, ["imp_s"])
            pti = [0]

            def attend(obank, ktiles, rhs_q, rk):
                n = len(ktiles)
                for idx, (lhsT, lk, Vap, vk, nk) in enumerate(ktiles):
                    sb_ = 2 + (pti[0] % 2)
                    pt_i = pti[0] % 2
                    pti[0] += 1
                    mm(P(sb_)[0:nk], lhsT, rhs_q, True, True, lk + rk, ["ps%d" % sb_])
                    act(PT[pt_i][0:nk], P(sb_)[0:nk], AF.Exp, ["ps%d" % sb_], ["PT%d" % pt_i])
                    for hh in range(4):
                        mm(ps[:, obank, hh * 65:(hh + 1) * 65], PT[pt_i][0:nk, hh * 128:(hh + 1) * 128], Vap, (idx == 0 and hh == 0), (idx == n - 1),
                           ["PT%d" % pt_i] + vk, ["ps%d" % obank])

            for b in range(4):
                ld(ptb_i, D["ptab"][0:1, ss * 512 + b * 128: ss * 512 + (b + 1) * 128].partition_broadcast(128), "smp1", ["ptb"])
                ts("dve", idxp, ptb_i, 128, pcol[:, 7:8], ALU.mult, ALU.add, ["ptb", "pcol"], ["idxp"])
                cp("dve", PTf, ptb_i, ["ptb"], ["PTf"])
                memset("dve", carry[:], 0.0, ["carry"])
                for Tq in range(32):
                    for s_ in range(4):
                        pg = 4 * Tq + s_
                        slot, sk = ring_slot()
                        S.dma("pool", lambda e, slot=slot, pg=pg: e.indirect_dma_start(out=slot, out_offset=None, in_=CC,
                              in_offset=bass.IndirectOffsetOnAxis(ap=idxp[:, pg:pg + 1], axis=0)), "g_" + sk, ["idxp", "CC"], [sk])
                        cp("pool", kvb[:, 0:512], slot, [sk], ["kvb"])
                        kc5 = kvb[:, 0:512].rearrange("p (e g d) -> p e g d", e=2, g=4)
                        for e_ in range(2):
                            for g in range(4):
                                tr(Pb(6)[0:64, (e_ * 4 + g) * 128:(e_ * 4 + g + 1) * 128], kc5[:, e_, g, :], ["kvb"], ["ps6"])
                        acopy(cxs[0:64, :, :, s_ * 128:(s_ + 1) * 128], Pb(6)[0:64, :].rearrange("p (e g t) -> p e g t", e=2, g=4), ["ps6"], ["cxs%d" % s_])
                    for e_ in range(2):
                        W, wk = wload(CH_C1(e_), nparts=64)
                        for p_ in range(2):
                            for r_ in range(16):
                                row = p_ * 16 + r_
                                rhs = cxs[0:64, e_, :, :].rearrange("p g (sg r) -> p g sg r", r=16)[:, :, :, r_]
                                mm(ps[:, 7, (e_ * 2 + p_) * 128:(e_ * 2 + p_ + 1) * 128].rearrange("p (g sg) -> p g sg", g=4),
                                   W[0:64, row * 128:(row + 1) * 128], rhs, r_ == 0, r_ == 15, [wk] + ["cxs%d" % q for q in range(4)], ["ps7"])
                    cp("dve", parts[:].rearrange("p e q g sg -> p (e q g sg)"), P(7), ["ps7"], ["parts"])
                    tt("dve", hid[:, :, :, 1:32], parts[:, :, 0, :, 0:31], parts[:, :, 1, :, 1:32], ALU.add, ["parts"], ["hid"])
                    tt("dve", hid[:, :, :, 0:1], carry[:].unsqueeze(3), parts[:, :, 1, :, 0:1], ALU.add, ["parts", "carry"], ["hid"])
                    cp("dve", carry[:].unsqueeze(3), parts[:, :, 0, :, 31:32], ["parts", "hid"], ["carry"])
                    for e_ in range(2):
                        ts("dve", hid[:, e_], hid[:, e_], petot[:, e_:e_ + 1], None, ALU.add, None, ["hid", "petot"], ["hid"])
                    hf_ = hid[:].rearrange("p e g s -> p (e g s)")
                    h2_ = hid2[:].rearrange("p e g s -> p (e g s)")
                    tt("dve", h2_, hf_, hf_, ALU.mult, ["hid"], ["hid2"])
                    ts("dve", h2_, h2_, 0.044715, 1.0, ALU.mult, ALU.add, ["hid2"], ["hid2"])
                    tt("dve", h2_, h2_, hf_, ALU.mult, ["hid2", "hid"], ["hid2"])
                    act(h2_, h2_, AF.Exp, ["hid2"], ["hid2"], scale=-1.5957691216)
                    ts("dve", h2_, h2_, 1.0, None, ALU.add, None, ["hid2"], ["hid2"])
                    S.op("dve", lambda e, a=h2_: e.reciprocal(out=a, in_=a), ["hid2"], ["hid2"])
                    tt("dve", hidb[:].rearrange("p e g s -> p (e g s)"), h2_, hf_, ALU.mult, ["hid2", "hid"], ["hidb"])
                    mm(ps[0:64, 7, 0:128], w2bf[:, 0, :], hidb[:, 0].rearrange("p g s -> p (g s)"), True, True, ["w2bf", "hidb"], ["ps7"])
                    cp("dve", cmpKT_s[0:64, :, 32 * Tq:32 * Tq + 32], ps[0:64, 7, 0:128].rearrange("p (g s) -> p g s", g=4), ["ps7"], ["cmpKT_s"])
                    pq = 32 * (Tq % 3)
                    for g in range(4):
                        mm(ps[pq:pq + 32, 7, 128 + g * 64:128 + (g + 1) * 64], hidb[:, 1, g, :], w2bf[:, 1, :], True, True, ["w2bf", "hidb"], ["ps7"])
                    cp("dve", cmpV_s[pq:pq + 32, Tq // 3, :, 0:64], ps[pq:pq + 32, 7, 128:384].rearrange("p (g d) -> p g d", g=4), ["ps7", "sv"], ["cmpV_s"])
                    memset("dve", cmpV_s[pq:pq + 32, Tq // 3, :, 64:65], 1.0, ["cmpV_s"])
                    if Tq == 0:
                        memset("dve", cmpV_s[0:1, 0, :, :], 0.0, ["cmpV_s"])
                chk("s_cmp%d" % b)
                for t in range(4):
                    slot, sk = ring_slot()
                    ld(slot, DS["swin"][b * 512 + t * 128: b * 512 + (t + 1) * 128, :], "g_" + sk, [sk])
                    cp("pool", kvb[:, 512:1024], slot, [sk], ["kvb2"])
                    kw = kvb[:, 512:1024].rearrange("p (e g d) -> p e g d", e=2, g=4)
                    for g in range(4):
                        tr(Pb(5)[0:64, g * 128:(g + 1) * 128], kw[:, 0, g, :], ["kvb2"], ["ps5"])
                    acopy(winKT_s[0:64, :, t * 128:(t + 1) * 128], Pb(5)[0:64, 0:512].rearrange("p (g t) -> p g t", g=4), ["ps5"], ["winKT_s"])
                    cp("act", winV_s[:, t, :, 0:64], kw[:, 1, :, :], ["kvb2", "sv"], ["winV_s"])
                for bi in range(2):
                    ts("dve", Vnew[:, bi, :, 0:64], vnew_all[:, bi], pcol[:, 3 + b:4 + b], None, ALU.mult, None, ["vnew_all", "pcol", "sv"], ["Vnew"])
                    for g in range(4):
                        cp("dve", Vnew[:, bi, g, 64:65], pcol[:, 3 + b:4 + b], ["pcol", "sv"], ["Vnew"])
                for g in range(4):
                    rhs_q = qTt[0:64, 4 * g:4 * g + 4, :]
                    rk = ["qT0"]
                    for hh in range(4):
                        for half in range(2):
                            mm(P(half), qTt[0:64, 4 * g + hh, :], cmpKT_s[0:64, g, half * 512:(half + 1) * 512], True, True, ["qT0", "cmpKT_s"], ["ps%d" % half])
                        ts("dve", ps[:, 0, 0:1], ps[:, 0, 0:1], NEG, None, ALU.add, None, ["ps0"], ["ps0"])
                        act(E1.rearrange("p (a n) -> p a n", n=512), ps[:, 0:2, :], AF.Exp, ["ps0", "ps1"], ["E1", "es_s"], accum_out=es_s[:, 0:1])
                        S.op("dve", lambda e: e.reciprocal(out=es_s[:, 1:2], in_=es_s[:, 0:1]), ["es_s"], ["es_s"])
                        if hh == 0:
                            ts("dve", imp_s[:, 0:1024], E1, es_s[:, 1:2], None, ALU.mult, None, ["E1", "es_s"], ["imp_s"])
                        else:
                            stt(imp_s[:, 0:1024], E1, es_s[:, 1:2], imp_s[:, 0:1024], ALU.mult, ALU.add, ["E1", "es_s", "imp_s"], ["imp_s"])
                    A = imp_s[:, 0:1024].rearrange("p (j t) -> p j t", t=4)
                    B = imp_s[:, 4:1028].rearrange("p (j t) -> p j t", t=4)
                    tt("dve", score_s, A[:, :, 1], A[:, :, 2], ALU.add, ["imp_s"], ["score_s"])
                    tt("dve", score_s, score_s, A[:, :, 3], ALU.add, ["imp_s", "score_s"], ["score_s"])
                    stt(score_s, score_s, 2.0, A[:, :, 0], ALU.mult, ALU.add, ["imp_s", "score_s"], ["score_s"])
                    tt("dve", score_s, score_s, B[:, :, 0], ALU.add, ["imp_s", "score_s"], ["score_s"])
                    memset("dve", score_s[:, 0:1], -1.0, ["score_s"])
                    memset("dve", score_s[:, 255:256], -1.0, ["score_s"])
                    S.op("dve", lambda e: e.max(out=mx_s[:, 0:8], in_=score_s), ["score_s"], ["mx_s"])
                    S.op("dve", lambda e: e.max_index(out=ix_s[:, 0:8], in_max=mx_s[:, 0:8], in_values=score_s), ["score_s", "mx_s"], ["ix_s"])
                    S.op("dve", lambda e: e.match_replace(out=score2_s, in_to_replace=mx_s[:, 0:8], in_values=score_s, imm_value=-1e9), ["score_s", "mx_s"], ["score2_s"])
                    S.op("dve", lambda e: e.max(out=mx_s[:, 8:16], in_=score2_s), ["score2_s"], ["mx_s"])
                    S.op("dve", lambda e: e.max_index(out=ix_s[:, 8:16], in_max=mx_s[:, 8:16], in_values=score2_s), ["score2_s", "mx_s"], ["ix_s"])
                    cp("dve", jl4[:, g, 0:13], ix_s[:, 0:13], ["ix_s"], ["jl4"])
                    memset("dve", jl4[:, g, 13:14], 0.0, ["jl4"])
                    memset("dve", jl4[:, g, 14:15], 255.0, ["jl4"])
                    memset("dve", jl4[:, g, 15:16], 0.0, ["jl4"])
                mm(ps[:, 7, 0:64], rowsel[:, b, :], jl4.rearrange("p g t -> p (g t)"), True, True, ["rowsel", "jl4"], ["ps7"])
                cp("dve", jb, ps[:, 7, 0:64], ["ps7"], ["jb"])
                cp("dve", ji, jb, ["jb"], ["ji"])
                S.op("dve", lambda e: e.tensor_scalar(out=pgi, in0=ji, scalar1=1, scalar2=None, op0=ALU.arith_shift_right), ["ji"], ["pgi"])
                cp("dve", pgf_, pgi, ["pgi"], ["pgf"])
                stt(parf, pgf_, -2.0, jb, ALU.mult, ALU.add, ["pgf", "jb"], ["parf"])
                for g in range(4):
                    tt("dve", ohs, iota_f.unsqueeze(1).to_broadcast([128, 16, 128]),
                       pgf_[:, g * 16:(g + 1) * 16].unsqueeze(2).to_broadcast([128, 16, 128]), ALU.is_equal, ["iota_f", "pgf"], ["wslot2"])
                    tt("dve", ohs, ohs, PTf.unsqueeze(1).to_broadcast([128, 16, 128]), ALU.mult, ["wslot2", "PTf"], ["wslot2"])
                    S.op("dve", lambda e, g=g: e.tensor_reduce(out=phys[:, g, :], in_=ohs, axis=mybir.AxisListType.X, op=ALU.add), ["wslot2"], ["phys"])
                ts("dve", rb.rearrange("p g t -> p (g t)"), phys.rearrange("p g t -> p (g t)"), 128.0, None, ALU.mult, None, ["phys"], ["rb"])
                stt(rb.rearrange("p g t -> p (g t)"), parf, 64.0, rb.rearrange("p g t -> p (g t)"), ALU.mult, ALU.add, ["parf", "rb"], ["rb"])
                rbp = rb.rearrange("p g (pr two) -> p g pr two", two=2)
                ts("dve", idxs_f, rbp[:, :, :, 0], pcol[:, 0:1], pcol[:, 2:3], ALU.mult, ALU.add, ["rb", "pcol"], ["idxs_f"])
                stt(idxs_f, rbp[:, :, :, 1], pcol[:, 1:2], idxs_f, ALU.mult, ALU.add, ["rb", "pcol", "idxs_f"], ["idxs_f"])
                cp("dve", idxs_i, idxs_f, ["idxs_f"], ["idxs_i"])
                chk("s_idx%d" % b)
                for g in range(4):
                    for pr in range(8):
                        slot, sk = ring_slot()
                        S.dma("pool", lambda e, slot=slot, g=g, pr=pr: e.indirect_dma_start(out=slot, out_offset=None, in_=CS,
                              in_offset=bass.IndirectOffsetOnAxis(ap=idxs_i[:, g, pr:pr + 1], axis=0)), "g_" + sk, ["idxs_i", "CS"], [sk])
                        s5 = slot.rearrange("p (e g d) -> p e g d", e=2, g=4)
                        cp("pool", kvb[:, 1024:1152].rearrange("p (e d) -> p e d", e=2), s5[:, :, g, :], [sk], ["kvb3"])
                        tr(Pb(5)[0:64, 512 + (pr % 4) * 128: 512 + (pr % 4 + 1) * 128], kvb[:, 1024:1088], ["kvb3"], ["ps5"])
                        acopy(selKT_s[0:64, g, pr * 128:(pr + 1) * 128], Pb(5)[0:64, 512 + (pr % 4) * 128: 512 + (pr % 4 + 1) * 128], ["ps5"], ["selKT_s"])
                        cp("act", selV_s[:, pr, g, 0:64], kvb[:, 1088:1152], ["kvb3", "sv"], ["selV_s"])
                    memset("dve", selV_s[64:128, 7, g, :], 0.0, ["selV_s"])
                    rhs_q = qTt[0:64, 4 * g:4 * g + 4, :]
                    rk = ["qT0"]
                    kt = [(cmpKT_s[0:64, g, 96 * t:96 * t + (96 if t < 10 else 64)], ["cmpKT_s"], cmpV_s[0:(96 if t < 10 else 64), t, g, :], ["cmpV_s", "sv"],
                           (96 if t < 10 else 64)) for t in range(11)]
                    attend(4, kt, rhs_q, rk)
                    kt = [(selKT_s[0:64, g, pr * 128:(pr + 1) * 128], ["selKT_s"], selV_s[:, pr, g, :], ["selV_s", "sv"], 128) for pr in range(8)]
                    kt.append((KTnew[0:64, 0, g, :], ["KTnew"], Vnew[:, 0, g, :], ["Vnew"], 128))
                    attend(5, kt, rhs_q, rk)
                    kt = [(winKT_s[0:64, g, t * 128:(t + 1) * 128], ["winKT_s"], winV_s[:, t, g, :], ["winV_s", "sv"], 128) for t in range(4)]
                    kt.append((KTnew[0:64, 1, g, :], ["KTnew"], Vnew[:, 1, g, :], ["Vnew"], 128))
                    attend(6, kt, rhs_q, rk)
                    for br in range(3):
                        cp("dve", den[:, br, :], ps[:, 4 + br, 0:260].rearrange("p (hh d) -> p hh d", d=65)[:, :, 64], ["ps%d" % (4 + br)], ["den"])
                    ts("dve", den[:], den[:], 1e-30, None, ALU.max, None, ["den"], ["den"])
                    S.op("dve", lambda e: e.reciprocal(out=den[:], in_=den[:]), ["den"], ["den"])
                    gv = gate[:, 0, 12 * g:12 * g + 12].rearrange("p (hh br) -> p br hh", br=3)
                    tt("dve", wgt[:], den[:], gv, ALU.mult, ["den", "gate0"], ["wgt"])
                    ts("dve", wgt[:], wgt[:], pcol[:, 3 + b:4 + b], None, ALU.mult, None, ["wgt", "pcol"], ["wgt"])
                    for br in range(3):
                        tt("dve", ocomb[:, br], ps[:, 4 + br, 0:260].rearrange("p (hh d) -> p hh d", d=65)[:, :, 0:64],
                           wgt[:, br, :].unsqueeze(2).to_broadcast([128, 4, 64]), ALU.mult, ["ps%d" % (4 + br), "wgt"], ["ocomb%d" % br])
                    tt("pool", ocomb[:, 0], ocomb[:, 0], ocomb[:, 1], ALU.add, ["ocomb0", "ocomb1"], ["ocomb0"])
                    tt("pool", ocomb[:, 0], ocomb[:, 0], ocomb[:, 2], ALU.add, ["ocomb0", "ocomb2"], ["ocomb0"])
                    oa = o_acc[:, 256 * g:256 * (g + 1)].rearrange("p (hh d) -> p hh d", d=64)
                    tt("pool", oa, oa, ocomb[:, 0], ALU.add, ["ocomb0", "kvf0", "kvf1"], ["kvf0", "kvf1"])
                chk("s_att%d" % b)
            S.alias(["cxs%d" % q for q in range(4)], ["oT0", "oT1", "oT2", "oT3"])
            cp("pool", o_bf, o_acc, ["kvf0", "kvf1"], ["o_bf"])
            for kc in range(8):
                tr(Pb(7)[:, kc * 128:(kc + 1) * 128], o_bf[:, kc * 128:(kc + 1) * 128], ["o_bf"], ["ps7"])
            acopy(oT[:, :, 0:128], Pb(7).rearrange("p (k t) -> p k t", t=128), ["ps7"], ["oT0"])
            for j in range(2):
                W, wk = wload(CH_O(j))
                Wv = W[:, :].rearrange("p (kc n) -> p kc n", n=512)
                for kc in range(8):
                    mm(P(j), oT[:, kc, 0:128], Wv[:, kc, :], kc == 0, kc == 7, [wk, "oT0"], ["ps%d" % j])
                tt("dve", h[:, 0, j * 512:(j + 1) * 512], P(j), h[:, 0, j * 512:(j + 1) * 512], ALU.add, ["ps%d" % j, "h0"], ["h0"])
            mlp(1, 1)
            rms(h[:, 0, :], ["h0"], 0)
            stt(ysb, h[:, 0, :], rstd[:, 0:1], gfbc[:], ALU.mult, ALU.mult, ["h0", "rstd0", "gfbc"], ["kvf0", "kvf1"])
            stq(DS["ys"], kvf[0:4, 0:1024], "st_s", ["kvf0", "kvf1"], ["out_s"])

        try:
            for sq_ in range(PSEQ):
                main_loop(sq_)
            if do_sample:
                for ss_ in range(PSEQ):
                    sample_phase(ss_)
        except _Stop:
            pass
        S.final_waits("pool", ["out_y", "out_cmp_p", "out_sel_p", "out_win_p", "out_pool_p", "out_s"])
        print("ops:", S.nops, {e: len(v) for e, v in S.ops.items()})
        S.emit(blk)
    return nc


_CACHE = {}


def kernel(**inputs):
    return run(inputs, gather="direct")


def run(inputs, n_tiles=NTILE, do_sample=True, stop=None, ncores=NC_USED, pool_pages=N_POOLPG, gather="allgather", trace=False):
    n = NC_USED
    consts = make_consts()
    ck = (n_tiles, do_sample, stop, pool_pages, gather)
    if ck not in _CACHE:
        _CACHE[ck] = build_program(n_tiles, do_sample, stop, pool_pages, gather)
    nc = _CACHE[ck]
    f = lambda a: np.ascontiguousarray(a)
    shared = {
        "norm_mix": f(inputs["norm_mix"]), "norm_ffn": f(inputs["norm_ffn"]), "pool_w": f(inputs["pool_w"][0]),
        "pool_scale": f(inputs["pool_scale"]).reshape(1, 1024), "w_qg": f(inputs["w_qg"][0]), "w_o": f(inputs["w_o"][0]),
        "norm_kv": f(inputs["norm_kv"]).reshape(1, 1024), "w_kv": f(inputs["w_kv"]), "cmp_pe": f(inputs["cmp_pe"]),
        "cmp_w1": f(inputs["cmp_w1"]), "cmp_w2": f(inputs["cmp_w2"]), "mlp_up": f(inputs["mlp_up"]),
        "mlp_down": f(inputs["mlp_down"]), "norm_final": f(inputs["norm_final"]).reshape(1, 1024),
    }
    if do_sample and "ccmp_list" not in inputs:
        ccmp = f(inputs["cache_cmp_kv"]).reshape(pool_pages * 128, 512)
        csel = f(inputs["cache_sel_kv"]).reshape(pool_pages * 128, 512)
        if gather == "direct":
            shared["ccmp"] = ccmp
            shared["csel"] = csel
    shared.update(consts)
    in_maps = []
    for c in range(n):
        m = dict(shared)
        ns_ = 4 * PSEQ
        m["x"] = f(inputs["x_prompt"][PSEQ * c:PSEQ * (c + 1)]).reshape(PSEQ * 4096, 1024)
        m["xs"] = f(inputs["x_sample"][ns_ * c:ns_ * (c + 1), 0, :])
        m["spool"] = f(inputs["state_pool"][0, ns_ * c:ns_ * (c + 1)]).reshape(15 * ns_, 1024)
        m["swin"] = f(inputs["state_win_kv"][ns_ * c:ns_ * (c + 1)]).reshape(512 * ns_, 512)
        m["ptab"] = f(inputs["page_table"][ns_ * c:ns_ * (c + 1)]).reshape(1, 128 * ns_).astype(np.int32)
        if do_sample and "ccmp_list" in inputs:
            m["ccmp"] = inputs["ccmp_list"][c]
            m["csel"] = inputs["csel_list"][c]
            m["ptab"] = inputs["ptab_list"][c]
        elif do_sample and gather != "direct":
            rs_ = pool_pages * 128 // 8
            m["ccmp"] = ccmp[c * rs_:(c + 1) * rs_]
            m["csel"] = csel[c * rs_:(c + 1) * rs_]
        in_maps.append(m)
    if trace:
        res = run_bass_kernel_spmd(nc, in_maps[:ncores], core_ids=list(range(ncores)), trace=True)
        print("EXEC_TIME_NS", getattr(res, "exec_time_ns", None), flush=True)
    else:
        res = run_bass_kernel_spmd(nc, in_maps[:ncores], core_ids=list(range(ncores)))
    R = list(res.results)
    while len(R) < n:
        R.append(R[0])
    cat = lambda k: np.stack([R[c][k] for c in range(n)], 0)
    y_prompt = cat("y").reshape(8, 4096, 1024)
    y_sample = cat("ys").reshape(32, 1, 1024)
    cmp_p = cat("cmp_p").reshape(8, 4096, 2, 4, 64)
    sel_p = cat("sel_p").reshape(8, 4096, 2, 4, 64)
    win_p = cat("win_p").reshape(8, 512, 2, 4, 64)
    pool_p = cat("pool_p").reshape(1, 8, 15, 1024)
    cmp_s = cat("cmp_s").reshape(32, 1, 2, 4, 64)
    sel_s = cat("sel_s").reshape(32, 1, 2, 4, 64)
    win_s = cat("win_s").reshape(32, 512, 2, 4, 64)
    pool_s = cat("pool_s").reshape(1, 32, 15, 1024)
    return (y_prompt, y_sample, cmp_p, sel_p, win_p, pool_p, cmp_s, sel_s, win_s, pool_s)
```

```python
import numpy as np
import ml_dtypes
from contextlib import ExitStack
import concourse.bass as bass
import concourse.mybir as mybir
from concourse.bass_utils import run_bass_kernel_spmd

F32 = mybir.dt.float32
BF16 = mybir.dt.bfloat16
I32 = mybir.dt.int32
U32 = mybir.dt.uint32
AF = mybir.ActivationFunctionType
ALU = mybir.AluOpType

NEG = -30000.0
T_SEQ = 4096
NSUB = 32
NTILE = 8
PAST = 16384
N_POOLPG = 5120
NC_USED = 8
PSEQ = 8 // NC_USED


class Sched:
    ENGS = ("pe", "dve", "act", "pool", "sp")

    def __init__(self, nc, stack, n_dma_sems=100):
        self.nc = nc
        self.sem = {e: stack.enter_context(nc.semaphore("s_" + e)) for e in self.ENGS}
        self.cnt = {e: 0 for e in self.ENGS}
        self.stack = stack
        self.dma_sem = {}
        self.dma_cnt = {}
        self.ops = {e: [] for e in self.ENGS}
        self.waited = {}
        self.last_w = {}
        self.readers = {}
        self.nops = 0

    def _dma_sem(self, key):
        if key not in self.dma_sem:
            self.dma_sem[key] = self.stack.enter_context(self.nc.semaphore("d%d" % len(self.dma_sem)))
            self.dma_cnt[key] = 0
        return self.dma_sem[key]

    def _deps(self, e, reads, writes):
        deps = {}

        def add(s, n):
            if deps.get(s, 0) < n:
                deps[s] = n
        for r in reads:
            d = self.last_w.get(r)
            if d is not None:
                add(*d)
        for w in writes:
            d = self.last_w.get(w)
            if d is not None:
                add(*d)
            for s, n in self.readers.get(w, {}).items():
                add(s, n)
        waits = []
        for s, n in deps.items():
            if e == "pe" and s == ("e", "pe"):
                continue
            k = (e, s)
            if self.waited.get(k, 0) < n:
                self.waited[k] = n
                waits.append((s, n))
        return waits

    def _semobj(self, s):
        return self.sem[s[1]] if s[0] == "e" else self.dma_sem[s[1]]

    def _record(self, me, reads, writes):
        s, n = me
        for r in reads:
            d = self.readers.setdefault(r, {})
            if d.get(s, 0) < n:
                d[s] = n
        for w in writes:
            self.last_w[w] = me
            self.readers[w] = {}

    def op(self, e, fn, reads=(), writes=()):
        waits = self._deps(e, reads, writes)
        self.cnt[e] += 1
        self._record((("e", e), self.cnt[e]), reads, writes)
        self.ops[e].append((waits, fn, self.sem[e], 1))
        self.nops += 1

    def dma(self, e, fn, key, reads=(), writes=()):
        sem = self._dma_sem(key)
        waits = self._deps(e, reads, writes)
        self.dma_cnt[key] += 16
        self._record((("d", key), self.dma_cnt[key]), reads, writes)
        self.ops[e].append((waits, fn, sem, 16))
        self.nops += 1

    def alias(self, old, new):
        deps = {}
        for k in old:
            d = self.last_w.get(k)
            if d is not None and deps.get(d[0], 0) < d[1]:
                deps[d[0]] = d[1]
            for s, n in self.readers.get(k, {}).items():
                if deps.get(s, 0) < n:
                    deps[s] = n
        for k in new:
            r = self.readers.setdefault(k, {})
            for s, n in deps.items():
                if r.get(s, 0) < n:
                    r[s] = n

    def barrier(self):
        targets = [(("e", e2), self.cnt[e2]) for e2 in self.ENGS if self.cnt[e2] > 0]
        targets += [(("d", k), n) for k, n in self.dma_cnt.items() if n > 0]
        for e in self.ENGS:
            waits = []
            for s_, n in targets:
                if e == "pe" and s_ == ("e", "pe"):
                    continue
                k = (e, s_)
                if self.waited.get(k, 0) < n:
                    self.waited[k] = n
                    waits.append((s_, n))
            self.ops[e].append((waits, None, None, 0))

    def final_waits(self, e, keys):
        waits = self._deps(e, keys, keys)
        self.ops[e].append((waits, None, None, 0))

    def emit(self, block):
        sched = self

        def mk(e):
            def body(engine):
                for waits, fn, sem, inc in sched.ops[e]:
                    for s, n in waits:
                        engine.wait_ge(sched._semobj(s), n)
                    if fn is not None:
                        fn(engine).then_inc(sem, inc)
            return body
        block.tensor(mk("pe"))
        block.vector(mk("dve"))
        block.scalar(mk("act"))
        block.gpsimd(mk("pool"))
        block.sync(mk("sp"))


POOL_WINDOWS = (2, 4, 8, 16)


def make_consts():
    bf = ml_dtypes.bfloat16
    c = {}
    c["c_ident"] = np.eye(128, dtype=np.float32).astype(bf)
    c["c_ident4"] = np.tile(np.eye(128, dtype=np.float32), (1, 4)).astype(bf)
    band = np.zeros((128, 12, 128), np.float32)
    i = np.arange(128)[:, None]
    j = np.arange(128)[None, :]
    for g, w in enumerate(POOL_WINDOWS):
        band[:, g * 3 + 0, :] = ((i <= j) & (i > j - w)) / w - (i == j)
        band[:, g * 3 + 1, :] = ((i - 128) > (j - w)) / w
        band[:, g * 3 + 2, :] = ((i <= j) & (i > j - w)) / np.minimum(j + 1, w) - (i == j)
    c["c_band"] = band.reshape(128, 12 * 128).astype(bf)
    tri = np.where(i <= j, 0.0, NEG).astype(np.float32)
    tri2 = np.where(i >= j, 0.0, NEG).astype(np.float32)
    c["c_tri"] = np.tile(tri, (1, 4)).astype(bf)
    c["c_tri2"] = np.tile(tri2, (1, 4)).astype(bf)
    key = np.arange(4096)[None, :]
    b = np.arange(64)[:, None]
    ind = np.zeros((128, 4096), np.float32)
    ind[64:128] = (key // 64 == b)
    c["c_ind"] = ind.astype(bf)
    ql = np.arange(128)[:, None]
    jj = np.arange(8)[None, :]
    c["c_cmq"] = np.where(ql >= 16 * jj + 15, 0.0, NEG).astype(np.float32)
    mpp = np.arange(504)[None, :] - 248
    c["c_cmpat"] = np.where(16 * mpp + 15 <= ql, 0.0, NEG).astype(np.float32).astype(bf)
    jp = np.arange(128)[None, :] - 64
    cc = (ql >= 64).astype(np.int64)
    G = np.where((jp == cc) | (jp == cc - 1), 100.0, np.where(jp > cc, -1000.0, 0.0))
    c["c_g"] = G.astype(np.float32)
    inv = 500000.0 ** (-np.arange(0, 16, 2, dtype=np.float32) / 16)
    pos = (np.arange(32)[None, :] * 128 + np.arange(128)[:, None]).astype(np.float32)
    ang = pos[:, :, None] * inv[None, None, :]
    c["c_cos"] = np.cos(ang).astype(np.float32).reshape(128, 256)
    c["c_sin"] = np.sin(ang).astype(np.float32).reshape(128, 256)
    angs = np.float32(PAST) * inv
    c["c_coss"] = np.tile(np.cos(angs).astype(np.float32)[None, :], (128, 1))
    c["c_sins"] = np.tile(np.sin(angs).astype(np.float32)[None, :], (128, 1))
    sband = np.zeros((128, 4, 4), np.float32)
    for b_ in range(4):
        for r_ in range(16):
            for g, w in enumerate(POOL_WINDOWS):
                sband[16 * b_ + r_, g, b_] = (1.0 / w if r_ >= 16 - w else 0.0) - (1.0 if r_ == 15 else 0.0)
    c["c_sband"] = sband.reshape(128, 16).astype(bf)
    rowsel = np.zeros((128, 4, 128), np.float32)
    for b_ in range(4):
        rowsel[b_, b_, :] = 1.0
    c["c_rowsel"] = rowsel.reshape(128, 512).astype(bf)
    pcol = np.zeros((128, 8), np.float32)
    pp = np.arange(128)
    pcol[:, 0] = pp < 64
    pcol[:, 1] = pp >= 64
    pcol[:, 2] = pp % 64
    for b_ in range(4):
        pcol[:, 3 + b_] = pp == b_
    pcol[:, 7] = pp
    c["c_pcol"] = pcol
    c["c_iota"] = np.tile(np.arange(128, dtype=np.float32)[None, :], (128, 1))
    return c


CONST_SPECS = {
    "c_sband": ([128, 16], BF16), "c_rowsel": ([128, 512], BF16), "c_pcol": ([128, 8], F32), "c_iota": ([128, 128], F32),
    "c_ident": ([128, 128], BF16), "c_ident4": ([128, 512], BF16), "c_band": ([128, 1536], BF16),
    "c_tri": ([128, 512], BF16), "c_tri2": ([128, 512], BF16), "c_ind": ([128, 4096], BF16),
    "c_cmq": ([128, 8], F32), "c_cmpat": ([128, 504], BF16), "c_g": ([128, 128], F32),
    "c_cos": ([128, 256], F32), "c_sin": ([128, 256], F32), "c_coss": ([128, 8], F32), "c_sins": ([128, 8], F32),
}

IN_SPECS = {
    "x": ([4096 * PSEQ, 1024], F32), "xs": ([4 * PSEQ, 1024], F32), "spool": ([60 * PSEQ, 1024], F32),
    "ccmp": ([N_POOLPG * 128, 512], F32), "csel": ([N_POOLPG * 128, 512], F32),
    "swin": ([2048 * PSEQ, 512], F32), "ptab": ([1, 512 * PSEQ], I32),
    "norm_mix": ([2, 1024], F32), "norm_ffn": ([2, 1024], F32), "pool_w": ([4, 256, 256], F32),
    "pool_scale": ([1, 1024], F32), "w_qg": ([1024, 1072], F32), "w_o": ([1024, 1024], F32),
    "norm_kv": ([1, 1024], F32), "w_kv": ([1024, 1536], F32), "cmp_pe": ([32, 2, 64], F32),
    "cmp_w1": ([2, 32, 64, 128], F32), "cmp_w2": ([2, 128, 64], F32),
    "mlp_up": ([2, 1024, 4096], F32), "mlp_down": ([2, 4096, 1024], F32), "norm_final": ([1, 1024], F32),
}
OUT_SPECS = {
    "y": [4096 * PSEQ, 1024], "ys": [4 * PSEQ, 1024], "cmp_p": [4096 * PSEQ, 512], "sel_p": [4096 * PSEQ, 512], "win_p": [512 * PSEQ, 512],
    "pool_p": [15 * PSEQ, 1024], "cmp_s": [4 * PSEQ, 512], "sel_s": [4 * PSEQ, 512], "win_s": [2048 * PSEQ, 512], "pool_s": [60 * PSEQ, 1024],
}

CH_UP = lambda l, fg: l * 8 + fg
CH_DN = lambda l, fg: 16 + l * 8 + fg
CH_KV = lambda j: 32 + j
CH_QG = lambda j: 35 + j
CH_O = lambda j: 38 + j
CH_C1 = lambda e: 40 + e
N_CH = 42


def build_program(n_tiles=NTILE, do_sample=True, stop=None, pool_pages=N_POOLPG, gather="allgather"):
    nc = bass.Bass("TRN2", target_bir_lowering=False)
    D = {}
    for k, (shp, dt) in IN_SPECS.items():
        if k in ("ccmp", "csel"):
            if not do_sample:
                continue
            if gather == "allgather":
                shp = [pool_pages * 128 // 8, 512]
            else:
                shp = [pool_pages * 128, 512]
        D[k] = nc.dram_tensor(k, shp, dt, kind="ExternalInput").ap()
    for k, (shp, dt) in CONST_SPECS.items():
        D[k] = nc.dram_tensor(k, shp, dt, kind="ExternalInput").ap()
    for k, shp in OUT_SPECS.items():
        D[k] = nc.dram_tensor(k, shp, F32, kind="ExternalOutput").ap()
    wscr = nc.dram_tensor("wscr", [N_CH, 128, 4096], BF16, kind="Internal").ap()
    if do_sample and gather == "allgather":
        cc_in = nc.dram_tensor("cc_in", [pool_pages * 128 // 8, 512], F32, kind="Internal").ap()
        cs_in = nc.dram_tensor("cs_in", [pool_pages * 128 // 8, 512], F32, kind="Internal").ap()
        CC = nc.dram_tensor("cc_full", [pool_pages * 128, 512], F32, kind="Internal").ap()
        CS = nc.dram_tensor("cs_full", [pool_pages * 128, 512], F32, kind="Internal").ap()
    elif do_sample:
        CC, CS = D["ccmp"], D["csel"]

    with ExitStack() as st:
        S = Sched(nc, st)
        total = [0]

        def sb(name, shape, dt):
            n = 1
            for s_ in shape[1:]:
                n *= s_
            total[0] += n * (2 if dt == BF16 else 4)
            return st.enter_context(nc.sbuf_tensor(name, shape, dt))

        selKT = sb("selKT", [128, 4, 4096], BF16)
        selV = sb("selV", [128, 32, 4, 65], BF16)
        winKT = sb("winKT", [128, 4, 1024], BF16)
        winV = sb("winV", [128, 8, 4, 65], BF16)
        cmpKT = sb("cmpKT", [128, 4, 288], BF16)
        cmpV = sb("cmpV", [128, 3, 4, 65], BF16)
        ident = sb("ident", [128, 128], BF16)
        ident4 = sb("ident4", [128, 512], BF16)
        band = sb("band", [128, 12, 128], BF16)
        tri = sb("tri", [128, 512], BF16)
        tri2 = sb("tri2", [128, 512], BF16)
        cmq = sb("cmq", [128, 8], F32)
        cmpat = sb("cmpat", [128, 504], BF16)
        gpat = sb("gpat", [128, 128], F32)
        cos_t = sb("cos_t", [128, 32, 8], F32)
        sin_t = sb("sin_t", [128, 32, 8], F32)
        g0bc = sb("g0bc", [128, 1024], F32)
        gfbc = sb("gfbc", [128, 1024], F32)
        poolW = sb("poolW", [128, 4, 2, 256], BF16)
        w2bf = sb("w2bf", [128, 2, 64], BF16)
        petot = sb("petot", [128, 2], F32)
        gcol = sb("gcol", [128, 5, 8], F32)
        wslot = [sb("wslot%d" % i, [128, 4096], BF16) for i in range(3)]
        h = sb("h", [128, 4, 1024], F32)
        nbf = [sb("nbf%d" % i, [128, 1024], BF16) for i in range(2)]
        ubf = [sb("ubf%d" % i, [128, 1024], BF16) for i in range(2)]
        nT = sb("nT", [128, 8, 512], BF16)
        big = sb("big", [128, 16384], BF16)
        junk = sb("junk", [128, 1024], BF16)
        ssq = sb("ssq", [128, 8], F32)
        rstd = sb("rstd", [128, 8], F32)
        relu_t = [sb("relu%d" % i, [128, 512], F32) for i in range(2)]
        kvf = sb("kvf", [128, 1536], F32)
        kvb = sb("kvb", [128, 1536], BF16)
        rt = sb("rt", [128, 4, 16, 8], F32)
        parts = sb("parts", [128, 2, 2, 4, 32], F32)
        carry = sb("carry", [128, 2, 4], F32)
        hid = sb("hid", [128, 2, 4, 32], F32)
        hid2 = sb("hid2", [128, 2, 4, 32], F32)
        hidb = sb("hidb", [128, 2, 4, 32], BF16)
        Ebuf = sb("Ebuf", [128, 4, 264], F32)
        esum = sb("esum", [128, 8], F32)
        imp = sb("imp", [128, 264], F32)
        score = sb("score", [128, 64], F32)
        score2 = sb("score2", [128, 64], F32)
        mx = sb("mx", [128, 16], F32)
        mbfull = sb("mbfull", [128, 128], BF16)
        gate = sb("gate", [128, 4, 48], F32)
        den = sb("den", [128, 3, 4], F32)
        wgt = sb("wgt", [128, 3, 4], F32)
        ocomb = sb("ocomb", [128, 3, 4, 64], F32)
        rowsel = sb("rowsel", [128, 4, 128], BF16)
        pcol = sb("pcol", [128, 8], F32)
        sband = sb("sband", [128, 4, 4], BF16)
        coss = sb("coss", [128, 8], F32)
        sins = sb("sins", [128, 8], F32)
        ps = st.enter_context(nc.psum_tensor("ps", [128, 8, 512], F32))

        aT = big[:, :].rearrange("p (f t) -> p f t", t=512)
        cmpXT = big[:, 0:4096].rearrange("p (e g t) -> p e g t", e=2, g=4)
        CX_KEYS = ["cmpXT%d" % s_ for s_ in range(4)]
        ysb = kvf[:, 0:1024]
        q_bf = big[:, 0:4096].rearrange("p (s c) -> p s c", c=1024)
        qT = [big[:, 4096 + i * 2048: 4096 + (i + 1) * 2048].rearrange("p (hh t) -> p hh t", t=128) for i in range(2)]
        oT = big[:, 8192:12288].rearrange("p (k t) -> p k t", t=512)
        o_bf = big[:, 12288:13312]
        PT = [big[:, 13312 + i * 512: 13312 + (i + 1) * 512] for i in range(2)]
        stage = [big[:, i * 8192:(i + 1) * 8192].bitcast(F32) for i in range(2)]
        AT_KEYS = ["aT%d" % f for f in range(32)]
        ATT_KEYS = ["q_bf%d" % s for s in range(4)] + ["qT0", "qT1", "qTm0", "qTm1", "o_bf", "PT0", "PT1"] + ["oT%d" % s for s in range(4)]
        STG_KEYS = ["stage0", "stage1"]
        print("SBUF bytes/partition:", total[0])

        blk = st.enter_context(nc.Block())

        def P(b):
            return ps[:, b, :]

        def Pb(b):
            return ps[:, b, :].bitcast(BF16)

        def act(out, in_, func, r, w, **kw):
            S.op("act", lambda e: e.activation(out=out, in_=in_, func=func, **kw), r, w)

        def acopy(out, in_, r, w):
            S.op("act", lambda e: e.copy(out=out, in_=in_), r, w)

        def cp(eng, out, in_, r, w):
            if eng == "act":
                return acopy(out, in_, r, w)
            S.op(eng, lambda e: e.tensor_copy(out=out, in_=in_), r, w)

        def tt(eng, out, in0, in1, op, r, w):
            S.op(eng, lambda e: e.tensor_tensor(out=out, in0=in0, in1=in1, op=op), r, w)

        def ts(eng, out, in0, s1, s2, op0, op1, r, w):
            if op1 is None:
                S.op(eng, lambda e: e.tensor_scalar(out=out, in0=in0, scalar1=s1, scalar2=None, op0=op0), r, w)
            else:
                S.op(eng, lambda e: e.tensor_scalar(out=out, in0=in0, scalar1=s1, scalar2=s2, op0=op0, op1=op1), r, w)

        def stt(out, in0, scalar, in1, op0, op1, r, w):
            S.op("dve", lambda e: e.scalar_tensor_tensor(out=out, in0=in0, scalar=scalar, in1=in1, op0=op0, op1=op1), r, w)

        def mm(out, lhsT, rhs, start, stop, r, w):
            S.op("pe", lambda e: e.matmul(out, lhsT=lhsT, rhs=rhs, start=start, stop=stop, skip_group_check=True), r, w)

        def tr(out, in_, r, w, idn=None):
            idn_ = ident[:] if idn is None else idn
            S.op("pe", lambda e: e.transpose(out=out, in_=in_, identity=idn_), list(r) + ["ident"], w)

        def memset(eng, ap, val, w):
            S.op(eng, lambda e: e.memset(ap, val), (), w)

        def ld(out, in_, key, w, r=(), q="sp", **kw):
            S.dma(q, lambda e: e.dma_start(out=out, in_=in_, **kw), key, r, w)

        def stq(out, in_, key, r, w, **kw):
            S.dma("pool", lambda e: e.dma_start(out=out, in_=in_, **kw), key, r, w)

        ld(ident[:], D["c_ident"], "c0", ["ident"])
        ld(ident4[:], D["c_ident4"], "c0", ["ident4"])
        ld(band[:].rearrange("p a b -> p (a b)"), D["c_band"], "c0", ["band"])
        ld(tri[:], D["c_tri"], "c0", ["tri"])
        ld(tri2[:], D["c_tri2"], "c0", ["tri2"])
        ld(cmq[:], D["c_cmq"], "c0", ["cmq"])
        ld(cmpat[:], D["c_cmpat"], "c0", ["cmpat"])
        ld(gpat[:], D["c_g"], "c0", ["gpat"])
        ld(cos_t[:].rearrange("p a b -> p (a b)"), D["c_cos"], "c0", ["cos"])
        ld(sin_t[:].rearrange("p a b -> p (a b)"), D["c_sin"], "c0", ["sin"])
        ld(rowsel[:].rearrange("p a b -> p (a b)"), D["c_rowsel"], "c0", ["rowsel"])
        ld(pcol[:], D["c_pcol"], "c0", ["pcol"])
        ld(sband[:].rearrange("p a b -> p (a b)"), D["c_sband"], "c0", ["sband"])
        ld(coss[:], D["c_coss"], "c0", ["coss"])
        ld(sins[:], D["c_sins"], "c0", ["sins"])
        if do_sample and gather == "allgather":
            rows_sh = pool_pages * 128 // 8
            nbig = rows_sh * 512 // 16384
            ld(cc_in.rearrange("(a b) c -> a (b c)", a=nbig), D["ccmp"].rearrange("(a b) c -> a (b c)", a=nbig), "ag_cp0", ["cc_in"])
            ld(cs_in.rearrange("(a b) c -> a (b c)", a=nbig), D["csel"].rearrange("(a b) c -> a (b c)", a=nbig), "ag_cp1", ["cs_in"])
            S.dma("pool", lambda e: e.collective_compute("AllGather", ALU.bypass, [list(range(8))], ins=[cc_in], outs=[CC]), "ag0", ["cc_in"], ["CC"])
            S.dma("pool", lambda e: e.collective_compute("AllGather", ALU.bypass, [list(range(8))], ins=[cs_in], outs=[CS]), "ag1", ["cs_in"], ["CS"])
        ld(g0bc[:], D["norm_mix"][0:1, :].partition_broadcast(128), "c0", ["g0bc"])
        ld(gfbc[:], D["norm_final"].partition_broadcast(128), "c0", ["gfbc"])
        for a in range(4):
            ld(selKT[64:128, a, :], D["c_ind"][64:128, :], "c0", ["selKT_ind"])
        ld(gcol[:, 0, :], D["norm_ffn"][0, :].rearrange("(kc p) -> p kc", p=128), "c0", ["gcol"], allow_slow_non_contiguous=True)
        ld(gcol[:, 1, :], D["norm_ffn"][1, :].rearrange("(kc p) -> p kc", p=128), "c0", ["gcol"], allow_slow_non_contiguous=True)
        ld(gcol[:, 2, :], D["norm_kv"][0, :].rearrange("(kc p) -> p kc", p=128), "c0", ["gcol"], allow_slow_non_contiguous=True)
        ld(gcol[:, 3, :], D["norm_mix"][1, :].rearrange("(kc p) -> p kc", p=128), "c0", ["gcol"], allow_slow_non_contiguous=True)
        ts("dve", gcol[:, 4, :], gcol[:, 3, :], 0.125, None, ALU.mult, None, ["gcol"], ["gcol"])
        memset("dve", selV[:, :, :, 64:65], 1.0, ["selV_ones"])
        memset("dve", winV[:, :, :, 64:65], 1.0, ["winV_ones"])
        memset("dve", cmpV[:], 0.0, ["cmpV_init"])
        memset("dve", cmpKT[:], 0.0, ["cmpKT_init"])
        memset("dve", imp[:], 0.0, ["imp"])
        memset("dve", Ebuf[:], 0.0, ["E"])
        memset("dve", carry[:], 0.0, ["carry"])
        memset("dve", mbfull[:], 0.0, ["mbfull"])

        conv_i = [0]

        def convert(chunk, pairs, scale_cols=None, nparts=128, width=4096, inner=512):
            i = conv_i[0] % 2
            conv_i[0] += 1
            sk, wk = "stage%d" % i, "wslot%d" % i
            stg = stage[i]
            for (dst, src) in pairs:
                ld(dst(stg), src, "stg%d" % i, [sk])
            eng = "dve" if (conv_i[0] % 2) else "pool"
            if scale_cols is None:
                cp(eng, wslot[i][0:nparts, 0:width], stg[0:nparts, 0:width], [sk], [wk])
            else:
                nk = width // inner
                for kc in range(nk):
                    ts(eng, wslot[i][0:nparts, kc * inner:(kc + 1) * inner], stg[0:nparts, kc * inner:(kc + 1) * inner],
                       gcol[:, scale_cols, kc:kc + 1], None, ALU.mult, None, [sk, "gcol"], [wk])
            stq(wscr[chunk, 0:nparts, 0:width], wslot[i][0:nparts, 0:width], "wst%d" % i, [wk], ["scr%d" % chunk])

        for l in range(2):
            for fg in range(8):
                convert(CH_UP(l, fg), [(lambda s_: s_[:, :].rearrange("p (kc n) -> p kc n", n=512),
                                        D["mlp_up"][l, :, fg * 512:(fg + 1) * 512].rearrange("(kc p) n -> p kc n", p=128))], scale_cols=l)
            for fg in range(8):
                convert(CH_DN(l, fg), [(lambda s_: s_[:, :].rearrange("p (fc n) -> p fc n", n=1024),
                                        D["mlp_down"][l, fg * 512:(fg + 1) * 512, :].rearrange("(fc p) n -> p fc n", p=128))])
        for j in range(3):
            convert(CH_KV(j), [(lambda s_: s_[:, :].rearrange("p (kc n) -> p kc n", n=512),
                                D["w_kv"][:, j * 512:(j + 1) * 512].rearrange("(kc p) n -> p kc n", p=128))], scale_cols=2)
        for j in range(2):
            convert(CH_QG(j), [(lambda s_: s_[:, :].rearrange("p (kc n) -> p kc n", n=512),
                                D["w_qg"][:, j * 512:(j + 1) * 512].rearrange("(kc p) n -> p kc n", p=128))], scale_cols=4)
        convert(CH_QG(2), [(lambda s_: s_[:, 0:384].rearrange("p (kc n) -> p kc n", n=48),
                            D["w_qg"][:, 1024:1072].rearrange("(kc p) n -> p kc n", p=128))], scale_cols=3, width=384, inner=48)
        for j in range(2):
            convert(CH_O(j), [(lambda s_: s_[:, :].rearrange("p (kc n) -> p kc n", n=512),
                               D["w_o"][:, j * 512:(j + 1) * 512].rearrange("(kc p) n -> p kc n", p=128))])
        for e_ in range(2):
            convert(CH_C1(e_), [(lambda s_: s_[0:64, :].rearrange("p (r k) -> p r k", k=128),
                                 D["cmp_w1"][e_].rearrange("r h k -> h r k"))], nparts=64)
        i_ = conv_i[0] % 2
        conv_i[0] += 1
        stg = stage[i_]
        ld(stg[:, 0:2048].rearrange("p (g kc d) -> p g kc d", g=4, kc=2), D["pool_w"].rearrange("g (kc p) d -> p g kc d", p=128), "stg%d" % i_, ["stage%d" % i_])
        ld(stg[:, 2048:3072], D["pool_scale"].partition_broadcast(128), "stg%d" % i_, ["stage%d" % i_])
        for g in range(4):
            for kc in range(2):
                tt("dve", poolW[:, g, kc, :], stg[:, (g * 2 + kc) * 256:(g * 2 + kc + 1) * 256], stg[:, 2048 + g * 256: 2048 + (g + 1) * 256],
                   ALU.mult, ["stage%d" % i_], ["poolW"])
        i_ = conv_i[0] % 2
        conv_i[0] += 1
        stg = stage[i_]
        ld(stg[:, 0:128].rearrange("p (e h) -> p e h", e=2), D["cmp_w2"].rearrange("e k h -> k e h"), "stg%d" % i_, ["stage%d" % i_])
        cp("dve", w2bf[:].rearrange("p e h -> p (e h)"), stg[:, 0:128], ["stage%d" % i_], ["w2bf"])
        wctr = [0]

        def wload(chunk, nparts=128, width=4096):
            i = wctr[0] % 3
            wctr[0] += 1
            ld(wslot[i][0:nparts, 0:width], wscr[chunk, 0:nparts, 0:width], "wl%d" % i, ["wslot%d" % i], r=["scr%d" % chunk])
            return wslot[i], "wslot%d" % i

        pe_b = sb("pe_b", [128, 32, 2], BF16)
        pe_f = sb("pe_f", [128, 32, 2], F32)
        ld(pe_f[0:64, :, :], D["cmp_pe"].rearrange("r e h -> h r e"), "c0", ["pe_f"], allow_slow_non_contiguous=True)
        cp("dve", pe_b[0:64].rearrange("p r e -> p (r e)"), pe_f[0:64].rearrange("p r e -> p (r e)"), ["pe_f"], ["pe_b"])
        for e_ in range(2):
            W, wk = wload(CH_C1(e_), nparts=64)
            for row in range(32):
                mm(ps[:, 7, e_:e_ + 1], W[0:64, row * 128:(row + 1) * 128], pe_b[0:64, row, e_:e_ + 1], row == 0, row == 31,
                   [wk, "pe_b"], ["ps7"])
        cp("dve", petot[:], ps[:, 7, 0:2], ["ps7"], ["petot"])

        S.alias(STG_KEYS, AT_KEYS + ATT_KEYS)

        def rms(src, src_keys, col):
            act(junk[:], src, AF.Square, src_keys, ["junk", "ssq%d" % col], accum_out=ssq[:, col:col + 1])
            act(rstd[:, col:col + 1], ssq[:, col:col + 1], AF.Sqrt, ["ssq%d" % col], ["rstd%d" % col], scale=1.0 / 1024, bias=1e-6)
            S.op("dve", lambda e: e.reciprocal(out=rstd[:, col:col + 1], in_=rstd[:, col:col + 1]), ["rstd%d" % col], ["rstd%d" % col])

        def norm_T(s, col, nb, ncols_valid=128):
            rms(h[:, s, :], ["h%d" % s], col)
            ts("dve", nbf[nb][:], h[:, s, :], rstd[:, col:col + 1], None, ALU.mult, None, ["h%d" % s, "rstd%d" % col], ["nbf%d" % nb])
            for kc in range(8):
                tr(Pb(4)[:, kc * 128:(kc + 1) * 128], nbf[nb][:, kc * 128:(kc + 1) * 128], ["nbf%d" % nb], ["ps4"])
            acopy(nT[:, :, s * 128:(s + 1) * 128], Pb(4).rearrange("p (k t) -> p k t", t=128), ["ps4"], ["nT%d" % s])

        def mlp(l, nsub):
            S.alias(ATT_KEYS, AT_KEYS)
            ntok = nsub * 128
            for s in range(nsub):
                norm_T(s, s, s % 2)
            for fg in range(8):
                W, wk = wload(CH_UP(l, fg))
                Wv = W[:, :].rearrange("p (kc n) -> p kc n", n=512)
                for fc in range(4):
                    f = fg * 4 + fc
                    b = f % 4
                    for kc in range(8):
                        mm(P(b)[:, 0:ntok], Wv[:, kc, fc * 128:(fc + 1) * 128], nT[:, kc, 0:ntok], kc == 0, kc == 7,
                           [wk] + ["nT%d" % s for s in range(nsub)], ["ps%d" % b])
                    rl = relu_t[f % 2]
                    act(rl[:, 0:ntok], P(b)[:, 0:ntok], AF.Relu, ["ps%d" % b], ["relu%d" % (f % 2)])
                    tt("pool", aT[:, f, 0:ntok], rl[:, 0:ntok], rl[:, 0:ntok], ALU.mult, ["relu%d" % (f % 2)], ["aT%d" % f])
            for fg in range(8):
                W, wk = wload(CH_DN(l, fg))
                Wv = W[:, :].rearrange("p (fc n) -> p fc n", n=1024)
                for fc in range(4):
                    f = fg * 4 + fc
                    for s in range(nsub):
                        for hf in range(2):
                            b = 2 * s + hf
                            mm(P(b), aT[:, f, s * 128:(s + 1) * 128], Wv[:, fc, hf * 512:(hf + 1) * 512], f == 0, f == 31,
                               [wk, "aT%d" % f], ["ps%d" % b])
            for s in range(nsub):
                tt("dve", h[:, s, :].rearrange("p (a n) -> p a n", n=512), ps[:, 2 * s:2 * s + 2, :], h[:, s, :].rearrange("p (a n) -> p a n", n=512),
                   ALU.add, ["ps%d" % (2 * s), "ps%d" % (2 * s + 1), "h%d" % s], ["h%d" % s])

        def rope(x1, x2, cosb, sinb, shape, keys_r, keys_w):
            a, b_ = shape
            t1 = rt[:, 0, 0:a * b_, :].rearrange("p (a b) d -> p a b d", a=a)
            t2 = rt[:, 1, 0:a * b_, :].rearrange("p (a b) d -> p a b d", a=a)
            t3 = rt[:, 2, 0:a * b_, :].rearrange("p (a b) d -> p a b d", a=a)
            t4 = rt[:, 3, 0:a * b_, :].rearrange("p (a b) d -> p a b d", a=a)
            tt("dve", t1, x1, cosb, ALU.mult, keys_r, ["rt0"])
            tt("dve", t2, x2, sinb, ALU.mult, keys_r, ["rt1"])
            tt("dve", t3, x2, cosb, ALU.mult, keys_r, ["rt2"])
            tt("dve", t4, x1, sinb, ALU.mult, keys_r, ["rt3"])
            tt("dve", x1, t1, t2, ALU.subtract, ["rt0", "rt1"], keys_w)
            tt("dve", x2, t3, t4, ALU.add, ["rt2", "rt3"], keys_w)

        class _Stop(Exception):
            pass

        def chk(name):
            if stop == name:
                raise _Stop()

        pti = [0]

        def main_loop(sq=0):
          nb_ctr = [0]
          DV = {k_: D[k_][sq * r_:(sq + 1) * r_] for k_, r_ in (("x", 4096), ("y", 4096), ("cmp_p", 4096), ("sel_p", 4096), ("win_p", 512), ("pool_p", 15))}
          if sq > 0:
              allk = ["cmpKT%d" % t_ for t_ in range(8)] + ["cmpV%d" % t_ for t_ in range(8)]
              memset("dve", cmpV[:], 0.0, ["cmpV_init"] + allk)
              memset("dve", cmpKT[:], 0.0, ["cmpKT_init"] + allk)
              memset("dve", imp[:], 0.0, ["imp"])
              memset("dve", carry[:], 0.0, ["carry"])
          for T in range(n_tiles if stop != "prologue" else 0):
              for s in range(4):
                  i = 4 * T + s
                  ld(h[:, s, :], DV["x"][i * 128:(i + 1) * 128, :], "xld%d" % s, ["h%d" % s])
              for s in range(4):
                  i = 4 * T + s
                  nb = nb_ctr[0] % 2
                  nb_ctr[0] += 1
                  rms(h[:, s, :], ["h%d" % s], s)
                  stt(ubf[nb][:], h[:, s, :], rstd[:, s:s + 1], g0bc[:], ALU.mult, ALU.mult, ["h%d" % s, "rstd%d" % s, "g0bc"], ["ubf%d" % nb])
                  if i == NSUB - 1:
                      stt(ysb, h[:, s, :], rstd[:, s:s + 1], g0bc[:], ALU.mult, ALU.mult, ["h%d" % s, "rstd%d" % s, "g0bc"], ["kvf0", "kvf1"])
                      stq(DV["pool_p"], kvf[113:128, 0:1024], "st_misc", ["kvf0", "kvf1"], ["out_pool_p"])
                  for c in range(8):
                      g = c // 2
                      b = c // 4
                      o = ps[:, b, (c % 4) * 128:(c % 4 + 1) * 128]
                      if i == 0:
                          mm(o, ubf[nb][:, c * 128:(c + 1) * 128], band[:, g * 3 + 2, :], True, True, ["ubf%d" % nb, "band"], ["ps%d" % b])
                      else:
                          mm(o, ubf[nb][:, c * 128:(c + 1) * 128], band[:, g * 3 + 0, :], True, False, ["ubf%d" % nb, "band"], ["ps%d" % b])
                          mm(o, ubf[1 - nb][:, c * 128:(c + 1) * 128], band[:, g * 3 + 1, :], False, True, ["ubf%d" % (1 - nb), "band"], ["ps%d" % b])
                  dTb = nT[:, :, 0:128]
                  acopy(dTb, ps[:, 0:2, :].rearrange("p b (c t) -> p (b c) t", t=128), ["ps0", "ps1"], ["nT0"])
                  for g in range(4):
                      b = 2 + g // 2
                      for kc in range(2):
                          mm(ps[:, b, (g % 2) * 256:(g % 2 + 1) * 256], dTb[:, 2 * g + kc, :], poolW[:, g, kc, :], kc == 0, kc == 1,
                             ["nT0", "poolW"], ["ps%d" % b])
                  tt("dve", h[:, s, :].rearrange("p (a n) -> p a n", n=512), ps[:, 2:4, :], h[:, s, :].rearrange("p (a n) -> p a n", n=512), ALU.add,
                     ["ps2", "ps3", "h%d" % s], ["h%d" % s])
              chk("l0")
              mlp(0, 4)
              chk("mlp0")
              S.alias(AT_KEYS, CX_KEYS)
              for s in range(4):
                  norm_T(s, s, s % 2)
              kvW = []
              for s in range(4):
                  i = 4 * T + s
                  for j in range(3):
                      if s == 0:
                          kvW.append(wload(CH_KV(j)))
                      W, wk = kvW[j]
                      Wv = W[:, :].rearrange("p (kc n) -> p kc n", n=512)
                      for kc in range(8):
                          mm(P(j), nT[:, kc, s * 128:(s + 1) * 128], Wv[:, kc, :], kc == 0, kc == 7, [wk, "nT%d" % s], ["ps%d" % j])
                      cp("act" if j != 1 else "dve", kvf[:, j * 512:(j + 1) * 512], P(j), ["ps%d" % j], ["kvf%d" % j])
                  kv5 = kvf[:, :].rearrange("p (br e g d) -> p br e g d", br=3, e=2, g=4)
                  x1 = kv5[:, :, 0, :, 0:8]
                  x2 = kv5[:, :, 0, :, 8:16]
                  cosb = cos_t[:, i, :].unsqueeze(1).unsqueeze(1).to_broadcast([128, 3, 4, 8])
                  sinb = sin_t[:, i, :].unsqueeze(1).unsqueeze(1).to_broadcast([128, 3, 4, 8])
                  rope(x1, x2, cosb, sinb, (3, 4), ["kvf0", "kvf1", "kvf2", "cos", "sin"], ["kvf0", "kvf1", "kvf2"])
                  chk("kv1")
                  stq(DV["cmp_p"][i * 128:(i + 1) * 128, :], kvf[:, 0:512], "st_kv0", ["kvf0"], ["out_cmp_p"])
                  stq(DV["sel_p"][i * 128:(i + 1) * 128, :], kvf[:, 512:1024], "st_kv1", ["kvf1"], ["out_sel_p"])
                  if i >= NSUB - 4:
                      ii = i - (NSUB - 4)
                      stq(DV["win_p"][ii * 128:(ii + 1) * 128, :], kvf[:, 1024:1536], "st_kv2", ["kvf2"], ["out_win_p"])
                  chk("kv2")
                  cp("pool", kvb[:], kvf[:], ["kvf0", "kvf1", "kvf2"], ["kvb"])
                  kb5 = kvb[:, :].rearrange("p (br e g d) -> p br e g d", br=3, e=2, g=4)
                  for g in range(4):
                      tr(Pb(5)[0:64, g * 128:(g + 1) * 128], kb5[:, 1, 0, g, :], ["kvb"], ["ps5"])
                  for g in range(4):
                      tr(Pb(5)[0:64, (4 + g) * 128:(5 + g) * 128], kb5[:, 2, 0, g, :], ["kvb"], ["ps5"])
                  chk("kv2b")
                  acopy(selKT[0:64, :, i * 128:(i + 1) * 128], Pb(5)[0:64, 0:512].rearrange("p (g t) -> p g t", t=128), ["ps5"], ["selKT%d" % i])
                  wsl = i % 8
                  cp("act", winKT[0:64, :, wsl * 128:(wsl + 1) * 128], Pb(5)[0:64, 512:1024].rearrange("p (g t) -> p g t", t=128), ["ps5"], ["winKT%d" % wsl])
                  chk("kv2c")
                  cp("act", selV[:, i, :, 0:64], kb5[:, 1, 1, :, :], ["kvb", "selV_ones"], ["selV%d" % i])
                  cp("act", winV[:, wsl, :, 0:64], kb5[:, 2, 1, :, :], ["kvb", "winV_ones"], ["winV%d" % wsl])
                  chk("kv3")
                  for e_ in range(2):
                      for g in range(4):
                          tr(Pb(6)[0:64, (e_ * 4 + g) * 128:(e_ * 4 + g + 1) * 128], kb5[:, 0, e_, g, :], ["kvb"], ["ps6"])
                  acopy(cmpXT[0:64, :, :, s * 128:(s + 1) * 128], Pb(6)[0:64, :].rearrange("p (e g t) -> p e g t", e=2, g=4), ["ps6"], ["cmpXT%d" % s])
              chk("kv4")
              for e_ in range(2):
                  W, wk = wload(CH_C1(e_), nparts=64)
                  for p_ in range(2):
                      for r_ in range(16):
                          row = p_ * 16 + r_
                          rhs = cmpXT[0:64, e_, :, :].rearrange("p g (sg r) -> p g sg r", r=16)[:, :, :, r_]
                          mm(ps[:, 7, (e_ * 2 + p_) * 128:(e_ * 2 + p_ + 1) * 128].rearrange("p (g sg) -> p g sg", g=4),
                             W[0:64, row * 128:(row + 1) * 128], rhs, r_ == 0, r_ == 15,
                             [wk] + ["cmpXT%d" % s for s in range(4)], ["ps7"])
              cp("dve", parts[:].rearrange("p e q g sg -> p (e q g sg)"), P(7), ["ps7"], ["parts"])
              chk("kv5")
              tt("dve", hid[:, :, :, 1:32], parts[:, :, 0, :, 0:31], parts[:, :, 1, :, 1:32], ALU.add, ["parts"], ["hid"])
              tt("dve", hid[:, :, :, 0:1], carry[:].unsqueeze(3), parts[:, :, 1, :, 0:1], ALU.add, ["parts", "carry"], ["hid"])
              cp("dve", carry[:].unsqueeze(3), parts[:, :, 0, :, 31:32], ["parts", "hid"], ["carry"])
              for e_ in range(2):
                  ts("dve", hid[:, e_], hid[:, e_], petot[:, e_:e_ + 1], None, ALU.add, None, ["hid", "petot"], ["hid"])
              hf_ = hid[:].rearrange("p e g s -> p (e g s)")
              h2_ = hid2[:].rearrange("p e g s -> p (e g s)")
              tt("dve", h2_, hf_, hf_, ALU.mult, ["hid"], ["hid2"])
              ts("dve", h2_, h2_, 0.044715, 1.0, ALU.mult, ALU.add, ["hid2"], ["hid2"])
              tt("dve", h2_, h2_, hf_, ALU.mult, ["hid2", "hid"], ["hid2"])
              act(h2_, h2_, AF.Exp, ["hid2"], ["hid2"], scale=-1.5957691216)
              ts("dve", h2_, h2_, 1.0, None, ALU.add, None, ["hid2"], ["hid2"])
              S.op("dve", lambda e, a=h2_: e.reciprocal(out=a, in_=a), ["hid2"], ["hid2"])
              tt("dve", hidb[:].rearrange("p e g s -> p (e g s)"), h2_, hf_, ALU.mult, ["hid2", "hid"], ["hidb"])
              chk("kv6")
              mm(ps[0:64, 7, 0:128], w2bf[:, 0, :], hidb[:, 0].rearrange("p g s -> p (g s)"), True, True, ["w2bf", "hidb"], ["ps7"])
              cp("dve", cmpKT[0:64, :, 32 * T:32 * T + 32], ps[0:64, 7, 0:128].rearrange("p (g s) -> p g s", g=4), ["ps7", "cmpKT_init"], ["cmpKT%d" % T])
              pq = 32 * (T % 3)
              for g in range(4):
                  mm(ps[pq:pq + 32, 7, 128 + g * 64:128 + (g + 1) * 64], hidb[:, 1, g, :], w2bf[:, 1, :], True, True, ["w2bf", "hidb"], ["ps7"])
              cp("dve", cmpV[pq:pq + 32, T // 3, :, 0:64], ps[pq:pq + 32, 7, 128:384].rearrange("p (g d) -> p g d", g=4), ["ps7", "cmpV_init"], ["cmpV%d" % T])
              memset("dve", cmpV[pq:pq + 32, T // 3, :, 64:65], 1.0, ["cmpV%d" % T])
              if T == 0:
                  memset("dve", cmpV[0:1, 0, :, :], 0.0, ["cmpV0"])

              chk("kv")
              S.alias(AT_KEYS + CX_KEYS, ATT_KEYS)
              for s in range(4):
                  norm_T(s, s, s % 2)
              qW = []
              for s in range(4):
                  i = 4 * T + s
                  for j in range(3):
                      if s == 0:
                          qW.append(wload(CH_QG(j), width=4096 if j < 2 else 384))
                      W, wk = qW[j]
                      if j < 2:
                          Wv = W[:, :].rearrange("p (kc n) -> p kc n", n=512)
                          for kc in range(8):
                              mm(P(j), nT[:, kc, s * 128:(s + 1) * 128], Wv[:, kc, :], kc == 0, kc == 7, [wk, "nT%d" % s], ["ps%d" % j])
                          cp("act" if j == 0 else "dve", kvf[:, j * 512:(j + 1) * 512], P(j), ["ps%d" % j], ["kvf%d" % j])
                      else:
                          Wv = W[:, 0:384].rearrange("p (kc n) -> p kc n", n=48)
                          for kc in range(8):
                              mm(P(2)[:, 0:48], nT[:, kc, s * 128:(s + 1) * 128], Wv[:, kc, :], kc == 0, kc == 7, [wk, "nT%d" % s], ["ps2"])
                          act(gate[:, s, :], P(2)[:, 0:48], AF.Exp, ["ps2"], ["gate%d" % s], scale=-1.0)
                          ts("dve", gate[:, s, :], gate[:, s, :], 1.0, None, ALU.add, None, ["gate%d" % s], ["gate%d" % s])
                          S.op("dve", lambda e, s=s: e.reciprocal(out=gate[:, s, :], in_=gate[:, s, :]), ["gate%d" % s], ["gate%d" % s])
                  q3 = kvf[:, 0:1024].rearrange("p (hh d) -> p hh d", d=64)
                  x1 = q3[:, :, 0:8].unsqueeze(1)
                  x2 = q3[:, :, 8:16].unsqueeze(1)
                  cosb = cos_t[:, i, :].unsqueeze(1).unsqueeze(1).to_broadcast([128, 1, 16, 8])
                  sinb = sin_t[:, i, :].unsqueeze(1).unsqueeze(1).to_broadcast([128, 1, 16, 8])
                  rope(x1, x2, cosb, sinb, (1, 16), ["kvf0", "kvf1", "cos", "sin"], ["kvf0", "kvf1"])
                  cp("pool", q_bf[:, s, :], kvf[:, 0:1024], ["kvf0", "kvf1"], ["q_bf%d" % s])

              chk("qg")
              for s in range(4):
                  i = 4 * T + s
                  qb = i % 2
                  qTt = qT[qb]
                  qk, qmk = "qT%d" % qb, "qTm%d" % qb
                  for hh in range(16):
                      bnk = 4 + hh // 8
                      tr(Pb(bnk)[0:64, (hh % 8) * 128:(hh % 8 + 1) * 128], q_bf[:, s, hh * 64:(hh + 1) * 64], ["q_bf%d" % s], ["ps%d" % bnk])
                  acopy(qTt[0:64, 0:8, :], Pb(4)[0:64, :].rearrange("p (hh t) -> p hh t", t=128), ["ps4"], [qk])
                  cp("act", qTt[0:64, 8:16, :], Pb(5)[0:64, :].rearrange("p (hh t) -> p hh t", t=128), ["ps5"], [qk])
                  ncm = 8 * i + 8
                  def chain(g):
                      hs = slice(4 * g, 4 * g + 4)
                      if i >= 8:
                          for hh in range(4):
                              bnk = hh // 2
                              mm(ps[:, bnk, (hh % 2) * 256:(hh % 2) * 256 + ncm], qTt[0:64, 4 * g + hh, :], cmpKT[0:64, g, 0:ncm], True, True,
                                 [qk] + ["cmpKT%d" % t for t in range(T + 1)], ["ps%d" % bnk])
                          sc4 = ps[:, 0:2, :].rearrange("p b (hh m) -> p (b hh) m", m=256)
                          tt("dve", sc4[:, :, ncm - 8:ncm], sc4[:, :, ncm - 8:ncm], cmq[:, :].unsqueeze(1).to_broadcast([128, 4, 8]), ALU.add,
                             ["ps0", "ps1", "cmq"], ["ps0", "ps1"])
                          ts("dve", sc4[:, :, 0:1], sc4[:, :, 0:1], NEG, None, ALU.add, None, ["ps0", "ps1"], ["ps0", "ps1"])
                          for hh in range(4):
                              act(Ebuf[:, hh, 0:ncm], sc4[:, hh, 0:ncm], AF.Exp, ["ps0", "ps1"], ["E", "esum"], accum_out=esum[:, hh:hh + 1])
                          ts("dve", esum[:, 0:4], esum[:, 0:4], 1e-30, None, ALU.max, None, ["esum"], ["esum"])
                          S.op("dve", lambda e: e.reciprocal(out=esum[:, 4:8], in_=esum[:, 0:4]), ["esum"], ["esum"])
                          ts("dve", imp[:, 0:ncm], Ebuf[:, 0, 0:ncm], esum[:, 4:5], None, ALU.mult, None, ["E", "esum"], ["imp"])
                          for hh in range(1, 4):
                              stt(imp[:, 0:ncm], Ebuf[:, hh, 0:ncm], esum[:, 4 + hh:5 + hh], imp[:, 0:ncm], ALU.mult, ALU.add, ["E", "esum", "imp"], ["imp"])
                          A = imp[:, 0:256].rearrange("p (j t) -> p j t", t=4)
                          B = imp[:, 4:260].rearrange("p (j t) -> p j t", t=4)
                          tt("dve", score[:], A[:, :, 1], A[:, :, 2], ALU.add, ["imp"], ["score"])
                          tt("dve", score[:], score[:], A[:, :, 3], ALU.add, ["imp", "score"], ["score"])
                          stt(score[:], score[:], 2.0, A[:, :, 0], ALU.mult, ALU.add, ["imp", "score"], ["score"])
                          tt("dve", score[:], score[:], B[:, :, 0], ALU.add, ["imp", "score"], ["score"])
                          tt("dve", score[:], score[:], gpat[:, 64 - 2 * i:128 - 2 * i], ALU.add, ["score", "gpat"], ["score"])
                          memset("dve", score[:, 0:1], 100.0, ["score"])
                          S.op("dve", lambda e: e.max(out=mx[:, 0:8], in_=score[:]), ["score"], ["mx"])
                          S.op("dve", lambda e: e.match_replace(out=score2[:], in_to_replace=mx[:, 0:8], in_values=score[:], imm_value=-1e9), ["score", "mx"], ["score2"])
                          S.op("dve", lambda e: e.max(out=mx[:, 8:16], in_=score2[:]), ["score2"], ["mx"])
                          ts("dve", mbfull[:, 64:128], score[:], mx[:, 15:16], 1.0, ALU.is_ge, ALU.subtract, ["score", "mx"], ["mbfull"])
                          tr(Pb(7)[:, 0:128], mbfull[:], ["mbfull"], ["ps7"])
                          S.op("act", lambda e, qTt=qTt, hs=hs: e.mul(out=qTt[64:128, hs, :], in_=Pb(7)[64:128, 0:128].unsqueeze(1).to_broadcast([64, 4, 128]), mul=-NEG),
                               ["ps7"], [qmk + "_%d" % g])
                      else:
                          memset("dve", qTt[64:128, hs, :], 0.0, [qmk + "_%d" % g])
                  def attn(g):
                      hs = slice(4 * g, 4 * g + 4)
                      rhs_aug = qTt[:, hs, :]
                      rhs_q = qTt[0:64, hs, :]
                      rk = [qk, qmk + "_%d" % g]
                      pend = [None]

                      def branch(obank, ktiles, nk=128):
                          n = len(ktiles)
                          for idx, (lhsT, lk, mrhs, mlhs, Vap, vk, aug) in enumerate(ktiles):
                              sb_ = 2 + (pti[0] % 2)
                              pt_i = pti[0] % 2
                              pti[0] += 1
                              mm(P(sb_)[0:nk], lhsT, rhs_aug if aug else rhs_q, True, mrhs is None, lk + rk, ["ps%d" % sb_])
                              if mrhs is not None:
                                  mm(P(sb_)[0:nk], mlhs, mrhs, False, True, ["ident", "ident4", "tri", "tri2", "cmpat"], ["ps%d" % sb_])
                              act(PT[pt_i][0:nk], P(sb_)[0:nk], AF.Exp, ["ps%d" % sb_], ["PT%d" % pt_i])

                              def pv(pt_i=pt_i, Vap=Vap, vk=vk, idx=idx, n=n, obank=obank, nk=nk):
                                  for hh in range(4):
                                      mm(ps[:, obank, hh * 65:(hh + 1) * 65], PT[pt_i][0:nk, hh * 128:(hh + 1) * 128], Vap, (idx == 0 and hh == 0), (idx == n - 1),
                                         ["PT%d" % pt_i] + vk, ["ps%d" % obank])
                              if pend[0] is not None:
                                  pend[0]()
                              pend[0] = pv

                      def flush():
                          if pend[0] is not None:
                              pend[0]()
                              pend[0] = None

                      kt = []
                      for t in range(3):
                          if 96 * t < ncm:
                              off = 96 * t - 8 * i + 248
                              kt.append((cmpKT[0:64, g, t * 96:(t + 1) * 96], ["cmpKT%d" % tt_ for tt_ in range(T + 1)] + ["cmpKT_init"],
                                         ident4[:], cmpat[:, off:off + 96], cmpV[0:96, t, g, :], ["cmpV%d" % tt_ for tt_ in range(T + 1)] + ["cmpV_init"], False))
                      branch(4, kt, nk=96)
                      kt = []
                      for j in range(i + 1):
                          kt.append((selKT[:, g, j * 128:(j + 1) * 128], ["selKT%d" % j, "selKT_ind"],
                                     tri[:] if j == i else None, ident[:], selV[:, j, g, :], ["selV%d" % j], True))
                      branch(5, kt)
                      kt = []
                      for j in range(max(0, i - 4), i + 1):
                          m_ = tri[:] if j == i else (tri2[:] if j == i - 4 else None)
                          kt.append((winKT[0:64, g, (j % 8) * 128:(j % 8 + 1) * 128], ["winKT%d" % (j % 8)],
                                     m_, ident[:], winV[:, j % 8, g, :], ["winV%d" % (j % 8)], False))
                      branch(6, kt)
                      flush()
                      for br in range(3):
                          cp("dve", den[:, br, :], ps[:, 4 + br, 0:260].rearrange("p (hh d) -> p hh d", d=65)[:, :, 64], ["ps%d" % (4 + br)], ["den"])
                      ts("dve", den[:], den[:], 1e-30, None, ALU.max, None, ["den"], ["den"])
                      S.op("dve", lambda e: e.reciprocal(out=den[:], in_=den[:]), ["den"], ["den"])
                      gv = gate[:, s, 12 * g:12 * g + 12].rearrange("p (hh br) -> p br hh", br=3)
                      tt("dve", wgt[:], den[:], gv, ALU.mult, ["den", "gate%d" % s], ["wgt"])
                      for br in range(3):
                          tt("dve", ocomb[:, br], ps[:, 4 + br, 0:260].rearrange("p (hh d) -> p hh d", d=65)[:, :, 0:64],
                             wgt[:, br, :].unsqueeze(2).to_broadcast([128, 4, 64]), ALU.mult, ["ps%d" % (4 + br), "wgt"], ["ocomb%d" % br])
                      tt("pool", ocomb[:, 0], ocomb[:, 0], ocomb[:, 1], ALU.add, ["ocomb0", "ocomb1"], ["ocomb0"])
                      tt("pool", o_bf[:, 256 * g:256 * (g + 1)].rearrange("p (hh d) -> p hh d", d=64), ocomb[:, 0], ocomb[:, 2], ALU.add,
                         ["ocomb0", "ocomb2"], ["o_bf"])
                  chain(0)
                  for g in range(4):
                      if g < 3:
                          chain(g + 1)
                      attn(g)
                  for kc in range(8):
                      tr(Pb(7)[:, kc * 128:(kc + 1) * 128], o_bf[:, kc * 128:(kc + 1) * 128], ["o_bf"], ["ps7"])
                  acopy(oT[:, :, s * 128:(s + 1) * 128], Pb(7).rearrange("p (k t) -> p k t", t=128), ["ps7"], ["oT%d" % s])
              chk("attn")
              oW = []
              for s in range(4):
                  for j in range(2):
                      if s == 0:
                          oW.append(wload(CH_O(j)))
                      W, wk = oW[j]
                      Wv = W[:, :].rearrange("p (kc n) -> p kc n", n=512)
                      b = (2 * s + j) % 4
                      for kc in range(8):
                          mm(P(b), oT[:, kc, s * 128:(s + 1) * 128], Wv[:, kc, :], kc == 0, kc == 7, [wk, "oT%d" % s], ["ps%d" % b])
                      tt("dve", h[:, s, j * 512:(j + 1) * 512], P(b), h[:, s, j * 512:(j + 1) * 512], ALU.add, ["ps%d" % b, "h%d" % s], ["h%d" % s])
              chk("wo")
              mlp(1, 4)
              chk("mlp1")
              for s in range(4):
                  i = 4 * T + s
                  rms(h[:, s, :], ["h%d" % s], s)
                  stt(ysb, h[:, s, :], rstd[:, s:s + 1], gfbc[:], ALU.mult, ALU.mult, ["h%d" % s, "rstd%d" % s, "gfbc"], ["kvf0", "kvf1"])
                  stq(DV["y"][i * 128:(i + 1) * 128, :], ysb, "st_y", ["kvf0", "kvf1"], ["out_y"])


        def sample_phase(ss=0):
            S.barrier()
            DS = {k_: D[k_][ss * r_:(ss + 1) * r_] for k_, r_ in (("xs", 4), ("spool", 60), ("swin", 2048), ("ys", 4), ("cmp_s", 4), ("sel_s", 4), ("win_s", 2048), ("pool_s", 60))}
            hflat = h[:].rearrange("p a b -> p (a b)")
            E1 = hflat[:, 1024:2048]
            imp_s = hflat[:, 2048:3076]
            misc = hflat[:, 3076:4096]
            ptb_i = misc[:, 0:128].bitcast(I32)
            idxp = misc[:, 128:256].bitcast(I32)
            PTf = misc[:, 256:384]
            iota_f = misc[:, 384:512]
            jl4 = misc[:, 512:544].bitcast(BF16).rearrange("p (g t) -> p g t", g=4)
            cxs = big[:, 8192:12288].rearrange("p (e g t) -> p e g t", e=2, g=4)
            jb = misc[:, 576:640]
            ji = misc[:, 640:704].bitcast(I32)
            pgi = misc[:, 704:768].bitcast(I32)
            pgf_ = misc[:, 768:832]
            parf = misc[:, 832:896]
            rb = misc[:, 896:960].rearrange("p (g t) -> p g t", g=4)
            idxs_f = misc[:, 960:992].rearrange("p (g t) -> p g t", g=4)
            idxs_i = misc[:, 992:1020].bitcast(I32)
            Ef = Ebuf[:].rearrange("p a b -> p (a b)")
            score_s = Ef[:, 0:256]
            score2_s = Ef[:, 256:512]
            mx_s = Ef[:, 512:528]
            ix_s = Ef[:, 528:544].bitcast(U32)
            phys = Ef[:, 544:608].rearrange("p (g t) -> p g t", g=4)
            idxs_i = Ef[:, 608:640].bitcast(I32).rearrange("p (g t) -> p g t", g=4)
            es_s = Ef[:, 640:648]
            cmpKT_s = selKT[:, 0, :].rearrange("p (g m) -> p g m", g=4)
            selKT_s = selKT[:, 1, :].rearrange("p (g t) -> p g t", g=4)
            winKT_s = selKT[:, 2, 0:2048].rearrange("p (g t) -> p g t", g=4)
            KTnew = selKT[:, 2, 2048:3072].rearrange("p (b g t) -> p b g t", b=2, g=4)
            sv = selV[:].rearrange("p a g d -> p (a g d)")
            cmpV_s = sv[:, 0:2860].rearrange("p (t g d) -> p t g d", g=4, d=65)
            selV_s = sv[:, 2860:4940].rearrange("p (t g d) -> p t g d", g=4, d=65)
            winV_s = sv[:, 4940:5980].rearrange("p (t g d) -> p t g d", g=4, d=65)
            Vnew = sv[:, 5980:6500].rearrange("p (t g d) -> p t g d", g=4, d=65)
            ring = nT[:].rearrange("p a b -> p (a b)").bitcast(F32)
            ohs = wslot[2][:, :].bitcast(F32).rearrange("p (t q) -> p t q", q=128)
            o_acc = kvf[:, 0:1024]
            ring_i = [0]

            def ring_slot():
                i = ring_i[0] % 4
                ring_i[0] += 1
                return ring[:, i * 512:(i + 1) * 512], "ring%d" % i

            memset("dve", h[:, 0, :], 0.0, ["h0"])
            ld(h[0:4, 0, :], DS["xs"], "xld0", ["h0"])
            ld(iota_f, D["c_iota"], "c0", ["iota_f"])
            rms(h[:, 0, :], ["h0"], 0)
            us_f = kvf[:, 0:1024]
            stt(us_f, h[:, 0, :], rstd[:, 0:1], g0bc[:], ALU.mult, ALU.mult, ["h0", "rstd0", "g0bc"], ["kvf0", "kvf1"])
            uext = hflat[0:64, 1024:2048]
            for b in range(4):
                ld(hflat[16 * b:16 * b + 15, 1024:2048], DS["spool"][15 * b:15 * b + 15, :], "smp0", ["uext"])
                ld(hflat[16 * b + 15:16 * b + 16, 1024:2048], kvf[b:b + 1, 0:1024], "smp0", ["uext"], r=["kvf0", "kvf1"])
                stq(DS["pool_s"][15 * b:15 * b + 14, :], DS["spool"][15 * b + 1:15 * b + 15, :], "st_s", [], ["out_s"])
                stq(DS["pool_s"][15 * b + 14:15 * b + 15, :], kvf[b:b + 1, 0:1024], "st_s", ["kvf0", "kvf1"], ["out_s"])
            cp("dve", ubf[0][0:64, :], uext, ["uext"], ["ubf0"])
            for c in range(8):
                g = c // 2
                b_ = c // 4
                mm(ps[:, b_, (c % 4) * 128:(c % 4) * 128 + 4], ubf[0][0:64, c * 128:(c + 1) * 128], sband[0:64, g, :], True, True, ["ubf0", "sband"], ["ps%d" % b_])
            dTb = nT[:, :, 0:128]
            memset("dve", dTb, 0.0, ["nT0"])
            acopy(dTb[:, :, 0:4], ps[:, 0:2, :].rearrange("p b (c t) -> p (b c) t", t=128)[:, :, 0:4], ["ps0", "ps1"], ["nT0"])
            for g in range(4):
                b_ = 2 + g // 2
                for kc in range(2):
                    mm(ps[:, b_, (g % 2) * 256:(g % 2 + 1) * 256], dTb[:, 2 * g + kc, :], poolW[:, g, kc, :], kc == 0, kc == 1, ["nT0", "poolW"], ["ps%d" % b_])
            tt("dve", h[:, 0, :].rearrange("p (a n) -> p a n", n=512), ps[:, 2:4, :], h[:, 0, :].rearrange("p (a n) -> p a n", n=512), ALU.add,
               ["ps2", "ps3", "h0"], ["h0"])
            chk("s_l0")
            mlp(0, 1)
            chk("s_mlp0")
            S.alias(AT_KEYS, ATT_KEYS + CX_KEYS + ["cxs%d_%d" % (a_, q) for a_ in range(2) for q in range(4)])
            norm_T(0, 0, 0)
            for j in range(3):
                W, wk = wload(CH_KV(j))
                Wv = W[:, :].rearrange("p (kc n) -> p kc n", n=512)
                for kc in range(8):
                    mm(P(j), nT[:, kc, 0:128], Wv[:, kc, :], kc == 0, kc == 7, [wk, "nT0"], ["ps%d" % j])
                cp("act", kvf[:, j * 512:(j + 1) * 512], P(j), ["ps%d" % j], ["kvf%d" % j])
            kv5 = kvf[:, :].rearrange("p (br e g d) -> p br e g d", br=3, e=2, g=4)
            cosb = coss[:, :].unsqueeze(1).unsqueeze(1).to_broadcast([128, 3, 4, 8])
            sinb = sins[:, :].unsqueeze(1).unsqueeze(1).to_broadcast([128, 3, 4, 8])
            rope(kv5[:, :, 0, :, 0:8], kv5[:, :, 0, :, 8:16], cosb, sinb, (3, 4), ["kvf0", "kvf1", "kvf2", "coss", "sins"], ["kvf0", "kvf1", "kvf2"])
            stq(DS["cmp_s"], kvf[0:4, 0:512], "st_s", ["kvf0"], ["out_s"])
            stq(DS["sel_s"], kvf[0:4, 512:1024], "st_s", ["kvf1"], ["out_s"])
            stq(DS["win_s"].rearrange("(b r) c -> b r c", b=4)[:, 511, :], kvf[0:4, 1024:1536], "st_s", ["kvf2"], ["out_s"])
            stq(DS["win_s"].rearrange("(b r) c -> b r c", b=4)[:, 0:511, :], DS["swin"].rearrange("(b r) c -> b r c", b=4)[:, 1:512, :], "st_s", [], ["out_s"])
            cp("pool", kvb[:], kvf[:], ["kvf0", "kvf1", "kvf2"], ["kvb"])
            kb5 = kvb[:, :].rearrange("p (br e g d) -> p br e g d", br=3, e=2, g=4)
            for bi, br in enumerate((1, 2)):
                for g in range(4):
                    tr(Pb(5)[0:64, (bi * 4 + g) * 128:(bi * 4 + g + 1) * 128], kb5[:, br, 0, g, :], ["kvb"], ["ps5"])
            acopy(KTnew[0:64], Pb(5)[0:64, :].rearrange("p (b g t) -> p b g t", b=2, g=4), ["ps5"], ["KTnew"])
            vnew_all = ubf[1][:, 0:512].rearrange("p (b g d) -> p b g d", b=2, g=4)
            for bi, br in enumerate((1, 2)):
                cp("act", vnew_all[:, bi], kb5[:, br, 1, :, :], ["kvb"], ["vnew_all"])
            chk("s_kv")
            norm_T(0, 0, 0)
            for j in range(3):
                W, wk = wload(CH_QG(j), width=4096 if j < 2 else 384)
                if j < 2:
                    Wv = W[:, :].rearrange("p (kc n) -> p kc n", n=512)
                    for kc in range(8):
                        mm(P(j), nT[:, kc, 0:128], Wv[:, kc, :], kc == 0, kc == 7, [wk, "nT0"], ["ps%d" % j])
                    cp("act", kvf[:, j * 512:(j + 1) * 512], P(j), ["ps%d" % j], ["kvf%d" % j])
                else:
                    Wv = W[:, 0:384].rearrange("p (kc n) -> p kc n", n=48)
                    for kc in range(8):
                        mm(P(2)[:, 0:48], nT[:, kc, 0:128], Wv[:, kc, :], kc == 0, kc == 7, [wk, "nT0"], ["ps2"])
                    act(gate[:, 0, :], P(2)[:, 0:48], AF.Exp, ["ps2"], ["gate0"], scale=-1.0)
                    ts("dve", gate[:, 0, :], gate[:, 0, :], 1.0, None, ALU.add, None, ["gate0"], ["gate0"])
                    S.op("dve", lambda e: e.reciprocal(out=gate[:, 0, :], in_=gate[:, 0, :]), ["gate0"], ["gate0"])
            q3 = kvf[:, 0:1024].rearrange("p (hh d) -> p hh d", d=64)
            cosb = coss[:, :].unsqueeze(1).unsqueeze(1).to_broadcast([128, 1, 16, 8])
            sinb = sins[:, :].unsqueeze(1).unsqueeze(1).to_broadcast([128, 1, 16, 8])
            rope(q3[:, :, 0:8].unsqueeze(1), q3[:, :, 8:16].unsqueeze(1), cosb, sinb, (1, 16), ["kvf0", "kvf1", "coss", "sins"], ["kvf0", "kvf1"])
            q_bf_s = big[:, 14336:15360]
            cp("pool", q_bf_s, kvf[:, 0:1024], ["kvf0", "kvf1"], ["q_bf0"])
            qTt = qT[0]
            for hh in range(16):
                bnk = 4 + hh // 8
                tr(Pb(bnk)[0:64, (hh % 8) * 128:(hh % 8 + 1) * 128], q_bf_s[:, hh * 64:(hh + 1) * 64], ["q_bf0"], ["ps%d" % bnk])
            acopy(qTt[0:64, 0:8, :], Pb(4)[0:64, :].rearrange("p (hh t) -> p hh t", t=128), ["ps4"], ["qT0"])
            cp("act", qTt[0:64, 8:16, :], Pb(5)[0:64, :].rearrange("p (hh t) -> p hh t", t=128), ["ps5"], ["qT0"])
            chk("s_qg")
            memset("dve", o_acc, 0.0, ["kvf0", "kvf1"])
            memset("dve", sv[:, 0:6500], 0.0, ["sv"])
            memset("dve", selV_s[:, :, :, 64:65], 1.0, ["sv"])
            memset("dve", winV_s[:, :, :, 64:65], 1.0, ["sv"])
            memset("dve", imp_s, 0.0, ["imp_s"])

            pend_s = [None]

            def attend(obank, ktiles, rhs_q, rk):
                n = len(ktiles)
                for idx, (lhsT, lk, Vap, vk, nk) in enumerate(ktiles):
                    sb_ = 2 + (pti[0] % 2)
                    pt_i = pti[0] % 2
                    pti[0] += 1
                    mm(P(sb_)[0:nk], lhsT, rhs_q, True, True, lk + rk, ["ps%d" % sb_])
                    act(PT[pt_i][0:nk], P(sb_)[0:nk], AF.Exp, ["ps%d" % sb_], ["PT%d" % pt_i])

                    def pv(pt_i=pt_i, Vap=Vap, vk=vk, idx=idx, n=n, obank=obank, nk=nk):
                        for hh in range(4):
                            mm(ps[:, obank, hh * 65:(hh + 1) * 65], PT[pt_i][0:nk, hh * 128:(hh + 1) * 128], Vap, (idx == 0 and hh == 0), (idx == n - 1),
                               ["PT%d" % pt_i] + vk, ["ps%d" % obank])
                    if pend_s[0] is not None:
                        pend_s[0]()
                    pend_s[0] = pv

            def flush_s():
                if pend_s[0] is not None:
                    pend_s[0]()
                    pend_s[0] = None

            for b in range(4):
                ld(ptb_i, D["ptab"][0:1, ss * 512 + b * 128: ss * 512 + (b + 1) * 128].partition_broadcast(128), "smp1", ["ptb"])
                ts("dve", idxp, ptb_i, 128, pcol[:, 7:8], ALU.mult, ALU.add, ["ptb", "pcol"], ["idxp"])
                cp("dve", PTf, ptb_i, ["ptb"], ["PTf"])
                memset("dve", carry[:], 0.0, ["carry"])
                cxs_b = [big[:, 8192:12288].rearrange("p (e g t) -> p e g t", e=2, g=4), big[:, 0:4096].rearrange("p (e g t) -> p e g t", e=2, g=4)]
                parts_b = [parts, relu_t[0][:, :].rearrange("p (e q g sg) -> p e q g sg", e=2, q=2, g=4)]
                parts_k = ["parts", "relu0"]
                pbank = [7, 4]

                def stageA(Tq):
                    bf_ = Tq % 2
                    cx = cxs_b[bf_]
                    for s_ in range(4):
                        pg = 4 * Tq + s_
                        slot, sk = ring_slot()
                        S.dma("pool", lambda e, slot=slot, pg=pg: e.indirect_dma_start(out=slot, out_offset=None, in_=CC,
                              in_offset=bass.IndirectOffsetOnAxis(ap=idxp[:, pg:pg + 1], axis=0)), "g_" + sk, ["idxp", "CC"], [sk])
                        kslot = kvb[:, (pg % 2) * 512:(pg % 2 + 1) * 512]
                        kkey = "kvb" if pg % 2 == 0 else "kvb2"
                        tb = 6 if pg % 2 == 0 else 3
                        cp("dve", kslot, slot, [sk], [kkey])
                        kc5 = kslot.rearrange("p (e g d) -> p e g d", e=2, g=4)
                        for e_ in range(2):
                            for g in range(4):
                                tr(Pb(tb)[0:64, (e_ * 4 + g) * 128:(e_ * 4 + g + 1) * 128], kc5[:, e_, g, :], [kkey], ["ps%d" % tb])
                        acopy(cx[0:64, :, :, s_ * 128:(s_ + 1) * 128], Pb(tb)[0:64, :].rearrange("p (e g t) -> p e g t", e=2, g=4), ["ps%d" % tb], ["cxs%d_%d" % (bf_, s_)])
                    pb_ = pbank[bf_]
                    for e_ in range(2):
                        W, wk = Wc[e_]
                        for p_ in range(2):
                            for r_ in range(16):
                                row = p_ * 16 + r_
                                rhs = cx[0:64, e_, :, :].rearrange("p g (sg r) -> p g sg r", r=16)[:, :, :, r_]
                                mm(ps[:, pb_, (e_ * 2 + p_) * 128:(e_ * 2 + p_ + 1) * 128].rearrange("p (g sg) -> p g sg", g=4),
                                   W[0:64, row * 128:(row + 1) * 128], rhs, r_ == 0, r_ == 15, [wk] + ["cxs%d_%d" % (bf_, q) for q in range(4)], ["ps%d" % pb_])
                    cp("dve", parts_b[bf_][:].rearrange("p e q g sg -> p (e q g sg)") if bf_ == 0 else relu_t[0][:, :], P(pb_), ["ps%d" % pb_], [parts_k[bf_]])

                def stageB(Tq):
                    bf_ = Tq % 2
                    pt_ = parts_b[bf_]
                    pk_ = parts_k[bf_]
                    tt("dve", hid[:, :, :, 1:32], pt_[:, :, 0, :, 0:31], pt_[:, :, 1, :, 1:32], ALU.add, [pk_], ["hid"])
                    tt("dve", hid[:, :, :, 0:1], carry[:].unsqueeze(3), pt_[:, :, 1, :, 0:1], ALU.add, [pk_, "carry"], ["hid"])
                    cp("dve", carry[:].unsqueeze(3), pt_[:, :, 0, :, 31:32], [pk_, "hid"], ["carry"])
                    for e_ in range(2):
                        ts("dve", hid[:, e_], hid[:, e_], petot[:, e_:e_ + 1], None, ALU.add, None, ["hid", "petot"], ["hid"])
                    hf_ = hid[:].rearrange("p e g s -> p (e g s)")
                    h2_ = hid2[:].rearrange("p e g s -> p (e g s)")
                    tt("dve", h2_, hf_, hf_, ALU.mult, ["hid"], ["hid2"])
                    ts("dve", h2_, h2_, 0.044715, 1.0, ALU.mult, ALU.add, ["hid2"], ["hid2"])
                    tt("dve", h2_, h2_, hf_, ALU.mult, ["hid2", "hid"], ["hid2"])
                    act(h2_, h2_, AF.Exp, ["hid2"], ["hid2"], scale=-1.5957691216)
                    ts("dve", h2_, h2_, 1.0, None, ALU.add, None, ["hid2"], ["hid2"])
                    S.op("dve", lambda e, a=h2_: e.reciprocal(out=a, in_=a), ["hid2"], ["hid2"])
                    tt("dve", hidb[:].rearrange("p e g s -> p (e g s)"), h2_, hf_, ALU.mult, ["hid2", "hid"], ["hidb"])
                    mm(ps[0:64, 5, 0:128], w2bf[:, 0, :], hidb[:, 0].rearrange("p g s -> p (g s)"), True, True, ["w2bf", "hidb"], ["ps5"])
                    cp("dve", cmpKT_s[0:64, :, 32 * Tq:32 * Tq + 32], ps[0:64, 5, 0:128].rearrange("p (g s) -> p g s", g=4), ["ps5"], ["cmpKT_s"])
                    pq = 32 * (Tq % 3)
                    for g in range(4):
                        mm(ps[pq:pq + 32, 5, 128 + g * 64:128 + (g + 1) * 64], hidb[:, 1, g, :], w2bf[:, 1, :], True, True, ["w2bf", "hidb"], ["ps5"])
                    cp("dve", cmpV_s[pq:pq + 32, Tq // 3, :, 0:64], ps[pq:pq + 32, 5, 128:384].rearrange("p (g d) -> p g d", g=4), ["ps5", "sv"], ["cmpV_s"])
                    memset("dve", cmpV_s[pq:pq + 32, Tq // 3, :, 64:65], 1.0, ["cmpV_s"])
                    if Tq == 0:
                        memset("dve", cmpV_s[0:1, 0, :, :], 0.0, ["cmpV_s"])

                Wc = [wload(CH_C1(e_), nparts=64) for e_ in range(2)]
                stageA(0)
                for Tq in range(32):
                    if Tq + 1 < 32:
                        stageA(Tq + 1)
                    stageB(Tq)
                chk("s_cmp%d" % b)
                for t in range(4):
                    slot, sk = ring_slot()
                    ld(slot, DS["swin"][b * 512 + t * 128: b * 512 + (t + 1) * 128, :], "g_" + sk, [sk])
                    cp("dve", kvb[:, 512:1024], slot, [sk], ["kvb2"])
                    kw = kvb[:, 512:1024].rearrange("p (e g d) -> p e g d", e=2, g=4)
                    for g in range(4):
                        tr(Pb(5)[0:64, g * 128:(g + 1) * 128], kw[:, 0, g, :], ["kvb2"], ["ps5"])
                    acopy(winKT_s[0:64, :, t * 128:(t + 1) * 128], Pb(5)[0:64, 0:512].rearrange("p (g t) -> p g t", g=4), ["ps5"], ["winKT_s"])
                    cp("act", winV_s[:, t, :, 0:64], kw[:, 1, :, :], ["kvb2", "sv"], ["winV_s"])
                for bi in range(2):
                    ts("dve", Vnew[:, bi, :, 0:64], vnew_all[:, bi], pcol[:, 3 + b:4 + b], None, ALU.mult, None, ["vnew_all", "pcol", "sv"], ["Vnew"])
                    for g in range(4):
                        cp("dve", Vnew[:, bi, g, 64:65], pcol[:, 3 + b:4 + b], ["pcol", "sv"], ["Vnew"])
                for g in range(4):
                    rhs_q = qTt[0:64, 4 * g:4 * g + 4, :]
                    rk = ["qT0"]
                    for hh in range(4):
                        for half in range(2):
                            mm(P(half), qTt[0:64, 4 * g + hh, :], cmpKT_s[0:64, g, half * 512:(half + 1) * 512], True, True, ["qT0", "cmpKT_s"], ["ps%d" % half])
                        ts("dve", ps[:, 0, 0:1], ps[:, 0, 0:1], NEG, None, ALU.add, None, ["ps0"], ["ps0"])
                        act(E1.rearrange("p (a n) -> p a n", n=512), ps[:, 0:2, :], AF.Exp, ["ps0", "ps1"], ["E1", "es_s"], accum_out=es_s[:, 0:1])
                        S.op("dve", lambda e: e.reciprocal(out=es_s[:, 1:2], in_=es_s[:, 0:1]), ["es_s"], ["es_s"])
                        if hh == 0:
                            ts("dve", imp_s[:, 0:1024], E1, es_s[:, 1:2], None, ALU.mult, None, ["E1", "es_s"], ["imp_s"])
                        else:
                            stt(imp_s[:, 0:1024], E1, es_s[:, 1:2], imp_s[:, 0:1024], ALU.mult, ALU.add, ["E1", "es_s", "imp_s"], ["imp_s"])
                    A = imp_s[:, 0:1024].rearrange("p (j t) -> p j t", t=4)
                    B = imp_s[:, 4:1028].rearrange("p (j t) -> p j t", t=4)
                    tt("dve", score_s, A[:, :, 1], A[:, :, 2], ALU.add, ["imp_s"], ["score_s"])
                    tt("dve", score_s, score_s, A[:, :, 3], ALU.add, ["imp_s", "score_s"], ["score_s"])
                    stt(score_s, score_s, 2.0, A[:, :, 0], ALU.mult, ALU.add, ["imp_s", "score_s"], ["score_s"])
                    tt("dve", score_s, score_s, B[:, :, 0], ALU.add, ["imp_s", "score_s"], ["score_s"])
                    memset("dve", score_s[:, 0:1], -1.0, ["score_s"])
                    memset("dve", score_s[:, 255:256], -1.0, ["score_s"])
                    S.op("dve", lambda e: e.max(out=mx_s[:, 0:8], in_=score_s), ["score_s"], ["mx_s"])
                    S.op("dve", lambda e: e.max_index(out=ix_s[:, 0:8], in_max=mx_s[:, 0:8], in_values=score_s), ["score_s", "mx_s"], ["ix_s"])
                    S.op("dve", lambda e: e.match_replace(out=score2_s, in_to_replace=mx_s[:, 0:8], in_values=score_s, imm_value=-1e9), ["score_s", "mx_s"], ["score2_s"])
                    S.op("dve", lambda e: e.max(out=mx_s[:, 8:16], in_=score2_s), ["score2_s"], ["mx_s"])
                    S.op("dve", lambda e: e.max_index(out=ix_s[:, 8:16], in_max=mx_s[:, 8:16], in_values=score2_s), ["score2_s", "mx_s"], ["ix_s"])
                    cp("dve", jl4[:, g, 0:13], ix_s[:, 0:13], ["ix_s"], ["jl4"])
                    memset("dve", jl4[:, g, 13:14], 0.0, ["jl4"])
                    memset("dve", jl4[:, g, 14:15], 255.0, ["jl4"])
                    memset("dve", jl4[:, g, 15:16], 0.0, ["jl4"])
                mm(ps[:, 7, 0:64], rowsel[:, b, :], jl4.rearrange("p g t -> p (g t)"), True, True, ["rowsel", "jl4"], ["ps7"])
                cp("dve", jb, ps[:, 7, 0:64], ["ps7"], ["jb"])
                cp("dve", ji, jb, ["jb"], ["ji"])
                S.op("dve", lambda e: e.tensor_scalar(out=pgi, in0=ji, scalar1=1, scalar2=None, op0=ALU.arith_shift_right), ["ji"], ["pgi"])
                cp("dve", pgf_, pgi, ["pgi"], ["pgf"])
                stt(parf, pgf_, -2.0, jb, ALU.mult, ALU.add, ["pgf", "jb"], ["parf"])
                for g in range(4):
                    tt("dve", ohs, iota_f.unsqueeze(1).to_broadcast([128, 16, 128]),
                       pgf_[:, g * 16:(g + 1) * 16].unsqueeze(2).to_broadcast([128, 16, 128]), ALU.is_equal, ["iota_f", "pgf"], ["wslot2"])
                    tt("dve", ohs, ohs, PTf.unsqueeze(1).to_broadcast([128, 16, 128]), ALU.mult, ["wslot2", "PTf"], ["wslot2"])
                    S.op("dve", lambda e, g=g: e.tensor_reduce(out=phys[:, g, :], in_=ohs, axis=mybir.AxisListType.X, op=ALU.add), ["wslot2"], ["phys"])
                ts("dve", rb.rearrange("p g t -> p (g t)"), phys.rearrange("p g t -> p (g t)"), 128.0, None, ALU.mult, None, ["phys"], ["rb"])
                stt(rb.rearrange("p g t -> p (g t)"), parf, 64.0, rb.rearrange("p g t -> p (g t)"), ALU.mult, ALU.add, ["parf", "rb"], ["rb"])
                rbp = rb.rearrange("p g (pr two) -> p g pr two", two=2)
                ts("dve", idxs_f, rbp[:, :, :, 0], pcol[:, 0:1], pcol[:, 2:3], ALU.mult, ALU.add, ["rb", "pcol"], ["idxs_f"])
                stt(idxs_f, rbp[:, :, :, 1], pcol[:, 1:2], idxs_f, ALU.mult, ALU.add, ["rb", "pcol", "idxs_f"], ["idxs_f"])
                cp("dve", idxs_i, idxs_f, ["idxs_f"], ["idxs_i"])
                chk("s_idx%d" % b)
                for g in range(4):
                    for pr in range(8):
                        slot, sk = ring_slot()
                        S.dma("pool", lambda e, slot=slot, g=g, pr=pr: e.indirect_dma_start(out=slot, out_offset=None, in_=CS,
                              in_offset=bass.IndirectOffsetOnAxis(ap=idxs_i[:, g, pr:pr + 1], axis=0)), "g_" + sk, ["idxs_i", "CS"], [sk])
                        s5 = slot.rearrange("p (e g d) -> p e g d", e=2, g=4)
                        cp("dve", kvb[:, 1024:1152].rearrange("p (e d) -> p e d", e=2), s5[:, :, g, :], [sk], ["kvb3"])
                        tr(Pb(5)[0:64, 512 + (pr % 4) * 128: 512 + (pr % 4 + 1) * 128], kvb[:, 1024:1088], ["kvb3"], ["ps5"])
                        acopy(selKT_s[0:64, g, pr * 128:(pr + 1) * 128], Pb(5)[0:64, 512 + (pr % 4) * 128: 512 + (pr % 4 + 1) * 128], ["ps5"], ["selKT_s"])
                        cp("act", selV_s[:, pr, g, 0:64], kvb[:, 1088:1152], ["kvb3", "sv"], ["selV_s"])
                    memset("dve", selV_s[64:128, 7, g, :], 0.0, ["selV_s"])
                    rhs_q = qTt[0:64, 4 * g:4 * g + 4, :]
                    rk = ["qT0"]
                    kt = [(cmpKT_s[0:64, g, 96 * t:96 * t + (96 if t < 10 else 64)], ["cmpKT_s"], cmpV_s[0:(96 if t < 10 else 64), t, g, :], ["cmpV_s", "sv"],
                           (96 if t < 10 else 64)) for t in range(11)]
                    attend(4, kt, rhs_q, rk)
                    kt = [(selKT_s[0:64, g, pr * 128:(pr + 1) * 128], ["selKT_s"], selV_s[:, pr, g, :], ["selV_s", "sv"], 128) for pr in range(8)]
                    kt.append((KTnew[0:64, 0, g, :], ["KTnew"], Vnew[:, 0, g, :], ["Vnew"], 128))
                    attend(5, kt, rhs_q, rk)
                    kt = [(winKT_s[0:64, g, t * 128:(t + 1) * 128], ["winKT_s"], winV_s[:, t, g, :], ["winV_s", "sv"], 128) for t in range(4)]
                    kt.append((KTnew[0:64, 1, g, :], ["KTnew"], Vnew[:, 1, g, :], ["Vnew"], 128))
                    attend(6, kt, rhs_q, rk)
                    flush_s()
                    for br in range(3):
                        cp("dve", den[:, br, :], ps[:, 4 + br, 0:260].rearrange("p (hh d) -> p hh d", d=65)[:, :, 64], ["ps%d" % (4 + br)], ["den"])
                    ts("dve", den[:], den[:], 1e-30, None, ALU.max, None, ["den"], ["den"])
                    S.op("dve", lambda e: e.reciprocal(out=den[:], in_=den[:]), ["den"], ["den"])
                    gv = gate[:, 0, 12 * g:12 * g + 12].rearrange("p (hh br) -> p br hh", br=3)
                    tt("dve", wgt[:], den[:], gv, ALU.mult, ["den", "gate0"], ["wgt"])
                    ts("dve", wgt[:], wgt[:], pcol[:, 3 + b:4 + b], None, ALU.mult, None, ["wgt", "pcol"], ["wgt"])
                    for br in range(3):
                        tt("dve", ocomb[:, br], ps[:, 4 + br, 0:260].rearrange("p (hh d) -> p hh d", d=65)[:, :, 0:64],
                           wgt[:, br, :].unsqueeze(2).to_broadcast([128, 4, 64]), ALU.mult, ["ps%d" % (4 + br), "wgt"], ["ocomb%d" % br])
                    tt("pool", ocomb[:, 0], ocomb[:, 0], ocomb[:, 1], ALU.add, ["ocomb0", "ocomb1"], ["ocomb0"])
                    tt("pool", ocomb[:, 0], ocomb[:, 0], ocomb[:, 2], ALU.add, ["ocomb0", "ocomb2"], ["ocomb0"])
                    oa = o_acc[:, 256 * g:256 * (g + 1)].rearrange("p (hh d) -> p hh d", d=64)
                    tt("pool", oa, oa, ocomb[:, 0], ALU.add, ["ocomb0", "kvf0", "kvf1"], ["kvf0", "kvf1"])
                chk("s_att%d" % b)
            S.alias(["cxs%d_%d" % (a_, q) for a_ in range(2) for q in range(4)], ATT_KEYS)
            cp("pool", o_bf, o_acc, ["kvf0", "kvf1"], ["o_bf"])
            for kc in range(8):
                tr(Pb(7)[:, kc * 128:(kc + 1) * 128], o_bf[:, kc * 128:(kc + 1) * 128], ["o_bf"], ["ps7"])
            acopy(oT[:, :, 0:128], Pb(7).rearrange("p (k t) -> p k t", t=128), ["ps7"], ["oT0"])
            for j in range(2):
                W, wk = wload(CH_O(j))
                Wv = W[:, :].rearrange("p (kc n) -> p kc n", n=512)
                for kc in range(8):
                    mm(P(j), oT[:, kc, 0:128], Wv[:, kc, :], kc == 0, kc == 7, [wk, "oT0"], ["ps%d" % j])
                tt("dve", h[:, 0, j * 512:(j + 1) * 512], P(j), h[:, 0, j * 512:(j + 1) * 512], ALU.add, ["ps%d" % j, "h0"], ["h0"])
            mlp(1, 1)
            rms(h[:, 0, :], ["h0"], 0)
            stt(ysb, h[:, 0, :], rstd[:, 0:1], gfbc[:], ALU.mult, ALU.mult, ["h0", "rstd0", "gfbc"], ["kvf0", "kvf1"])
            stq(DS["ys"], kvf[0:4, 0:1024], "st_s", ["kvf0", "kvf1"], ["out_s"])

        try:
            for sq_ in range(PSEQ):
                main_loop(sq_)
            if do_sample:
                for ss_ in range(PSEQ):
                    sample_phase(ss_)
        except _Stop:
            pass
        S.final_waits("pool", ["out_y", "out_cmp_p", "out_sel_p", "out_win_p", "out_pool_p", "out_s"])
        print("ops:", S.nops, {e: len(v) for e, v in S.ops.items()})
        S.emit(blk)
    return nc


_CACHE = {}


def kernel(**inputs):
    return run(inputs, gather="direct")


def run(inputs, n_tiles=NTILE, do_sample=True, stop=None, ncores=NC_USED, pool_pages=N_POOLPG, gather="allgather", trace=False):
    n = NC_USED
    consts = make_consts()
    ck = (n_tiles, do_sample, stop, pool_pages, gather)
    if ck not in _CACHE:
        _CACHE[ck] = build_program(n_tiles, do_sample, stop, pool_pages, gather)
    nc = _CACHE[ck]
    f = lambda a: np.ascontiguousarray(a)
    shared = {
        "norm_mix": f(inputs["norm_mix"]), "norm_ffn": f(inputs["norm_ffn"]), "pool_w": f(inputs["pool_w"][0]),
        "pool_scale": f(inputs["pool_scale"]).reshape(1, 1024), "w_qg": f(inputs["w_qg"][0]), "w_o": f(inputs["w_o"][0]),
        "norm_kv": f(inputs["norm_kv"]).reshape(1, 1024), "w_kv": f(inputs["w_kv"]), "cmp_pe": f(inputs["cmp_pe"]),
        "cmp_w1": f(inputs["cmp_w1"]), "cmp_w2": f(inputs["cmp_w2"]), "mlp_up": f(inputs["mlp_up"]),
        "mlp_down": f(inputs["mlp_down"]), "norm_final": f(inputs["norm_final"]).reshape(1, 1024),
    }
    if do_sample and "ccmp_list" not in inputs:
        ccmp = f(inputs["cache_cmp_kv"]).reshape(pool_pages * 128, 512)
        csel = f(inputs["cache_sel_kv"]).reshape(pool_pages * 128, 512)
        if gather == "direct":
            shared["ccmp"] = ccmp
            shared["csel"] = csel
    shared.update(consts)
    in_maps = []
    for c in range(n):
        m = dict(shared)
        ns_ = 4 * PSEQ
        m["x"] = f(inputs["x_prompt"][PSEQ * c:PSEQ * (c + 1)]).reshape(PSEQ * 4096, 1024)
        m["xs"] = f(inputs["x_sample"][ns_ * c:ns_ * (c + 1), 0, :])
        m["spool"] = f(inputs["state_pool"][0, ns_ * c:ns_ * (c + 1)]).reshape(15 * ns_, 1024)
        m["swin"] = f(inputs["state_win_kv"][ns_ * c:ns_ * (c + 1)]).reshape(512 * ns_, 512)
        m["ptab"] = f(inputs["page_table"][ns_ * c:ns_ * (c + 1)]).reshape(1, 128 * ns_).astype(np.int32)
        if do_sample and "ccmp_list" in inputs:
            m["ccmp"] = inputs["ccmp_list"][c]
            m["csel"] = inputs["csel_list"][c]
            m["ptab"] = inputs["ptab_list"][c]
        elif do_sample and gather != "direct":
            rs_ = pool_pages * 128 // 8
            m["ccmp"] = ccmp[c * rs_:(c + 1) * rs_]
            m["csel"] = csel[c * rs_:(c + 1) * rs_]
        in_maps.append(m)
    if trace:
        res = run_bass_kernel_spmd(nc, in_maps[:ncores], core_ids=list(range(ncores)), trace=True)
        print("EXEC_TIME_NS", getattr(res, "exec_time_ns", None), flush=True)
    else:
        res = run_bass_kernel_spmd(nc, in_maps[:ncores], core_ids=list(range(ncores)))
    R = list(res.results)
    while len(R) < n:
        R.append(R[0])
    cat = lambda k: np.stack([R[c][k] for c in range(n)], 0)
    y_prompt = cat("y").reshape(8, 4096, 1024)
    y_sample = cat("ys").reshape(32, 1, 1024)
    cmp_p = cat("cmp_p").reshape(8, 4096, 2, 4, 64)
    sel_p = cat("sel_p").reshape(8, 4096, 2, 4, 64)
    win_p = cat("win_p").reshape(8, 512, 2, 4, 64)
    pool_p = cat("pool_p").reshape(1, 8, 15, 1024)
    cmp_s = cat("cmp_s").reshape(32, 1, 2, 4, 64)
    sel_s = cat("sel_s").reshape(32, 1, 2, 4, 64)
    win_s = cat("win_s").reshape(32, 512, 2, 4, 64)
    pool_s = cat("pool_s").reshape(1, 32, 15, 1024)
    return (y_prompt, y_sample, cmp_p, sel_p, win_p, pool_p, cmp_s, sel_s, win_s, pool_s)
```

```python
import numpy as np
import ml_dtypes
from contextlib import ExitStack
import concourse.bass as bass
import concourse.mybir as mybir
from concourse.bass_utils import run_bass_kernel_spmd

F32 = mybir.dt.float32
BF16 = mybir.dt.bfloat16
I32 = mybir.dt.int32
U32 = mybir.dt.uint32
AF = mybir.ActivationFunctionType
ALU = mybir.AluOpType

NEG = -30000.0
T_SEQ = 4096
NSUB = 32
NTILE = 8
PAST = 16384
N_POOLPG = 5120
NC_USED = 8
PSEQ = 8 // NC_USED


class Sched:
    ENGS = ("pe", "dve", "act", "pool", "sp")

    def __init__(self, nc, stack, n_dma_sems=100):
        self.nc = nc
        self.sem = {e: stack.enter_context(nc.semaphore("s_" + e)) for e in self.ENGS}
        self.cnt = {e: 0 for e in self.ENGS}
        self.stack = stack
        self.dma_sem = {}
        self.dma_cnt = {}
        self.ops = {e: [] for e in self.ENGS}
        self.waited = {}
        self.last_w = {}
        self.readers = {}
        self.nops = 0

    def _dma_sem(self, key):
        if key not in self.dma_sem:
            self.dma_sem[key] = self.stack.enter_context(self.nc.semaphore("d%d" % len(self.dma_sem)))
            self.dma_cnt[key] = 0
        return self.dma_sem[key]

    def _deps(self, e, reads, writes):
        deps = {}

        def add(s, n):
            if deps.get(s, 0) < n:
                deps[s] = n
        for r in reads:
            d = self.last_w.get(r)
            if d is not None:
                add(*d)
        for w in writes:
            d = self.last_w.get(w)
            if d is not None:
                add(*d)
            for s, n in self.readers.get(w, {}).items():
                add(s, n)
        waits = []
        for s, n in deps.items():
            if e == "pe" and s == ("e", "pe"):
                continue
            k = (e, s)
            if self.waited.get(k, 0) < n:
                self.waited[k] = n
                waits.append((s, n))
        return waits

    def _semobj(self, s):
        return self.sem[s[1]] if s[0] == "e" else self.dma_sem[s[1]]

    def _record(self, me, reads, writes):
        s, n = me
        for r in reads:
            d = self.readers.setdefault(r, {})
            if d.get(s, 0) < n:
                d[s] = n
        for w in writes:
            self.last_w[w] = me
            self.readers[w] = {}

    def op(self, e, fn, reads=(), writes=()):
        waits = self._deps(e, reads, writes)
        self.cnt[e] += 1
        self._record((("e", e), self.cnt[e]), reads, writes)
        self.ops[e].append((waits, fn, self.sem[e], 1))
        self.nops += 1

    def dma(self, e, fn, key, reads=(), writes=()):
        sem = self._dma_sem(key)
        waits = self._deps(e, reads, writes)
        self.dma_cnt[key] += 16
        self._record((("d", key), self.dma_cnt[key]), reads, writes)
        self.ops[e].append((waits, fn, sem, 16))
        self.nops += 1

    def alias(self, old, new):
        deps = {}
        for k in old:
            d = self.last_w.get(k)
            if d is not None and deps.get(d[0], 0) < d[1]:
                deps[d[0]] = d[1]
            for s, n in self.readers.get(k, {}).items():
                if deps.get(s, 0) < n:
                    deps[s] = n
        for k in new:
            r = self.readers.setdefault(k, {})
            for s, n in deps.items():
                if r.get(s, 0) < n:
                    r[s] = n

    def barrier(self):
        targets = [(("e", e2), self.cnt[e2]) for e2 in self.ENGS if self.cnt[e2] > 0]
        targets += [(("d", k), n) for k, n in self.dma_cnt.items() if n > 0]
        for e in self.ENGS:
            waits = []
            for s_, n in targets:
                if e == "pe" and s_ == ("e", "pe"):
                    continue
                k = (e, s_)
                if self.waited.get(k, 0) < n:
                    self.waited[k] = n
                    waits.append((s_, n))
            self.ops[e].append((waits, None, None, 0))

    def final_waits(self, e, keys):
        waits = self._deps(e, keys, keys)
        self.ops[e].append((waits, None, None, 0))

    def emit(self, block):
        sched = self

        def mk(e):
            def body(engine):
                for waits, fn, sem, inc in sched.ops[e]:
                    for s, n in waits:
                        engine.wait_ge(sched._semobj(s), n)
                    if fn is not None:
                        fn(engine).then_inc(sem, inc)
            return body
        block.tensor(mk("pe"))
        block.vector(mk("dve"))
        block.scalar(mk("act"))
        block.gpsimd(mk("pool"))
        block.sync(mk("sp"))


POOL_WINDOWS = (2, 4, 8, 16)


def make_consts():
    bf = ml_dtypes.bfloat16
    c = {}
    c["c_ident"] = np.eye(128, dtype=np.float32).astype(bf)
    c["c_ident4"] = np.tile(np.eye(128, dtype=np.float32), (1, 4)).astype(bf)
    band = np.zeros((128, 12, 128), np.float32)
    i = np.arange(128)[:, None]
    j = np.arange(128)[None, :]
    for g, w in enumerate(POOL_WINDOWS):
        band[:, g * 3 + 0, :] = ((i <= j) & (i > j - w)) / w - (i == j)
        band[:, g * 3 + 1, :] = ((i - 128) > (j - w)) / w
        band[:, g * 3 + 2, :] = ((i <= j) & (i > j - w)) / np.minimum(j + 1, w) - (i == j)
    c["c_band"] = band.reshape(128, 12 * 128).astype(bf)
    tri = np.where(i <= j, 0.0, NEG).astype(np.float32)
    tri2 = np.where(i >= j, 0.0, NEG).astype(np.float32)
    c["c_tri"] = np.tile(tri, (1, 4)).astype(bf)
    c["c_tri2"] = np.tile(tri2, (1, 4)).astype(bf)
    key = np.arange(4096)[None, :]
    b = np.arange(64)[:, None]
    ind = np.zeros((128, 4096), np.float32)
    ind[64:128] = (key // 64 == b)
    c["c_ind"] = ind.astype(bf)
    ql = np.arange(128)[:, None]
    jj = np.arange(8)[None, :]
    c["c_cmq"] = np.where(ql >= 16 * jj + 15, 0.0, NEG).astype(np.float32)
    mpp = np.arange(504)[None, :] - 248
    c["c_cmpat"] = np.where(16 * mpp + 15 <= ql, 0.0, NEG).astype(np.float32).astype(bf)
    jp = np.arange(128)[None, :] - 64
    cc = (ql >= 64).astype(np.int64)
    G = np.where((jp == cc) | (jp == cc - 1), 100.0, np.where(jp > cc, -1000.0, 0.0))
    c["c_g"] = G.astype(np.float32)
    inv = 500000.0 ** (-np.arange(0, 16, 2, dtype=np.float32) / 16)
    pos = (np.arange(32)[None, :] * 128 + np.arange(128)[:, None]).astype(np.float32)
    ang = pos[:, :, None] * inv[None, None, :]
    c["c_cos"] = np.cos(ang).astype(np.float32).reshape(128, 256)
    c["c_sin"] = np.sin(ang).astype(np.float32).reshape(128, 256)
    angs = np.float32(PAST) * inv
    c["c_coss"] = np.tile(np.cos(angs).astype(np.float32)[None, :], (128, 1))
    c["c_sins"] = np.tile(np.sin(angs).astype(np.float32)[None, :], (128, 1))
    sband = np.zeros((128, 4, 4), np.float32)
    for b_ in range(4):
        for r_ in range(16):
            for g, w in enumerate(POOL_WINDOWS):
                sband[16 * b_ + r_, g, b_] = (1.0 / w if r_ >= 16 - w else 0.0) - (1.0 if r_ == 15 else 0.0)
    c["c_sband"] = sband.reshape(128, 16).astype(bf)
    rowsel = np.zeros((128, 4, 128), np.float32)
    for b_ in range(4):
        rowsel[b_, b_, :] = 1.0
    c["c_rowsel"] = rowsel.reshape(128, 512).astype(bf)
    pcol = np.zeros((128, 8), np.float32)
    pp = np.arange(128)
    pcol[:, 0] = pp < 64
    pcol[:, 1] = pp >= 64
    pcol[:, 2] = pp % 64
    for b_ in range(4):
        pcol[:, 3 + b_] = pp == b_
    pcol[:, 7] = pp
    c["c_pcol"] = pcol
    c["c_iota"] = np.tile(np.arange(128, dtype=np.float32)[None, :], (128, 1))
    return c


CONST_SPECS = {
    "c_sband": ([128, 16], BF16), "c_rowsel": ([128, 512], BF16), "c_pcol": ([128, 8], F32), "c_iota": ([128, 128], F32),
    "c_ident": ([128, 128], BF16), "c_ident4": ([128, 512], BF16), "c_band": ([128, 1536], BF16),
    "c_tri": ([128, 512], BF16), "c_tri2": ([128, 512], BF16), "c_ind": ([128, 4096], BF16),
    "c_cmq": ([128, 8], F32), "c_cmpat": ([128, 504], BF16), "c_g": ([128, 128], F32),
    "c_cos": ([128, 256], F32), "c_sin": ([128, 256], F32), "c_coss": ([128, 8], F32), "c_sins": ([128, 8], F32),
}

IN_SPECS = {
    "x": ([4096 * PSEQ, 1024], F32), "xs": ([4 * PSEQ, 1024], F32), "spool": ([60 * PSEQ, 1024], F32),
    "ccmp": ([N_POOLPG * 128, 512], F32), "csel": ([N_POOLPG * 128, 512], F32),
    "swin": ([2048 * PSEQ, 512], F32), "ptab": ([1, 512 * PSEQ], I32),
    "norm_mix": ([2, 1024], F32), "norm_ffn": ([2, 1024], F32), "pool_w": ([4, 256, 256], F32),
    "pool_scale": ([1, 1024], F32), "w_qg": ([1024, 1072], F32), "w_o": ([1024, 1024], F32),
    "norm_kv": ([1, 1024], F32), "w_kv": ([1024, 1536], F32), "cmp_pe": ([32, 2, 64], F32),
    "cmp_w1": ([2, 32, 64, 128], F32), "cmp_w2": ([2, 128, 64], F32),
    "mlp_up": ([2, 1024, 4096], F32), "mlp_down": ([2, 4096, 1024], F32), "norm_final": ([1, 1024], F32),
}
OUT_SPECS = {
    "y": [4096 * PSEQ, 1024], "ys": [4 * PSEQ, 1024], "cmp_p": [4096 * PSEQ, 512], "sel_p": [4096 * PSEQ, 512], "win_p": [512 * PSEQ, 512],
    "pool_p": [15 * PSEQ, 1024], "cmp_s": [4 * PSEQ, 512], "sel_s": [4 * PSEQ, 512], "win_s": [2048 * PSEQ, 512], "pool_s": [60 * PSEQ, 1024],
}

CH_UP = lambda l, fg: l * 8 + fg
CH_DN = lambda l, fg: 16 + l * 8 + fg
CH_KV = lambda j: 32 + j
CH_QG = lambda j: 35 + j
CH_O = lambda j: 38 + j
CH_C1 = lambda e: 40 + e
N_CH = 42


def build_program(n_tiles=NTILE, do_sample=True, stop=None, pool_pages=N_POOLPG, gather="allgather"):
    nc = bass.Bass("TRN2", target_bir_lowering=False)
    D = {}
    for k, (shp, dt) in IN_SPECS.items():
        if k in ("ccmp", "csel"):
            if not do_sample:
                continue
            if gather == "allgather":
                shp = [pool_pages * 128 // 8, 512]
            else:
                shp = [pool_pages * 128, 512]
        D[k] = nc.dram_tensor(k, shp, dt, kind="ExternalInput").ap()
    for k, (shp, dt) in CONST_SPECS.items():
        D[k] = nc.dram_tensor(k, shp, dt, kind="ExternalInput").ap()
    for k, shp in OUT_SPECS.items():
        D[k] = nc.dram_tensor(k, shp, F32, kind="ExternalOutput").ap()
    wscr = nc.dram_tensor("wscr", [N_CH, 128, 4096], BF16, kind="Internal").ap()
    if do_sample and gather == "allgather":
        cc_in = nc.dram_tensor("cc_in", [pool_pages * 128 // 8, 512], F32, kind="Internal").ap()
        cs_in = nc.dram_tensor("cs_in", [pool_pages * 128 // 8, 512], F32, kind="Internal").ap()
        CC = nc.dram_tensor("cc_full", [pool_pages * 128, 512], F32, kind="Internal").ap()
        CS = nc.dram_tensor("cs_full", [pool_pages * 128, 512], F32, kind="Internal").ap()
    elif do_sample:
        CC, CS = D["ccmp"], D["csel"]

    with ExitStack() as st:
        S = Sched(nc, st)
        total = [0]

        def sb(name, shape, dt):
            n = 1
            for s_ in shape[1:]:
                n *= s_
            total[0] += n * (2 if dt == BF16 else 4)
            return st.enter_context(nc.sbuf_tensor(name, shape, dt))

        selKT = sb("selKT", [128, 4, 4096], BF16)
        selV = sb("selV", [128, 32, 4, 65], BF16)
        winKT = sb("winKT", [128, 4, 1024], BF16)
        winV = sb("winV", [128, 8, 4, 65], BF16)
        cmpKT = sb("cmpKT", [128, 4, 288], BF16)
        cmpV = sb("cmpV", [128, 3, 4, 65], BF16)
        ident = sb("ident", [128, 128], BF16)
        ident4 = sb("ident4", [128, 512], BF16)
        band = sb("band", [128, 12, 128], BF16)
        tri = sb("tri", [128, 512], BF16)
        tri2 = sb("tri2", [128, 512], BF16)
        cmq = sb("cmq", [128, 8], F32)
        cmpat = sb("cmpat", [128, 504], BF16)
        gpat = sb("gpat", [128, 128], F32)
        cos_t = sb("cos_t", [128, 32, 8], F32)
        sin_t = sb("sin_t", [128, 32, 8], F32)
        g0bc = sb("g0bc", [128, 1024], F32)
        gfbc = sb("gfbc", [128, 1024], F32)
        poolW = sb("poolW", [128, 4, 2, 256], BF16)
        w2bf = sb("w2bf", [128, 2, 64], BF16)
        petot = sb("petot", [128, 2], F32)
        gcol = sb("gcol", [128, 5, 8], F32)
        wslot = [sb("wslot%d" % i, [128, 4096], BF16) for i in range(3)]
        h = sb("h", [128, 4, 1024], F32)
        nbf = [sb("nbf%d" % i, [128, 1024], BF16) for i in range(2)]
        ubf = [sb("ubf%d" % i, [128, 1024], BF16) for i in range(2)]
        nT = sb("nT", [128, 8, 512], BF16)
        big = sb("big", [128, 16384], BF16)
        junk = sb("junk", [128, 1024], BF16)
        ssq = sb("ssq", [128, 8], F32)
        rstd = sb("rstd", [128, 8], F32)
        relu_t = [sb("relu%d" % i, [128, 512], F32) for i in range(2)]
        kvf = sb("kvf", [128, 1536], F32)
        kvb = sb("kvb", [128, 1536], BF16)
        rt = sb("rt", [128, 4, 16, 8], F32)
        parts = sb("parts", [128, 2, 2, 4, 32], F32)
        carry = sb("carry", [128, 2, 4], F32)
        hid = sb("hid", [128, 2, 4, 32], F32)
        hid2 = sb("hid2", [128, 2, 4, 32], F32)
        hidb = sb("hidb", [128, 2, 4, 32], BF16)
        Ebuf = sb("Ebuf", [128, 4, 264], F32)
        esum = sb("esum", [128, 8], F32)
        imp = sb("imp", [128, 264], F32)
        score = sb("score", [128, 64], F32)
        score2 = sb("score2", [128, 64], F32)
        mx = sb("mx", [128, 16], F32)
        mbfull = sb("mbfull", [128, 128], BF16)
        gate = sb("gate", [128, 4, 48], F32)
        den = sb("den", [128, 3, 4], F32)
        wgt = sb("wgt", [128, 3, 4], F32)
        ocomb = sb("ocomb", [128, 3, 4, 64], F32)
        rowsel = sb("rowsel", [128, 4, 128], BF16)
        pcol = sb("pcol", [128, 8], F32)
        sband = sb("sband", [128, 4, 4], BF16)
        coss = sb("coss", [128, 8], F32)
        sins = sb("sins", [128, 8], F32)
        ps = st.enter_context(nc.psum_tensor("ps", [128, 8, 512], F32))

        aT = big[:, :].rearrange("p (f t) -> p f t", t=512)
        cmpXT = big[:, 0:4096].rearrange("p (e g t) -> p e g t", e=2, g=4)
        CX_KEYS = ["cmpXT%d" % s_ for s_ in range(4)]
        ysb = kvf[:, 0:1024]
        q_bf = big[:, 0:4096].rearrange("p (s c) -> p s c", c=1024)
        qT = [big[:, 4096 + i * 2048: 4096 + (i + 1) * 2048].rearrange("p (hh t) -> p hh t", t=128) for i in range(2)]
        oT = big[:, 8192:12288].rearrange("p (k t) -> p k t", t=512)
        o_bf = big[:, 12288:13312]
        PT = [big[:, 13312 + i * 512: 13312 + (i + 1) * 512] for i in range(2)] + [big[:, 15360:15872]]
        SBANK = [2, 3, 7]
        stage = [big[:, i * 8192:(i + 1) * 8192].bitcast(F32) for i in range(2)]
        AT_KEYS = ["aT%d" % f for f in range(32)]
        ATT_KEYS = ["q_bf%d" % s for s in range(4)] + ["qT0", "qT1", "qTm0", "qTm1", "o_bf", "PT0", "PT1", "PT2"] + ["oT%d" % s for s in range(4)]
        STG_KEYS = ["stage0", "stage1"]
        print("SBUF bytes/partition:", total[0])

        blk = st.enter_context(nc.Block())

        def P(b):
            return ps[:, b, :]

        def Pb(b):
            return ps[:, b, :].bitcast(BF16)

        def act(out, in_, func, r, w, **kw):
            S.op("act", lambda e: e.activation(out=out, in_=in_, func=func, **kw), r, w)

        def acopy(out, in_, r, w):
            S.op("act", lambda e: e.copy(out=out, in_=in_), r, w)

        def cp(eng, out, in_, r, w):
            if eng == "act":
                return acopy(out, in_, r, w)
            S.op(eng, lambda e: e.tensor_copy(out=out, in_=in_), r, w)

        def tt(eng, out, in0, in1, op, r, w):
            S.op(eng, lambda e: e.tensor_tensor(out=out, in0=in0, in1=in1, op=op), r, w)

        def ts(eng, out, in0, s1, s2, op0, op1, r, w):
            if op1 is None:
                S.op(eng, lambda e: e.tensor_scalar(out=out, in0=in0, scalar1=s1, scalar2=None, op0=op0), r, w)
            else:
                S.op(eng, lambda e: e.tensor_scalar(out=out, in0=in0, scalar1=s1, scalar2=s2, op0=op0, op1=op1), r, w)

        def stt(out, in0, scalar, in1, op0, op1, r, w):
            S.op("dve", lambda e: e.scalar_tensor_tensor(out=out, in0=in0, scalar=scalar, in1=in1, op0=op0, op1=op1), r, w)

        def mm(out, lhsT, rhs, start, stop, r, w):
            S.op("pe", lambda e: e.matmul(out, lhsT=lhsT, rhs=rhs, start=start, stop=stop, skip_group_check=True), r, w)

        def tr(out, in_, r, w, idn=None):
            idn_ = ident[:] if idn is None else idn
            S.op("pe", lambda e: e.transpose(out=out, in_=in_, identity=idn_), list(r) + ["ident"], w)

        def memset(eng, ap, val, w):
            S.op(eng, lambda e: e.memset(ap, val), (), w)

        def ld(out, in_, key, w, r=(), q="sp", **kw):
            S.dma(q, lambda e: e.dma_start(out=out, in_=in_, **kw), key, r, w)

        def stq(out, in_, key, r, w, **kw):
            S.dma("pool", lambda e: e.dma_start(out=out, in_=in_, **kw), key, r, w)

        ld(ident[:], D["c_ident"], "c0", ["ident"])
        ld(ident4[:], D["c_ident4"], "c0", ["ident4"])
        ld(band[:].rearrange("p a b -> p (a b)"), D["c_band"], "c0", ["band"])
        ld(tri[:], D["c_tri"], "c0", ["tri"])
        ld(tri2[:], D["c_tri2"], "c0", ["tri2"])
        ld(cmq[:], D["c_cmq"], "c0", ["cmq"])
        ld(cmpat[:], D["c_cmpat"], "c0", ["cmpat"])
        ld(gpat[:], D["c_g"], "c0", ["gpat"])
        ld(cos_t[:].rearrange("p a b -> p (a b)"), D["c_cos"], "c0", ["cos"])
        ld(sin_t[:].rearrange("p a b -> p (a b)"), D["c_sin"], "c0", ["sin"])
        ld(rowsel[:].rearrange("p a b -> p (a b)"), D["c_rowsel"], "c0", ["rowsel"])
        ld(pcol[:], D["c_pcol"], "c0", ["pcol"])
        ld(sband[:].rearrange("p a b -> p (a b)"), D["c_sband"], "c0", ["sband"])
        ld(coss[:], D["c_coss"], "c0", ["coss"])
        ld(sins[:], D["c_sins"], "c0", ["sins"])
        if do_sample and gather == "allgather":
            rows_sh = pool_pages * 128 // 8
            nbig = rows_sh * 512 // 16384
            ld(cc_in.rearrange("(a b) c -> a (b c)", a=nbig), D["ccmp"].rearrange("(a b) c -> a (b c)", a=nbig), "ag_cp0", ["cc_in"])
            ld(cs_in.rearrange("(a b) c -> a (b c)", a=nbig), D["csel"].rearrange("(a b) c -> a (b c)", a=nbig), "ag_cp1", ["cs_in"])
            S.dma("pool", lambda e: e.collective_compute("AllGather", ALU.bypass, [list(range(8))], ins=[cc_in], outs=[CC]), "ag0", ["cc_in"], ["CC"])
            S.dma("pool", lambda e: e.collective_compute("AllGather", ALU.bypass, [list(range(8))], ins=[cs_in], outs=[CS]), "ag1", ["cs_in"], ["CS"])
        ld(g0bc[:], D["norm_mix"][0:1, :].partition_broadcast(128), "c0", ["g0bc"])
        ld(gfbc[:], D["norm_final"].partition_broadcast(128), "c0", ["gfbc"])
        for a in range(4):
            ld(selKT[64:128, a, :], D["c_ind"][64:128, :], "c0", ["selKT_ind"])
        ld(gcol[:, 0, :], D["norm_ffn"][0, :].rearrange("(kc p) -> p kc", p=128), "c0", ["gcol"], allow_slow_non_contiguous=True)
        ld(gcol[:, 1, :], D["norm_ffn"][1, :].rearrange("(kc p) -> p kc", p=128), "c0", ["gcol"], allow_slow_non_contiguous=True)
        ld(gcol[:, 2, :], D["norm_kv"][0, :].rearrange("(kc p) -> p kc", p=128), "c0", ["gcol"], allow_slow_non_contiguous=True)
        ld(gcol[:, 3, :], D["norm_mix"][1, :].rearrange("(kc p) -> p kc", p=128), "c0", ["gcol"], allow_slow_non_contiguous=True)
        ts("dve", gcol[:, 4, :], gcol[:, 3, :], 0.125, None, ALU.mult, None, ["gcol"], ["gcol"])
        memset("dve", selV[:, :, :, 64:65], 1.0, ["selV_ones"])
        memset("dve", winV[:, :, :, 64:65], 1.0, ["winV_ones"])
        memset("dve", cmpV[:], 0.0, ["cmpV_init"])
        memset("dve", cmpKT[:], 0.0, ["cmpKT_init"])
        memset("dve", imp[:], 0.0, ["imp"])
        memset("dve", Ebuf[:], 0.0, ["E"])
        memset("dve", carry[:], 0.0, ["carry"])
        memset("dve", mbfull[:], 0.0, ["mbfull"])

        conv_i = [0]

        def convert(chunk, pairs, scale_cols=None, nparts=128, width=4096, inner=512):
            i = conv_i[0] % 2
            conv_i[0] += 1
            sk, wk = "stage%d" % i, "wslot%d" % i
            stg = stage[i]
            for (dst, src) in pairs:
                ld(dst(stg), src, "stg%d" % i, [sk])
            eng = "dve" if (conv_i[0] % 2) else "pool"
            if scale_cols is None:
                cp(eng, wslot[i][0:nparts, 0:width], stg[0:nparts, 0:width], [sk], [wk])
            else:
                nk = width // inner
                for kc in range(nk):
                    ts(eng, wslot[i][0:nparts, kc * inner:(kc + 1) * inner], stg[0:nparts, kc * inner:(kc + 1) * inner],
                       gcol[:, scale_cols, kc:kc + 1], None, ALU.mult, None, [sk, "gcol"], [wk])
            stq(wscr[chunk, 0:nparts, 0:width], wslot[i][0:nparts, 0:width], "wst%d" % i, [wk], ["scr%d" % chunk])

        for l in range(2):
            for fg in range(8):
                convert(CH_UP(l, fg), [(lambda s_: s_[:, :].rearrange("p (kc n) -> p kc n", n=512),
                                        D["mlp_up"][l, :, fg * 512:(fg + 1) * 512].rearrange("(kc p) n -> p kc n", p=128))], scale_cols=l)
            for fg in range(8):
                convert(CH_DN(l, fg), [(lambda s_: s_[:, :].rearrange("p (fc n) -> p fc n", n=1024),
                                        D["mlp_down"][l, fg * 512:(fg + 1) * 512, :].rearrange("(fc p) n -> p fc n", p=128))])
        for j in range(3):
            convert(CH_KV(j), [(lambda s_: s_[:, :].rearrange("p (kc n) -> p kc n", n=512),
                                D["w_kv"][:, j * 512:(j + 1) * 512].rearrange("(kc p) n -> p kc n", p=128))], scale_cols=2)
        for j in range(2):
            convert(CH_QG(j), [(lambda s_: s_[:, :].rearrange("p (kc n) -> p kc n", n=512),
                                D["w_qg"][:, j * 512:(j + 1) * 512].rearrange("(kc p) n -> p kc n", p=128))], scale_cols=4)
        convert(CH_QG(2), [(lambda s_: s_[:, 0:384].rearrange("p (kc n) -> p kc n", n=48),
                            D["w_qg"][:, 1024:1072].rearrange("(kc p) n -> p kc n", p=128))], scale_cols=3, width=384, inner=48)
        for j in range(2):
            convert(CH_O(j), [(lambda s_: s_[:, :].rearrange("p (kc n) -> p kc n", n=512),
                               D["w_o"][:, j * 512:(j + 1) * 512].rearrange("(kc p) n -> p kc n", p=128))])
        for e_ in range(2):
            convert(CH_C1(e_), [(lambda s_: s_[0:64, :].rearrange("p (r k) -> p r k", k=128),
                                 D["cmp_w1"][e_].rearrange("r h k -> h r k"))], nparts=64)
        i_ = conv_i[0] % 2
        conv_i[0] += 1
        stg = stage[i_]
        ld(stg[:, 0:2048].rearrange("p (g kc d) -> p g kc d", g=4, kc=2), D["pool_w"].rearrange("g (kc p) d -> p g kc d", p=128), "stg%d" % i_, ["stage%d" % i_])
        ld(stg[:, 2048:3072], D["pool_scale"].partition_broadcast(128), "stg%d" % i_, ["stage%d" % i_])
        for g in range(4):
            for kc in range(2):
                tt("dve", poolW[:, g, kc, :], stg[:, (g * 2 + kc) * 256:(g * 2 + kc + 1) * 256], stg[:, 2048 + g * 256: 2048 + (g + 1) * 256],
                   ALU.mult, ["stage%d" % i_], ["poolW"])
        i_ = conv_i[0] % 2
        conv_i[0] += 1
        stg = stage[i_]
        ld(stg[:, 0:128].rearrange("p (e h) -> p e h", e=2), D["cmp_w2"].rearrange("e k h -> k e h"), "stg%d" % i_, ["stage%d" % i_])
        cp("dve", w2bf[:].rearrange("p e h -> p (e h)"), stg[:, 0:128], ["stage%d" % i_], ["w2bf"])
        wctr = [0]

        def wload(chunk, nparts=128, width=4096):
            i = wctr[0] % 3
            wctr[0] += 1
            ld(wslot[i][0:nparts, 0:width], wscr[chunk, 0:nparts, 0:width], "wl%d" % i, ["wslot%d" % i], r=["scr%d" % chunk])
            return wslot[i], "wslot%d" % i

        pe_b = sb("pe_b", [128, 32, 2], BF16)
        pe_f = sb("pe_f", [128, 32, 2], F32)
        ld(pe_f[0:64, :, :], D["cmp_pe"].rearrange("r e h -> h r e"), "c0", ["pe_f"], allow_slow_non_contiguous=True)
        cp("dve", pe_b[0:64].rearrange("p r e -> p (r e)"), pe_f[0:64].rearrange("p r e -> p (r e)"), ["pe_f"], ["pe_b"])
        for e_ in range(2):
            W, wk = wload(CH_C1(e_), nparts=64)
            for row in range(32):
                mm(ps[:, 7, e_:e_ + 1], W[0:64, row * 128:(row + 1) * 128], pe_b[0:64, row, e_:e_ + 1], row == 0, row == 31,
                   [wk, "pe_b"], ["ps7"])
        cp("dve", petot[:], ps[:, 7, 0:2], ["ps7"], ["petot"])

        S.alias(STG_KEYS, AT_KEYS + ATT_KEYS)

        def rms(src, src_keys, col):
            act(junk[:], src, AF.Square, src_keys, ["junk", "ssq%d" % col], accum_out=ssq[:, col:col + 1])
            act(rstd[:, col:col + 1], ssq[:, col:col + 1], AF.Sqrt, ["ssq%d" % col], ["rstd%d" % col], scale=1.0 / 1024, bias=1e-6)
            S.op("dve", lambda e: e.reciprocal(out=rstd[:, col:col + 1], in_=rstd[:, col:col + 1]), ["rstd%d" % col], ["rstd%d" % col])

        def norm_T(s, col, nb, ncols_valid=128):
            rms(h[:, s, :], ["h%d" % s], col)
            ts("dve", nbf[nb][:], h[:, s, :], rstd[:, col:col + 1], None, ALU.mult, None, ["h%d" % s, "rstd%d" % col], ["nbf%d" % nb])
            for kc in range(8):
                tr(Pb(4)[:, kc * 128:(kc + 1) * 128], nbf[nb][:, kc * 128:(kc + 1) * 128], ["nbf%d" % nb], ["ps4"])
            acopy(nT[:, :, s * 128:(s + 1) * 128], Pb(4).rearrange("p (k t) -> p k t", t=128), ["ps4"], ["nT%d" % s])

        def mlp(l, nsub):
            S.alias(ATT_KEYS, AT_KEYS)
            ntok = nsub * 128
            for s in range(nsub):
                norm_T(s, s, s % 2)
            for fg in range(8):
                W, wk = wload(CH_UP(l, fg))
                Wv = W[:, :].rearrange("p (kc n) -> p kc n", n=512)
                for fc in range(4):
                    f = fg * 4 + fc
                    b = f % 4
                    for kc in range(8):
                        mm(P(b)[:, 0:ntok], Wv[:, kc, fc * 128:(fc + 1) * 128], nT[:, kc, 0:ntok], kc == 0, kc == 7,
                           [wk] + ["nT%d" % s for s in range(nsub)], ["ps%d" % b])
                    rl = relu_t[f % 2]
                    act(rl[:, 0:ntok], P(b)[:, 0:ntok], AF.Relu, ["ps%d" % b], ["relu%d" % (f % 2)])
                    tt("pool", aT[:, f, 0:ntok], rl[:, 0:ntok], rl[:, 0:ntok], ALU.mult, ["relu%d" % (f % 2)], ["aT%d" % f])
            for fg in range(8):
                W, wk = wload(CH_DN(l, fg))
                Wv = W[:, :].rearrange("p (fc n) -> p fc n", n=1024)
                for fc in range(4):
                    f = fg * 4 + fc
                    for s in range(nsub):
                        for hf in range(2):
                            b = 2 * s + hf
                            mm(P(b), aT[:, f, s * 128:(s + 1) * 128], Wv[:, fc, hf * 512:(hf + 1) * 512], f == 0, f == 31,
                               [wk, "aT%d" % f], ["ps%d" % b])
            for s in range(nsub):
                tt("dve", h[:, s, :].rearrange("p (a n) -> p a n", n=512), ps[:, 2 * s:2 * s + 2, :], h[:, s, :].rearrange("p (a n) -> p a n", n=512),
                   ALU.add, ["ps%d" % (2 * s), "ps%d" % (2 * s + 1), "h%d" % s], ["h%d" % s])

        def rope(x1, x2, cosb, sinb, shape, keys_r, keys_w):
            a, b_ = shape
            t1 = rt[:, 0, 0:a * b_, :].rearrange("p (a b) d -> p a b d", a=a)
            t2 = rt[:, 1, 0:a * b_, :].rearrange("p (a b) d -> p a b d", a=a)
            t3 = rt[:, 2, 0:a * b_, :].rearrange("p (a b) d -> p a b d", a=a)
            t4 = rt[:, 3, 0:a * b_, :].rearrange("p (a b) d -> p a b d", a=a)
            tt("dve", t1, x1, cosb, ALU.mult, keys_r, ["rt0"])
            tt("dve", t2, x2, sinb, ALU.mult, keys_r, ["rt1"])
            tt("dve", t3, x2, cosb, ALU.mult, keys_r, ["rt2"])
            tt("dve", t4, x1, sinb, ALU.mult, keys_r, ["rt3"])
            tt("dve", x1, t1, t2, ALU.subtract, ["rt0", "rt1"], keys_w)
            tt("dve", x2, t3, t4, ALU.add, ["rt2", "rt3"], keys_w)

        class _Stop(Exception):
            pass

        def chk(name):
            if stop == name:
                raise _Stop()

        pti = [0]

        def main_loop(sq=0):
          nb_ctr = [0]
          DV = {k_: D[k_][sq * r_:(sq + 1) * r_] for k_, r_ in (("x", 4096), ("y", 4096), ("cmp_p", 4096), ("sel_p", 4096), ("win_p", 512), ("pool_p", 15))}
          if sq > 0:
              allk = ["cmpKT%d" % t_ for t_ in range(8)] + ["cmpV%d" % t_ for t_ in range(8)]
              memset("dve", cmpV[:], 0.0, ["cmpV_init"] + allk)
              memset("dve", cmpKT[:], 0.0, ["cmpKT_init"] + allk)
              memset("dve", imp[:], 0.0, ["imp"])
              memset("dve", carry[:], 0.0, ["carry"])
          for T in range(n_tiles if stop != "prologue" else 0):
              for s in range(4):
                  i = 4 * T + s
                  ld(h[:, s, :], DV["x"][i * 128:(i + 1) * 128, :], "xld%d" % s, ["h%d" % s])
              for s in range(4):
                  i = 4 * T + s
                  nb = nb_ctr[0] % 2
                  nb_ctr[0] += 1
                  rms(h[:, s, :], ["h%d" % s], s)
                  stt(ubf[nb][:], h[:, s, :], rstd[:, s:s + 1], g0bc[:], ALU.mult, ALU.mult, ["h%d" % s, "rstd%d" % s, "g0bc"], ["ubf%d" % nb])
                  if i == NSUB - 1:
                      stt(ysb, h[:, s, :], rstd[:, s:s + 1], g0bc[:], ALU.mult, ALU.mult, ["h%d" % s, "rstd%d" % s, "g0bc"], ["kvf0", "kvf1"])
                      stq(DV["pool_p"], kvf[113:128, 0:1024], "st_misc", ["kvf0", "kvf1"], ["out_pool_p"])
                  for c in range(8):
                      g = c // 2
                      b = c // 4
                      o = ps[:, b, (c % 4) * 128:(c % 4 + 1) * 128]
                      if i == 0:
                          mm(o, ubf[nb][:, c * 128:(c + 1) * 128], band[:, g * 3 + 2, :], True, True, ["ubf%d" % nb, "band"], ["ps%d" % b])
                      else:
                          mm(o, ubf[nb][:, c * 128:(c + 1) * 128], band[:, g * 3 + 0, :], True, False, ["ubf%d" % nb, "band"], ["ps%d" % b])
                          mm(o, ubf[1 - nb][:, c * 128:(c + 1) * 128], band[:, g * 3 + 1, :], False, True, ["ubf%d" % (1 - nb), "band"], ["ps%d" % b])
                  dTb = nT[:, :, 0:128]
                  acopy(dTb, ps[:, 0:2, :].rearrange("p b (c t) -> p (b c) t", t=128), ["ps0", "ps1"], ["nT0"])
                  for g in range(4):
                      b = 2 + g // 2
                      for kc in range(2):
                          mm(ps[:, b, (g % 2) * 256:(g % 2 + 1) * 256], dTb[:, 2 * g + kc, :], poolW[:, g, kc, :], kc == 0, kc == 1,
                             ["nT0", "poolW"], ["ps%d" % b])
                  tt("dve", h[:, s, :].rearrange("p (a n) -> p a n", n=512), ps[:, 2:4, :], h[:, s, :].rearrange("p (a n) -> p a n", n=512), ALU.add,
                     ["ps2", "ps3", "h%d" % s], ["h%d" % s])
              chk("l0")
              mlp(0, 4)
              chk("mlp0")
              S.alias(AT_KEYS, CX_KEYS)
              for s in range(4):
                  norm_T(s, s, s % 2)
              kvW = []
              for s in range(4):
                  i = 4 * T + s
                  for j in range(3):
                      if s == 0:
                          kvW.append(wload(CH_KV(j)))
                      W, wk = kvW[j]
                      Wv = W[:, :].rearrange("p (kc n) -> p kc n", n=512)
                      for kc in range(8):
                          mm(P(j), nT[:, kc, s * 128:(s + 1) * 128], Wv[:, kc, :], kc == 0, kc == 7, [wk, "nT%d" % s], ["ps%d" % j])
                      cp("act" if j != 1 else "dve", kvf[:, j * 512:(j + 1) * 512], P(j), ["ps%d" % j], ["kvf%d" % j])
                  kv5 = kvf[:, :].rearrange("p (br e g d) -> p br e g d", br=3, e=2, g=4)
                  x1 = kv5[:, :, 0, :, 0:8]
                  x2 = kv5[:, :, 0, :, 8:16]
                  cosb = cos_t[:, i, :].unsqueeze(1).unsqueeze(1).to_broadcast([128, 3, 4, 8])
                  sinb = sin_t[:, i, :].unsqueeze(1).unsqueeze(1).to_broadcast([128, 3, 4, 8])
                  rope(x1, x2, cosb, sinb, (3, 4), ["kvf0", "kvf1", "kvf2", "cos", "sin"], ["kvf0", "kvf1", "kvf2"])
                  chk("kv1")
                  stq(DV["cmp_p"][i * 128:(i + 1) * 128, :], kvf[:, 0:512], "st_kv0", ["kvf0"], ["out_cmp_p"])
                  stq(DV["sel_p"][i * 128:(i + 1) * 128, :], kvf[:, 512:1024], "st_kv1", ["kvf1"], ["out_sel_p"])
                  if i >= NSUB - 4:
                      ii = i - (NSUB - 4)
                      stq(DV["win_p"][ii * 128:(ii + 1) * 128, :], kvf[:, 1024:1536], "st_kv2", ["kvf2"], ["out_win_p"])
                  chk("kv2")
                  cp("pool", kvb[:], kvf[:], ["kvf0", "kvf1", "kvf2"], ["kvb"])
                  kb5 = kvb[:, :].rearrange("p (br e g d) -> p br e g d", br=3, e=2, g=4)
                  for g in range(4):
                      tr(Pb(5)[0:64, g * 128:(g + 1) * 128], kb5[:, 1, 0, g, :], ["kvb"], ["ps5"])
                  for g in range(4):
                      tr(Pb(5)[0:64, (4 + g) * 128:(5 + g) * 128], kb5[:, 2, 0, g, :], ["kvb"], ["ps5"])
                  chk("kv2b")
                  acopy(selKT[0:64, :, i * 128:(i + 1) * 128], Pb(5)[0:64, 0:512].rearrange("p (g t) -> p g t", t=128), ["ps5"], ["selKT%d" % i])
                  wsl = i % 8
                  cp("act", winKT[0:64, :, wsl * 128:(wsl + 1) * 128], Pb(5)[0:64, 512:1024].rearrange("p (g t) -> p g t", t=128), ["ps5"], ["winKT%d" % wsl])
                  chk("kv2c")
                  cp("act", selV[:, i, :, 0:64], kb5[:, 1, 1, :, :], ["kvb", "selV_ones"], ["selV%d" % i])
                  cp("act", winV[:, wsl, :, 0:64], kb5[:, 2, 1, :, :], ["kvb", "winV_ones"], ["winV%d" % wsl])
                  chk("kv3")
                  for e_ in range(2):
                      for g in range(4):
                          tr(Pb(6)[0:64, (e_ * 4 + g) * 128:(e_ * 4 + g + 1) * 128], kb5[:, 0, e_, g, :], ["kvb"], ["ps6"])
                  acopy(cmpXT[0:64, :, :, s * 128:(s + 1) * 128], Pb(6)[0:64, :].rearrange("p (e g t) -> p e g t", e=2, g=4), ["ps6"], ["cmpXT%d" % s])
              chk("kv4")
              for e_ in range(2):
                  W, wk = wload(CH_C1(e_), nparts=64)
                  for p_ in range(2):
                      for r_ in range(16):
                          row = p_ * 16 + r_
                          rhs = cmpXT[0:64, e_, :, :].rearrange("p g (sg r) -> p g sg r", r=16)[:, :, :, r_]
                          mm(ps[:, 7, (e_ * 2 + p_) * 128:(e_ * 2 + p_ + 1) * 128].rearrange("p (g sg) -> p g sg", g=4),
                             W[0:64, row * 128:(row + 1) * 128], rhs, r_ == 0, r_ == 15,
                             [wk] + ["cmpXT%d" % s for s in range(4)], ["ps7"])
              cp("dve", parts[:].rearrange("p e q g sg -> p (e q g sg)"), P(7), ["ps7"], ["parts"])
              chk("kv5")
              tt("dve", hid[:, :, :, 1:32], parts[:, :, 0, :, 0:31], parts[:, :, 1, :, 1:32], ALU.add, ["parts"], ["hid"])
              tt("dve", hid[:, :, :, 0:1], carry[:].unsqueeze(3), parts[:, :, 1, :, 0:1], ALU.add, ["parts", "carry"], ["hid"])
              cp("dve", carry[:].unsqueeze(3), parts[:, :, 0, :, 31:32], ["parts", "hid"], ["carry"])
              for e_ in range(2):
                  ts("dve", hid[:, e_], hid[:, e_], petot[:, e_:e_ + 1], None, ALU.add, None, ["hid", "petot"], ["hid"])
              hf_ = hid[:].rearrange("p e g s -> p (e g s)")
              h2_ = hid2[:].rearrange("p e g s -> p (e g s)")
              tt("dve", h2_, hf_, hf_, ALU.mult, ["hid"], ["hid2"])
              ts("dve", h2_, h2_, 0.044715, 1.0, ALU.mult, ALU.add, ["hid2"], ["hid2"])
              tt("dve", h2_, h2_, hf_, ALU.mult, ["hid2", "hid"], ["hid2"])
              act(h2_, h2_, AF.Exp, ["hid2"], ["hid2"], scale=-1.5957691216)
              ts("dve", h2_, h2_, 1.0, None, ALU.add, None, ["hid2"], ["hid2"])
              S.op("dve", lambda e, a=h2_: e.reciprocal(out=a, in_=a), ["hid2"], ["hid2"])
              tt("dve", hidb[:].rearrange("p e g s -> p (e g s)"), h2_, hf_, ALU.mult, ["hid2", "hid"], ["hidb"])
              chk("kv6")
              mm(ps[0:64, 7, 0:128], w2bf[:, 0, :], hidb[:, 0].rearrange("p g s -> p (g s)"), True, True, ["w2bf", "hidb"], ["ps7"])
              cp("dve", cmpKT[0:64, :, 32 * T:32 * T + 32], ps[0:64, 7, 0:128].rearrange("p (g s) -> p g s", g=4), ["ps7", "cmpKT_init"], ["cmpKT%d" % T])
              pq = 32 * (T % 3)
              for g in range(4):
                  mm(ps[pq:pq + 32, 7, 128 + g * 64:128 + (g + 1) * 64], hidb[:, 1, g, :], w2bf[:, 1, :], True, True, ["w2bf", "hidb"], ["ps7"])
              cp("dve", cmpV[pq:pq + 32, T // 3, :, 0:64], ps[pq:pq + 32, 7, 128:384].rearrange("p (g d) -> p g d", g=4), ["ps7", "cmpV_init"], ["cmpV%d" % T])
              memset("dve", cmpV[pq:pq + 32, T // 3, :, 64:65], 1.0, ["cmpV%d" % T])
              if T == 0:
                  memset("dve", cmpV[0:1, 0, :, :], 0.0, ["cmpV0"])

              chk("kv")
              S.alias(AT_KEYS + CX_KEYS, ATT_KEYS)
              for s in range(4):
                  norm_T(s, s, s % 2)
              qW = []
              for s in range(4):
                  i = 4 * T + s
                  for j in range(3):
                      if s == 0:
                          qW.append(wload(CH_QG(j), width=4096 if j < 2 else 384))
                      W, wk = qW[j]
                      if j < 2:
                          Wv = W[:, :].rearrange("p (kc n) -> p kc n", n=512)
                          for kc in range(8):
                              mm(P(j), nT[:, kc, s * 128:(s + 1) * 128], Wv[:, kc, :], kc == 0, kc == 7, [wk, "nT%d" % s], ["ps%d" % j])
                          cp("act" if j == 0 else "dve", kvf[:, j * 512:(j + 1) * 512], P(j), ["ps%d" % j], ["kvf%d" % j])
                      else:
                          Wv = W[:, 0:384].rearrange("p (kc n) -> p kc n", n=48)
                          for kc in range(8):
                              mm(P(2)[:, 0:48], nT[:, kc, s * 128:(s + 1) * 128], Wv[:, kc, :], kc == 0, kc == 7, [wk, "nT%d" % s], ["ps2"])
                          act(gate[:, s, :], P(2)[:, 0:48], AF.Exp, ["ps2"], ["gate%d" % s], scale=-1.0)
                          ts("dve", gate[:, s, :], gate[:, s, :], 1.0, None, ALU.add, None, ["gate%d" % s], ["gate%d" % s])
                          S.op("dve", lambda e, s=s: e.reciprocal(out=gate[:, s, :], in_=gate[:, s, :]), ["gate%d" % s], ["gate%d" % s])
                  q3 = kvf[:, 0:1024].rearrange("p (hh d) -> p hh d", d=64)
                  x1 = q3[:, :, 0:8].unsqueeze(1)
                  x2 = q3[:, :, 8:16].unsqueeze(1)
                  cosb = cos_t[:, i, :].unsqueeze(1).unsqueeze(1).to_broadcast([128, 1, 16, 8])
                  sinb = sin_t[:, i, :].unsqueeze(1).unsqueeze(1).to_broadcast([128, 1, 16, 8])
                  rope(x1, x2, cosb, sinb, (1, 16), ["kvf0", "kvf1", "cos", "sin"], ["kvf0", "kvf1"])
                  cp("pool", q_bf[:, s, :], kvf[:, 0:1024], ["kvf0", "kvf1"], ["q_bf%d" % s])

              chk("qg")
              for s in range(4):
                  i = 4 * T + s
                  qb = i % 2
                  qTt = qT[qb]
                  qk, qmk = "qT%d" % qb, "qTm%d" % qb
                  for hh in range(16):
                      bnk = 4 + hh // 8
                      tr(Pb(bnk)[0:64, (hh % 8) * 128:(hh % 8 + 1) * 128], q_bf[:, s, hh * 64:(hh + 1) * 64], ["q_bf%d" % s], ["ps%d" % bnk])
                  acopy(qTt[0:64, 0:8, :], Pb(4)[0:64, :].rearrange("p (hh t) -> p hh t", t=128), ["ps4"], [qk])
                  cp("act", qTt[0:64, 8:16, :], Pb(5)[0:64, :].rearrange("p (hh t) -> p hh t", t=128), ["ps5"], [qk])
                  ncm = 8 * i + 8
                  def chain(g):
                      hs = slice(4 * g, 4 * g + 4)
                      if i >= 8:
                          for hh in range(4):
                              bnk = hh // 2
                              mm(ps[:, bnk, (hh % 2) * 256:(hh % 2) * 256 + ncm], qTt[0:64, 4 * g + hh, :], cmpKT[0:64, g, 0:ncm], True, True,
                                 [qk] + ["cmpKT%d" % t for t in range(T + 1)], ["ps%d" % bnk])
                          sc4 = ps[:, 0:2, :].rearrange("p b (hh m) -> p (b hh) m", m=256)
                          tt("dve", sc4[:, :, ncm - 8:ncm], sc4[:, :, ncm - 8:ncm], cmq[:, :].unsqueeze(1).to_broadcast([128, 4, 8]), ALU.add,
                             ["ps0", "ps1", "cmq"], ["ps0", "ps1"])
                          ts("dve", sc4[:, :, 0:1], sc4[:, :, 0:1], NEG, None, ALU.add, None, ["ps0", "ps1"], ["ps0", "ps1"])
                          for hh in range(4):
                              act(Ebuf[:, hh, 0:ncm], sc4[:, hh, 0:ncm], AF.Exp, ["ps0", "ps1"], ["E", "esum"], accum_out=esum[:, hh:hh + 1])
                          ts("dve", esum[:, 0:4], esum[:, 0:4], 1e-30, None, ALU.max, None, ["esum"], ["esum"])
                          S.op("dve", lambda e: e.reciprocal(out=esum[:, 4:8], in_=esum[:, 0:4]), ["esum"], ["esum"])
                          ts("dve", imp[:, 0:ncm], Ebuf[:, 0, 0:ncm], esum[:, 4:5], None, ALU.mult, None, ["E", "esum"], ["imp"])
                          for hh in range(1, 4):
                              stt(imp[:, 0:ncm], Ebuf[:, hh, 0:ncm], esum[:, 4 + hh:5 + hh], imp[:, 0:ncm], ALU.mult, ALU.add, ["E", "esum", "imp"], ["imp"])
                          A = imp[:, 0:256].rearrange("p (j t) -> p j t", t=4)
                          B = imp[:, 4:260].rearrange("p (j t) -> p j t", t=4)
                          tt("dve", score[:], A[:, :, 1], A[:, :, 2], ALU.add, ["imp"], ["score"])
                          tt("dve", score[:], score[:], A[:, :, 3], ALU.add, ["imp", "score"], ["score"])
                          stt(score[:], score[:], 2.0, A[:, :, 0], ALU.mult, ALU.add, ["imp", "score"], ["score"])
                          tt("dve", score[:], score[:], B[:, :, 0], ALU.add, ["imp", "score"], ["score"])
                          tt("dve", score[:], score[:], gpat[:, 64 - 2 * i:128 - 2 * i], ALU.add, ["score", "gpat"], ["score"])
                          memset("dve", score[:, 0:1], 100.0, ["score"])
                          S.op("dve", lambda e: e.max(out=mx[:, 0:8], in_=score[:]), ["score"], ["mx"])
                          S.op("dve", lambda e: e.match_replace(out=score2[:], in_to_replace=mx[:, 0:8], in_values=score[:], imm_value=-1e9), ["score", "mx"], ["score2"])
                          S.op("dve", lambda e: e.max(out=mx[:, 8:16], in_=score2[:]), ["score2"], ["mx"])
                          ts("dve", mbfull[:, 64:128], score[:], mx[:, 15:16], 1.0, ALU.is_ge, ALU.subtract, ["score", "mx"], ["mbfull"])
                          tr(Pb(7)[:, 0:128], mbfull[:], ["mbfull"], ["ps7"])
                          S.op("act", lambda e, qTt=qTt, hs=hs: e.mul(out=qTt[64:128, hs, :], in_=Pb(7)[64:128, 0:128].unsqueeze(1).to_broadcast([64, 4, 128]), mul=-NEG),
                               ["ps7"], [qmk + "_%d" % g])
                      else:
                          memset("dve", qTt[64:128, hs, :], 0.0, [qmk + "_%d" % g])
                  def attn(g):
                      hs = slice(4 * g, 4 * g + 4)
                      rhs_aug = qTt[:, hs, :]
                      rhs_q = qTt[0:64, hs, :]
                      rk = [qk, qmk + "_%d" % g]
                      pend = []

                      def branch(obank, ktiles, nk=128):
                          n = len(ktiles)
                          for idx, (lhsT, lk, mrhs, mlhs, Vap, vk, aug) in enumerate(ktiles):
                              sb_ = SBANK[pti[0] % 3]
                              pt_i = pti[0] % 3
                              pti[0] += 1
                              mm(P(sb_)[0:nk], lhsT, rhs_aug if aug else rhs_q, True, mrhs is None, lk + rk, ["ps%d" % sb_])
                              if mrhs is not None:
                                  mm(P(sb_)[0:nk], mlhs, mrhs, False, True, ["ident", "ident4", "tri", "tri2", "cmpat"], ["ps%d" % sb_])
                              act(PT[pt_i][0:nk], P(sb_)[0:nk], AF.Exp, ["ps%d" % sb_], ["PT%d" % pt_i])

                              def pv(pt_i=pt_i, Vap=Vap, vk=vk, idx=idx, n=n, obank=obank, nk=nk):
                                  for hh in range(4):
                                      mm(ps[:, obank, hh * 65:(hh + 1) * 65], PT[pt_i][0:nk, hh * 128:(hh + 1) * 128], Vap, (idx == 0 and hh == 0), (idx == n - 1),
                                         ["PT%d" % pt_i] + vk, ["ps%d" % obank])
                              pend.append(pv)
                              if len(pend) > 2:
                                  pend.pop(0)()

                      def flush():
                          while pend:
                              pend.pop(0)()

                      kt = []
                      for t in range(3):
                          if 96 * t < ncm:
                              off = 96 * t - 8 * i + 248
                              kt.append((cmpKT[0:64, g, t * 96:(t + 1) * 96], ["cmpKT%d" % tt_ for tt_ in range(T + 1)] + ["cmpKT_init"],
                                         ident4[:], cmpat[:, off:off + 96], cmpV[0:96, t, g, :], ["cmpV%d" % tt_ for tt_ in range(T + 1)] + ["cmpV_init"], False))
                      branch(4, kt, nk=96)
                      kt = []
                      for j in range(i + 1):
                          kt.append((selKT[:, g, j * 128:(j + 1) * 128], ["selKT%d" % j, "selKT_ind"],
                                     tri[:] if j == i else None, ident[:], selV[:, j, g, :], ["selV%d" % j], True))
                      branch(5, kt)
                      kt = []
                      for j in range(max(0, i - 4), i + 1):
                          m_ = tri[:] if j == i else (tri2[:] if j == i - 4 else None)
                          kt.append((winKT[0:64, g, (j % 8) * 128:(j % 8 + 1) * 128], ["winKT%d" % (j % 8)],
                                     m_, ident[:], winV[:, j % 8, g, :], ["winV%d" % (j % 8)], False))
                      branch(6, kt)
                      flush()
                      for br in range(3):
                          cp("dve", den[:, br, :], ps[:, 4 + br, 0:260].rearrange("p (hh d) -> p hh d", d=65)[:, :, 64], ["ps%d" % (4 + br)], ["den"])
                      ts("dve", den[:], den[:], 1e-30, None, ALU.max, None, ["den"], ["den"])
                      S.op("dve", lambda e: e.reciprocal(out=den[:], in_=den[:]), ["den"], ["den"])
                      gv = gate[:, s, 12 * g:12 * g + 12].rearrange("p (hh br) -> p br hh", br=3)
                      tt("dve", wgt[:], den[:], gv, ALU.mult, ["den", "gate%d" % s], ["wgt"])
                      for br in range(3):
                          tt("dve", ocomb[:, br], ps[:, 4 + br, 0:260].rearrange("p (hh d) -> p hh d", d=65)[:, :, 0:64],
                             wgt[:, br, :].unsqueeze(2).to_broadcast([128, 4, 64]), ALU.mult, ["ps%d" % (4 + br), "wgt"], ["ocomb%d" % br])
                      tt("pool", ocomb[:, 0], ocomb[:, 0], ocomb[:, 1], ALU.add, ["ocomb0", "ocomb1"], ["ocomb0"])
                      tt("pool", o_bf[:, 256 * g:256 * (g + 1)].rearrange("p (hh d) -> p hh d", d=64), ocomb[:, 0], ocomb[:, 2], ALU.add,
                         ["ocomb0", "ocomb2"], ["o_bf"])
                  chain(0)
                  for g in range(4):
                      if g < 3:
                          chain(g + 1)
                      attn(g)
                  for kc in range(8):
                      tr(Pb(7)[:, kc * 128:(kc + 1) * 128], o_bf[:, kc * 128:(kc + 1) * 128], ["o_bf"], ["ps7"])
                  acopy(oT[:, :, s * 128:(s + 1) * 128], Pb(7).rearrange("p (k t) -> p k t", t=128), ["ps7"], ["oT%d" % s])
              chk("attn")
              oW = []
              for s in range(4):
                  for j in range(2):
                      if s == 0:
                          oW.append(wload(CH_O(j)))
                      W, wk = oW[j]
                      Wv = W[:, :].rearrange("p (kc n) -> p kc n", n=512)
                      b = (2 * s + j) % 4
                      for kc in range(8):
                          mm(P(b), oT[:, kc, s * 128:(s + 1) * 128], Wv[:, kc, :], kc == 0, kc == 7, [wk, "oT%d" % s], ["ps%d" % b])
                      tt("dve", h[:, s, j * 512:(j + 1) * 512], P(b), h[:, s, j * 512:(j + 1) * 512], ALU.add, ["ps%d" % b, "h%d" % s], ["h%d" % s])
              chk("wo")
              mlp(1, 4)
              chk("mlp1")
              for s in range(4):
                  i = 4 * T + s
                  rms(h[:, s, :], ["h%d" % s], s)
                  stt(ysb, h[:, s, :], rstd[:, s:s + 1], gfbc[:], ALU.mult, ALU.mult, ["h%d" % s, "rstd%d" % s, "gfbc"], ["kvf0", "kvf1"])
                  stq(DV["y"][i * 128:(i + 1) * 128, :], ysb, "st_y", ["kvf0", "kvf1"], ["out_y"])


        def sample_phase(ss=0):
            S.barrier()
            DS = {k_: D[k_][ss * r_:(ss + 1) * r_] for k_, r_ in (("xs", 4), ("spool", 60), ("swin", 2048), ("ys", 4), ("cmp_s", 4), ("sel_s", 4), ("win_s", 2048), ("pool_s", 60))}
            hflat = h[:].rearrange("p a b -> p (a b)")
            E1 = hflat[:, 1024:2048]
            imp_s = hflat[:, 2048:3076]
            misc = hflat[:, 3076:4096]
            ptb_i = misc[:, 0:128].bitcast(I32)
            idxp = misc[:, 128:256].bitcast(I32)
            PTf = misc[:, 256:384]
            iota_f = misc[:, 384:512]
            jl4 = misc[:, 512:544].bitcast(BF16).rearrange("p (g t) -> p g t", g=4)
            cxs = big[:, 8192:12288].rearrange("p (e g t) -> p e g t", e=2, g=4)
            jb = misc[:, 576:640]
            ji = misc[:, 640:704].bitcast(I32)
            pgi = misc[:, 704:768].bitcast(I32)
            pgf_ = misc[:, 768:832]
            parf = misc[:, 832:896]
            rb = misc[:, 896:960].rearrange("p (g t) -> p g t", g=4)
            idxs_f = misc[:, 960:992].rearrange("p (g t) -> p g t", g=4)
            idxs_i = misc[:, 992:1020].bitcast(I32)
            Ef = Ebuf[:].rearrange("p a b -> p (a b)")
            score_s = Ef[:, 0:256]
            score2_s = Ef[:, 256:512]
            mx_s = Ef[:, 512:528]
            ix_s = Ef[:, 528:544].bitcast(U32)
            phys = Ef[:, 544:608].rearrange("p (g t) -> p g t", g=4)
            idxs_i = Ef[:, 608:640].bitcast(I32).rearrange("p (g t) -> p g t", g=4)
            es_s = Ef[:, 640:648]
            cmpKT_s = selKT[:, 0, :].rearrange("p (g m) -> p g m", g=4)
            selKT_s = selKT[:, 1, :].rearrange("p (g t) -> p g t", g=4)
            winKT_s = selKT[:, 2, 0:2048].rearrange("p (g t) -> p g t", g=4)
            KTnew = selKT[:, 2, 2048:3072].rearrange("p (b g t) -> p b g t", b=2, g=4)
            sv = selV[:].rearrange("p a g d -> p (a g d)")
            cmpV_s = sv[:, 0:2860].rearrange("p (t g d) -> p t g d", g=4, d=65)
            selV_s = sv[:, 2860:4940].rearrange("p (t g d) -> p t g d", g=4, d=65)
            winV_s = sv[:, 4940:5980].rearrange("p (t g d) -> p t g d", g=4, d=65)
            Vnew = sv[:, 5980:6500].rearrange("p (t g d) -> p t g d", g=4, d=65)
            ring = nT[:].rearrange("p a b -> p (a b)").bitcast(F32)
            ohs = wslot[2][:, :].bitcast(F32).rearrange("p (t q) -> p t q", q=128)
            o_acc = kvf[:, 0:1024]
            ring_i = [0]

            def ring_slot():
                i = ring_i[0] % 4
                ring_i[0] += 1
                return ring[:, i * 512:(i + 1) * 512], "ring%d" % i

            memset("dve", h[:, 0, :], 0.0, ["h0"])
            ld(h[0:4, 0, :], DS["xs"], "xld0", ["h0"])
            ld(iota_f, D["c_iota"], "c0", ["iota_f"])
            rms(h[:, 0, :], ["h0"], 0)
            us_f = kvf[:, 0:1024]
            stt(us_f, h[:, 0, :], rstd[:, 0:1], g0bc[:], ALU.mult, ALU.mult, ["h0", "rstd0", "g0bc"], ["kvf0", "kvf1"])
            uext = hflat[0:64, 1024:2048]
            for b in range(4):
                ld(hflat[16 * b:16 * b + 15, 1024:2048], DS["spool"][15 * b:15 * b + 15, :], "smp0", ["uext"])
                ld(hflat[16 * b + 15:16 * b + 16, 1024:2048], kvf[b:b + 1, 0:1024], "smp0", ["uext"], r=["kvf0", "kvf1"])
                stq(DS["pool_s"][15 * b:15 * b + 14, :], DS["spool"][15 * b + 1:15 * b + 15, :], "st_s", [], ["out_s"])
                stq(DS["pool_s"][15 * b + 14:15 * b + 15, :], kvf[b:b + 1, 0:1024], "st_s", ["kvf0", "kvf1"], ["out_s"])
            cp("dve", ubf[0][0:64, :], uext, ["uext"], ["ubf0"])
            for c in range(8):
                g = c // 2
                b_ = c // 4
                mm(ps[:, b_, (c % 4) * 128:(c % 4) * 128 + 4], ubf[0][0:64, c * 128:(c + 1) * 128], sband[0:64, g, :], True, True, ["ubf0", "sband"], ["ps%d" % b_])
            dTb = nT[:, :, 0:128]
            memset("dve", dTb, 0.0, ["nT0"])
            acopy(dTb[:, :, 0:4], ps[:, 0:2, :].rearrange("p b (c t) -> p (b c) t", t=128)[:, :, 0:4], ["ps0", "ps1"], ["nT0"])
            for g in range(4):
                b_ = 2 + g // 2
                for kc in range(2):
                    mm(ps[:, b_, (g % 2) * 256:(g % 2 + 1) * 256], dTb[:, 2 * g + kc, :], poolW[:, g, kc, :], kc == 0, kc == 1, ["nT0", "poolW"], ["ps%d" % b_])
            tt("dve", h[:, 0, :].rearrange("p (a n) -> p a n", n=512), ps[:, 2:4, :], h[:, 0, :].rearrange("p (a n) -> p a n", n=512), ALU.add,
               ["ps2", "ps3", "h0"], ["h0"])
            chk("s_l0")
            mlp(0, 1)
            chk("s_mlp0")
            S.alias(AT_KEYS, ATT_KEYS + CX_KEYS + ["cxs%d_%d" % (a_, q) for a_ in range(2) for q in range(4)])
            norm_T(0, 0, 0)
            for j in range(3):
                W, wk = wload(CH_KV(j))
                Wv = W[:, :].rearrange("p (kc n) -> p kc n", n=512)
                for kc in range(8):
                    mm(P(j), nT[:, kc, 0:128], Wv[:, kc, :], kc == 0, kc == 7, [wk, "nT0"], ["ps%d" % j])
                cp("act", kvf[:, j * 512:(j + 1) * 512], P(j), ["ps%d" % j], ["kvf%d" % j])
            kv5 = kvf[:, :].rearrange("p (br e g d) -> p br e g d", br=3, e=2, g=4)
            cosb = coss[:, :].unsqueeze(1).unsqueeze(1).to_broadcast([128, 3, 4, 8])
            sinb = sins[:, :].unsqueeze(1).unsqueeze(1).to_broadcast([128, 3, 4, 8])
            rope(kv5[:, :, 0, :, 0:8], kv5[:, :, 0, :, 8:16], cosb, sinb, (3, 4), ["kvf0", "kvf1", "kvf2", "coss", "sins"], ["kvf0", "kvf1", "kvf2"])
            stq(DS["cmp_s"], kvf[0:4, 0:512], "st_s", ["kvf0"], ["out_s"])
            stq(DS["sel_s"], kvf[0:4, 512:1024], "st_s", ["kvf1"], ["out_s"])
            stq(DS["win_s"].rearrange("(b r) c -> b r c", b=4)[:, 511, :], kvf[0:4, 1024:1536], "st_s", ["kvf2"], ["out_s"])
            stq(DS["win_s"].rearrange("(b r) c -> b r c", b=4)[:, 0:511, :], DS["swin"].rearrange("(b r) c -> b r c", b=4)[:, 1:512, :], "st_s", [], ["out_s"])
            cp("pool", kvb[:], kvf[:], ["kvf0", "kvf1", "kvf2"], ["kvb"])
            kb5 = kvb[:, :].rearrange("p (br e g d) -> p br e g d", br=3, e=2, g=4)
            for bi, br in enumerate((1, 2)):
                for g in range(4):
                    tr(Pb(5)[0:64, (bi * 4 + g) * 128:(bi * 4 + g + 1) * 128], kb5[:, br, 0, g, :], ["kvb"], ["ps5"])
            acopy(KTnew[0:64], Pb(5)[0:64, :].rearrange("p (b g t) -> p b g t", b=2, g=4), ["ps5"], ["KTnew"])
            vnew_all = ubf[1][:, 0:512].rearrange("p (b g d) -> p b g d", b=2, g=4)
            for bi, br in enumerate((1, 2)):
                cp("act", vnew_all[:, bi], kb5[:, br, 1, :, :], ["kvb"], ["vnew_all"])
            chk("s_kv")
            norm_T(0, 0, 0)
            for j in range(3):
                W, wk = wload(CH_QG(j), width=4096 if j < 2 else 384)
                if j < 2:
                    Wv = W[:, :].rearrange("p (kc n) -> p kc n", n=512)
                    for kc in range(8):
                        mm(P(j), nT[:, kc, 0:128], Wv[:, kc, :], kc == 0, kc == 7, [wk, "nT0"], ["ps%d" % j])
                    cp("act", kvf[:, j * 512:(j + 1) * 512], P(j), ["ps%d" % j], ["kvf%d" % j])
                else:
                    Wv = W[:, 0:384].rearrange("p (kc n) -> p kc n", n=48)
                    for kc in range(8):
                        mm(P(2)[:, 0:48], nT[:, kc, 0:128], Wv[:, kc, :], kc == 0, kc == 7, [wk, "nT0"], ["ps2"])
                    act(gate[:, 0, :], P(2)[:, 0:48], AF.Exp, ["ps2"], ["gate0"], scale=-1.0)
                    ts("dve", gate[:, 0, :], gate[:, 0, :], 1.0, None, ALU.add, None, ["gate0"], ["gate0"])
                    S.op("dve", lambda e: e.reciprocal(out=gate[:, 0, :], in_=gate[:, 0, :]), ["gate0"], ["gate0"])
            q3 = kvf[:, 0:1024].rearrange("p (hh d) -> p hh d", d=64)
            cosb = coss[:, :].unsqueeze(1).unsqueeze(1).to_broadcast([128, 1, 16, 8])
            sinb = sins[:, :].unsqueeze(1).unsqueeze(1).to_broadcast([128, 1, 16, 8])
            rope(q3[:, :, 0:8].unsqueeze(1), q3[:, :, 8:16].unsqueeze(1), cosb, sinb, (1, 16), ["kvf0", "kvf1", "coss", "sins"], ["kvf0", "kvf1"])
            q_bf_s = big[:, 14336:15360]
            cp("pool", q_bf_s, kvf[:, 0:1024], ["kvf0", "kvf1"], ["q_bf0"])
            qTt = qT[0]
            for hh in range(16):
                bnk = 4 + hh // 8
                tr(Pb(bnk)[0:64, (hh % 8) * 128:(hh % 8 + 1) * 128], q_bf_s[:, hh * 64:(hh + 1) * 64], ["q_bf0"], ["ps%d" % bnk])
            acopy(qTt[0:64, 0:8, :], Pb(4)[0:64, :].rearrange("p (hh t) -> p hh t", t=128), ["ps4"], ["qT0"])
            cp("act", qTt[0:64, 8:16, :], Pb(5)[0:64, :].rearrange("p (hh t) -> p hh t", t=128), ["ps5"], ["qT0"])
            chk("s_qg")
            memset("dve", o_acc, 0.0, ["kvf0", "kvf1"])
            memset("dve", sv[:, 0:6500], 0.0, ["sv"])
            memset("dve", selV_s[:, :, :, 64:65], 1.0, ["sv"])
            memset("dve", winV_s[:, :, :, 64:65], 1.0, ["sv"])
            memset("dve", imp_s, 0.0, ["imp_s"])

            pend_s = []

            def attend(obank, ktiles, rhs_q, rk):
                n = len(ktiles)
                for idx, (lhsT, lk, Vap, vk, nk) in enumerate(ktiles):
                    sb_ = SBANK[pti[0] % 3]
                    pt_i = pti[0] % 3
                    pti[0] += 1
                    mm(P(sb_)[0:nk], lhsT, rhs_q, True, True, lk + rk, ["ps%d" % sb_])
                    act(PT[pt_i][0:nk], P(sb_)[0:nk], AF.Exp, ["ps%d" % sb_], ["PT%d" % pt_i])

                    def pv(pt_i=pt_i, Vap=Vap, vk=vk, idx=idx, n=n, obank=obank, nk=nk):
                        for hh in range(4):
                            mm(ps[:, obank, hh * 65:(hh + 1) * 65], PT[pt_i][0:nk, hh * 128:(hh + 1) * 128], Vap, (idx == 0 and hh == 0), (idx == n - 1),
                               ["PT%d" % pt_i] + vk, ["ps%d" % obank])
                    pend_s.append(pv)
                    if len(pend_s) > 2:
                        pend_s.pop(0)()

            def flush_s():
                while pend_s:
                    pend_s.pop(0)()

            for b in range(4):
                ld(ptb_i, D["ptab"][0:1, ss * 512 + b * 128: ss * 512 + (b + 1) * 128].partition_broadcast(128), "smp1", ["ptb"])
                ts("dve", idxp, ptb_i, 128, pcol[:, 7:8], ALU.mult, ALU.add, ["ptb", "pcol"], ["idxp"])
                cp("dve", PTf, ptb_i, ["ptb"], ["PTf"])
                memset("dve", carry[:], 0.0, ["carry"])
                cxs_b = [big[:, 8192:12288].rearrange("p (e g t) -> p e g t", e=2, g=4), big[:, 0:4096].rearrange("p (e g t) -> p e g t", e=2, g=4)]
                parts_b = [parts, relu_t[0][:, :].rearrange("p (e q g sg) -> p e q g sg", e=2, q=2, g=4)]
                parts_k = ["parts", "relu0"]
                pbank = [7, 4]

                def stageA(Tq):
                    bf_ = Tq % 2
                    cx = cxs_b[bf_]
                    for s_ in range(4):
                        pg = 4 * Tq + s_
                        slot, sk = ring_slot()
                        S.dma("pool", lambda e, slot=slot, pg=pg: e.indirect_dma_start(out=slot, out_offset=None, in_=CC,
                              in_offset=bass.IndirectOffsetOnAxis(ap=idxp[:, pg:pg + 1], axis=0)), "g_" + sk, ["idxp", "CC"], [sk])
                        kslot = kvb[:, (pg % 2) * 512:(pg % 2 + 1) * 512]
                        kkey = "kvb" if pg % 2 == 0 else "kvb2"
                        tb = 6 if pg % 2 == 0 else 3
                        cp("dve", kslot, slot, [sk], [kkey])
                        kc5 = kslot.rearrange("p (e g d) -> p e g d", e=2, g=4)
                        for e_ in range(2):
                            for g in range(4):
                                tr(Pb(tb)[0:64, (e_ * 4 + g) * 128:(e_ * 4 + g + 1) * 128], kc5[:, e_, g, :], [kkey], ["ps%d" % tb])
                        acopy(cx[0:64, :, :, s_ * 128:(s_ + 1) * 128], Pb(tb)[0:64, :].rearrange("p (e g t) -> p e g t", e=2, g=4), ["ps%d" % tb], ["cxs%d_%d" % (bf_, s_)])
                    pb_ = pbank[bf_]
                    for e_ in range(2):
                        W, wk = Wc[e_]
                        for p_ in range(2):
                            for r_ in range(16):
                                row = p_ * 16 + r_
                                rhs = cx[0:64, e_, :, :].rearrange("p g (sg r) -> p g sg r", r=16)[:, :, :, r_]
                                mm(ps[:, pb_, (e_ * 2 + p_) * 128:(e_ * 2 + p_ + 1) * 128].rearrange("p (g sg) -> p g sg", g=4),
                                   W[0:64, row * 128:(row + 1) * 128], rhs, r_ == 0, r_ == 15, [wk] + ["cxs%d_%d" % (bf_, q) for q in range(4)], ["ps%d" % pb_])
                    cp("dve", parts_b[bf_][:].rearrange("p e q g sg -> p (e q g sg)") if bf_ == 0 else relu_t[0][:, :], P(pb_), ["ps%d" % pb_], [parts_k[bf_]])

                def stageB(Tq):
                    bf_ = Tq % 2
                    pt_ = parts_b[bf_]
                    pk_ = parts_k[bf_]
                    tt("dve", hid[:, :, :, 1:32], pt_[:, :, 0, :, 0:31], pt_[:, :, 1, :, 1:32], ALU.add, [pk_], ["hid"])
                    tt("dve", hid[:, :, :, 0:1], carry[:].unsqueeze(3), pt_[:, :, 1, :, 0:1], ALU.add, [pk_, "carry"], ["hid"])
                    cp("dve", carry[:].unsqueeze(3), pt_[:, :, 0, :, 31:32], [pk_, "hid"], ["carry"])
                    for e_ in range(2):
                        ts("dve", hid[:, e_], hid[:, e_], petot[:, e_:e_ + 1], None, ALU.add, None, ["hid", "petot"], ["hid"])
                    hf_ = hid[:].rearrange("p e g s -> p (e g s)")
                    h2_ = hid2[:].rearrange("p e g s -> p (e g s)")
                    tt("dve", h2_, hf_, hf_, ALU.mult, ["hid"], ["hid2"])
                    ts("dve", h2_, h2_, 0.044715, 1.0, ALU.mult, ALU.add, ["hid2"], ["hid2"])
                    tt("dve", h2_, h2_, hf_, ALU.mult, ["hid2", "hid"], ["hid2"])
                    act(h2_, h2_, AF.Exp, ["hid2"], ["hid2"], scale=-1.5957691216)
                    ts("dve", h2_, h2_, 1.0, None, ALU.add, None, ["hid2"], ["hid2"])
                    S.op("dve", lambda e, a=h2_: e.reciprocal(out=a, in_=a), ["hid2"], ["hid2"])
                    tt("dve", hidb[:].rearrange("p e g s -> p (e g s)"), h2_, hf_, ALU.mult, ["hid2", "hid"], ["hidb"])
                    mm(ps[0:64, 5, 0:128], w2bf[:, 0, :], hidb[:, 0].rearrange("p g s -> p (g s)"), True, True, ["w2bf", "hidb"], ["ps5"])
                    cp("dve", cmpKT_s[0:64, :, 32 * Tq:32 * Tq + 32], ps[0:64, 5, 0:128].rearrange("p (g s) -> p g s", g=4), ["ps5"], ["cmpKT_s"])
                    pq = 32 * (Tq % 3)
                    for g in range(4):
                        mm(ps[pq:pq + 32, 5, 128 + g * 64:128 + (g + 1) * 64], hidb[:, 1, g, :], w2bf[:, 1, :], True, True, ["w2bf", "hidb"], ["ps5"])
                    cp("dve", cmpV_s[pq:pq + 32, Tq // 3, :, 0:64], ps[pq:pq + 32, 5, 128:384].rearrange("p (g d) -> p g d", g=4), ["ps5", "sv"], ["cmpV_s"])
                    memset("dve", cmpV_s[pq:pq + 32, Tq // 3, :, 64:65], 1.0, ["cmpV_s"])
                    if Tq == 0:
                        memset("dve", cmpV_s[0:1, 0, :, :], 0.0, ["cmpV_s"])

                Wc = [wload(CH_C1(e_), nparts=64) for e_ in range(2)]
                stageA(0)
                for Tq in range(32):
                    if Tq + 1 < 32:
                        stageA(Tq + 1)
                    stageB(Tq)
                chk("s_cmp%d" % b)
                for t in range(4):
                    slot, sk = ring_slot()
                    ld(slot, DS["swin"][b * 512 + t * 128: b * 512 + (t + 1) * 128, :], "g_" + sk, [sk])
                    cp("dve", kvb[:, 512:1024], slot, [sk], ["kvb2"])
                    kw = kvb[:, 512:1024].rearrange("p (e g d) -> p e g d", e=2, g=4)
                    for g in range(4):
                        tr(Pb(5)[0:64, g * 128:(g + 1) * 128], kw[:, 0, g, :], ["kvb2"], ["ps5"])
                    acopy(winKT_s[0:64, :, t * 128:(t + 1) * 128], Pb(5)[0:64, 0:512].rearrange("p (g t) -> p g t", g=4), ["ps5"], ["winKT_s"])
                    cp("act", winV_s[:, t, :, 0:64], kw[:, 1, :, :], ["kvb2", "sv"], ["winV_s"])
                for bi in range(2):
                    ts("dve", Vnew[:, bi, :, 0:64], vnew_all[:, bi], pcol[:, 3 + b:4 + b], None, ALU.mult, None, ["vnew_all", "pcol", "sv"], ["Vnew"])
                    for g in range(4):
                        cp("dve", Vnew[:, bi, g, 64:65], pcol[:, 3 + b:4 + b], ["pcol", "sv"], ["Vnew"])
                for g in range(4):
                    rhs_q = qTt[0:64, 4 * g:4 * g + 4, :]
                    rk = ["qT0"]
                    for hh in range(4):
                        for half in range(2):
                            mm(P(half), qTt[0:64, 4 * g + hh, :], cmpKT_s[0:64, g, half * 512:(half + 1) * 512], True, True, ["qT0", "cmpKT_s"], ["ps%d" % half])
                        ts("dve", ps[:, 0, 0:1], ps[:, 0, 0:1], NEG, None, ALU.add, None, ["ps0"], ["ps0"])
                        act(E1.rearrange("p (a n) -> p a n", n=512), ps[:, 0:2, :], AF.Exp, ["ps0", "ps1"], ["E1", "es_s"], accum_out=es_s[:, 0:1])
                        S.op("dve", lambda e: e.reciprocal(out=es_s[:, 1:2], in_=es_s[:, 0:1]), ["es_s"], ["es_s"])
                        if hh == 0:
                            ts("dve", imp_s[:, 0:1024], E1, es_s[:, 1:2], None, ALU.mult, None, ["E1", "es_s"], ["imp_s"])
                        else:
                            stt(imp_s[:, 0:1024], E1, es_s[:, 1:2], imp_s[:, 0:1024], ALU.mult, ALU.add, ["E1", "es_s", "imp_s"], ["imp_s"])
                    A = imp_s[:, 0:1024].rearrange("p (j t) -> p j t", t=4)
                    B = imp_s[:, 4:1028].rearrange("p (j t) -> p j t", t=4)
                    tt("dve", score_s, A[:, :, 1], A[:, :, 2], ALU.add, ["imp_s"], ["score_s"])
                    tt("dve", score_s, score_s, A[:, :, 3], ALU.add, ["imp_s", "score_s"], ["score_s"])
                    stt(score_s, score_s, 2.0, A[:, :, 0], ALU.mult, ALU.add, ["imp_s", "score_s"], ["score_s"])
                    tt("dve", score_s, score_s, B[:, :, 0], ALU.add, ["imp_s", "score_s"], ["score_s"])
                    memset("dve", score_s[:, 0:1], -1.0, ["score_s"])
                    memset("dve", score_s[:, 255:256], -1.0, ["score_s"])
                    S.op("dve", lambda e: e.max(out=mx_s[:, 0:8], in_=score_s), ["score_s"], ["mx_s"])
                    S.op("dve", lambda e: e.max_index(out=ix_s[:, 0:8], in_max=mx_s[:, 0:8], in_values=score_s), ["score_s", "mx_s"], ["ix_s"])
                    S.op("dve", lambda e: e.match_replace(out=score2_s, in_to_replace=mx_s[:, 0:8], in_values=score_s, imm_value=-1e9), ["score_s", "mx_s"], ["score2_s"])
                    S.op("dve", lambda e: e.max(out=mx_s[:, 8:16], in_=score2_s), ["score2_s"], ["mx_s"])
                    S.op("dve", lambda e: e.max_index(out=ix_s[:, 8:16], in_max=mx_s[:, 8:16], in_values=score2_s), ["score2_s", "mx_s"], ["ix_s"])
                    cp("dve", jl4[:, g, 0:13], ix_s[:, 0:13], ["ix_s"], ["jl4"])
                    memset("dve", jl4[:, g, 13:14], 0.0, ["jl4"])
                    memset("dve", jl4[:, g, 14:15], 255.0, ["jl4"])
                    memset("dve", jl4[:, g, 15:16], 0.0, ["jl4"])
                mm(ps[:, 7, 0:64], rowsel[:, b, :], jl4.rearrange("p g t -> p (g t)"), True, True, ["rowsel", "jl4"], ["ps7"])
                cp("dve", jb, ps[:, 7, 0:64], ["ps7"], ["jb"])
                cp("dve", ji, jb, ["jb"], ["ji"])
                S.op("dve", lambda e: e.tensor_scalar(out=pgi, in0=ji, scalar1=1, scalar2=None, op0=ALU.arith_shift_right), ["ji"], ["pgi"])
                cp("dve", pgf_, pgi, ["pgi"], ["pgf"])
                stt(parf, pgf_, -2.0, jb, ALU.mult, ALU.add, ["pgf", "jb"], ["parf"])
                for g in range(4):
                    tt("dve", ohs, iota_f.unsqueeze(1).to_broadcast([128, 16, 128]),
                       pgf_[:, g * 16:(g + 1) * 16].unsqueeze(2).to_broadcast([128, 16, 128]), ALU.is_equal, ["iota_f", "pgf"], ["wslot2"])
                    tt("dve", ohs, ohs, PTf.unsqueeze(1).to_broadcast([128, 16, 128]), ALU.mult, ["wslot2", "PTf"], ["wslot2"])
                    S.op("dve", lambda e, g=g: e.tensor_reduce(out=phys[:, g, :], in_=ohs, axis=mybir.AxisListType.X, op=ALU.add), ["wslot2"], ["phys"])
                ts("dve", rb.rearrange("p g t -> p (g t)"), phys.rearrange("p g t -> p (g t)"), 128.0, None, ALU.mult, None, ["phys"], ["rb"])
                stt(rb.rearrange("p g t -> p (g t)"), parf, 64.0, rb.rearrange("p g t -> p (g t)"), ALU.mult, ALU.add, ["parf", "rb"], ["rb"])
                rbp = rb.rearrange("p g (pr two) -> p g pr two", two=2)
                ts("dve", idxs_f, rbp[:, :, :, 0], pcol[:, 0:1], pcol[:, 2:3], ALU.mult, ALU.add, ["rb", "pcol"], ["idxs_f"])
                stt(idxs_f, rbp[:, :, :, 1], pcol[:, 1:2], idxs_f, ALU.mult, ALU.add, ["rb", "pcol", "idxs_f"], ["idxs_f"])
                cp("dve", idxs_i, idxs_f, ["idxs_f"], ["idxs_i"])
                chk("s_idx%d" % b)
                for g in range(4):
                    for pr in range(8):
                        slot, sk = ring_slot()
                        S.dma("pool", lambda e, slot=slot, g=g, pr=pr: e.indirect_dma_start(out=slot, out_offset=None, in_=CS,
                              in_offset=bass.IndirectOffsetOnAxis(ap=idxs_i[:, g, pr:pr + 1], axis=0)), "g_" + sk, ["idxs_i", "CS"], [sk])
                        s5 = slot.rearrange("p (e g d) -> p e g d", e=2, g=4)
                        cp("dve", kvb[:, 1024:1152].rearrange("p (e d) -> p e d", e=2), s5[:, :, g, :], [sk], ["kvb3"])
                        tr(Pb(5)[0:64, 512 + (pr % 4) * 128: 512 + (pr % 4 + 1) * 128], kvb[:, 1024:1088], ["kvb3"], ["ps5"])
                        acopy(selKT_s[0:64, g, pr * 128:(pr + 1) * 128], Pb(5)[0:64, 512 + (pr % 4) * 128: 512 + (pr % 4 + 1) * 128], ["ps5"], ["selKT_s"])
                        cp("act", selV_s[:, pr, g, 0:64], kvb[:, 1088:1152], ["kvb3", "sv"], ["selV_s"])
                    memset("dve", selV_s[64:128, 7, g, :], 0.0, ["selV_s"])
                    rhs_q = qTt[0:64, 4 * g:4 * g + 4, :]
                    rk = ["qT0"]
                    kt = [(cmpKT_s[0:64, g, 96 * t:96 * t + (96 if t < 10 else 64)], ["cmpKT_s"], cmpV_s[0:(96 if t < 10 else 64), t, g, :], ["cmpV_s", "sv"],
                           (96 if t < 10 else 64)) for t in range(11)]
                    attend(4, kt, rhs_q, rk)
                    kt = [(selKT_s[0:64, g, pr * 128:(pr + 1) * 128], ["selKT_s"], selV_s[:, pr, g, :], ["selV_s", "sv"], 128) for pr in range(8)]
                    kt.append((KTnew[0:64, 0, g, :], ["KTnew"], Vnew[:, 0, g, :], ["Vnew"], 128))
                    attend(5, kt, rhs_q, rk)
                    kt = [(winKT_s[0:64, g, t * 128:(t + 1) * 128], ["winKT_s"], winV_s[:, t, g, :], ["winV_s", "sv"], 128) for t in range(4)]
                    kt.append((KTnew[0:64, 1, g, :], ["KTnew"], Vnew[:, 1, g, :], ["Vnew"], 128))
                    attend(6, kt, rhs_q, rk)
                    flush_s()
                    for br in range(3):
                        cp("dve", den[:, br, :], ps[:, 4 + br, 0:260].rearrange("p (hh d) -> p hh d", d=65)[:, :, 64], ["ps%d" % (4 + br)], ["den"])
                    ts("dve", den[:], den[:], 1e-30, None, ALU.max, None, ["den"], ["den"])
                    S.op("dve", lambda e: e.reciprocal(out=den[:], in_=den[:]), ["den"], ["den"])
                    gv = gate[:, 0, 12 * g:12 * g + 12].rearrange("p (hh br) -> p br hh", br=3)
                    tt("dve", wgt[:], den[:], gv, ALU.mult, ["den", "gate0"], ["wgt"])
                    ts("dve", wgt[:], wgt[:], pcol[:, 3 + b:4 + b], None, ALU.mult, None, ["wgt", "pcol"], ["wgt"])
                    for br in range(3):
                        tt("dve", ocomb[:, br], ps[:, 4 + br, 0:260].rearrange("p (hh d) -> p hh d", d=65)[:, :, 0:64],
                           wgt[:, br, :].unsqueeze(2).to_broadcast([128, 4, 64]), ALU.mult, ["ps%d" % (4 + br), "wgt"], ["ocomb%d" % br])
                    tt("pool", ocomb[:, 0], ocomb[:, 0], ocomb[:, 1], ALU.add, ["ocomb0", "ocomb1"], ["ocomb0"])
                    tt("pool", ocomb[:, 0], ocomb[:, 0], ocomb[:, 2], ALU.add, ["ocomb0", "ocomb2"], ["ocomb0"])
                    oa = o_acc[:, 256 * g:256 * (g + 1)].rearrange("p (hh d) -> p hh d", d=64)
                    tt("pool", oa, oa, ocomb[:, 0], ALU.add, ["ocomb0", "kvf0", "kvf1"], ["kvf0", "kvf1"])
                chk("s_att%d" % b)
            S.alias(["cxs%d_%d" % (a_, q) for a_ in range(2) for q in range(4)], ATT_KEYS)
            cp("pool", o_bf, o_acc, ["kvf0", "kvf1"], ["o_bf"])
            for kc in range(8):
                tr(Pb(7)[:, kc * 128:(kc + 1) * 128], o_bf[:, kc * 128:(kc + 1) * 128], ["o_bf"], ["ps7"])
            acopy(oT[:, :, 0:128], Pb(7).rearrange("p (k t) -> p k t", t=128), ["ps7"], ["oT0"])
            for j in range(2):
                W, wk = wload(CH_O(j))
                Wv = W[:, :].rearrange("p (kc n) -> p kc n", n=512)
                for kc in range(8):
                    mm(P(j), oT[:, kc, 0:128], Wv[:, kc, :], kc == 0, kc == 7, [wk, "oT0"], ["ps%d" % j])
                tt("dve", h[:, 0, j * 512:(j + 1) * 512], P(j), h[:, 0, j * 512:(j + 1) * 512], ALU.add, ["ps%d" % j, "h0"], ["h0"])
            mlp(1, 1)
            rms(h[:, 0, :], ["h0"], 0)
            stt(ysb, h[:, 0, :], rstd[:, 0:1], gfbc[:], ALU.mult, ALU.mult, ["h0", "rstd0", "gfbc"], ["kvf0", "kvf1"])
            stq(DS["ys"], kvf[0:4, 0:1024], "st_s", ["kvf0", "kvf1"], ["out_s"])

        try:
            for sq_ in range(PSEQ):
                main_loop(sq_)
            if do_sample:
                for ss_ in range(PSEQ):
                    sample_phase(ss_)
        except _Stop:
            pass
        S.final_waits("pool", ["out_y", "out_cmp_p", "out_sel_p", "out_win_p", "out_pool_p", "out_s"])
        print("ops:", S.nops, {e: len(v) for e, v in S.ops.items()})
        S.emit(blk)
    return nc


_CACHE = {}


def kernel(**inputs):
    return run(inputs, gather="direct")


def run(inputs, n_tiles=NTILE, do_sample=True, stop=None, ncores=NC_USED, pool_pages=N_POOLPG, gather="allgather", trace=False):
    n = NC_USED
    consts = make_consts()
    ck = (n_tiles, do_sample, stop, pool_pages, gather)
    if ck not in _CACHE:
        _CACHE[ck] = build_program(n_tiles, do_sample, stop, pool_pages, gather)
    nc = _CACHE[ck]
    f = lambda a: np.ascontiguousarray(a)
    shared = {
        "norm_mix": f(inputs["norm_mix"]), "norm_ffn": f(inputs["norm_ffn"]), "pool_w": f(inputs["pool_w"][0]),
        "pool_scale": f(inputs["pool_scale"]).reshape(1, 1024), "w_qg": f(inputs["w_qg"][0]), "w_o": f(inputs["w_o"][0]),
        "norm_kv": f(inputs["norm_kv"]).reshape(1, 1024), "w_kv": f(inputs["w_kv"]), "cmp_pe": f(inputs["cmp_pe"]),
        "cmp_w1": f(inputs["cmp_w1"]), "cmp_w2": f(inputs["cmp_w2"]), "mlp_up": f(inputs["mlp_up"]),
        "mlp_down": f(inputs["mlp_down"]), "norm_final": f(inputs["norm_final"]).reshape(1, 1024),
    }
    if do_sample and "ccmp_list" not in inputs:
        ccmp = f(inputs["cache_cmp_kv"]).reshape(pool_pages * 128, 512)
        csel = f(inputs["cache_sel_kv"]).reshape(pool_pages * 128, 512)
        if gather == "direct":
            shared["ccmp"] = ccmp
            shared["csel"] = csel
    shared.update(consts)
    in_maps = []
    for c in range(n):
        m = dict(shared)
        ns_ = 4 * PSEQ
        m["x"] = f(inputs["x_prompt"][PSEQ * c:PSEQ * (c + 1)]).reshape(PSEQ * 4096, 1024)
        m["xs"] = f(inputs["x_sample"][ns_ * c:ns_ * (c + 1), 0, :])
        m["spool"] = f(inputs["state_pool"][0, ns_ * c:ns_ * (c + 1)]).reshape(15 * ns_, 1024)
        m["swin"] = f(inputs["state_win_kv"][ns_ * c:ns_ * (c + 1)]).reshape(512 * ns_, 512)
        m["ptab"] = f(inputs["page_table"][ns_ * c:ns_ * (c + 1)]).reshape(1, 128 * ns_).astype(np.int32)
        if do_sample and "ccmp_list" in inputs:
            m["ccmp"] = inputs["ccmp_list"][c]
            m["csel"] = inputs["csel_list"][c]
            m["ptab"] = inputs["ptab_list"][c]
        elif do_sample and gather != "direct":
            rs_ = pool_pages * 128 // 8
            m["ccmp"] = ccmp[c * rs_:(c + 1) * rs_]
            m["csel"] = csel[c * rs_:(c + 1) * rs_]
        in_maps.append(m)
    if trace:
        res = run_bass_kernel_spmd(nc, in_maps[:ncores], core_ids=list(range(ncores)), trace=True)
        print("EXEC_TIME_NS", getattr(res, "exec_time_ns", None), flush=True)
    else:
        res = run_bass_kernel_spmd(nc, in_maps[:ncores], core_ids=list(range(ncores)))
    R = list(res.results)
    while len(R) < n:
        R.append(R[0])
    cat = lambda k: np.stack([R[c][k] for c in range(n)], 0)
    y_prompt = cat("y").reshape(8, 4096, 1024)
    y_sample = cat("ys").reshape(32, 1, 1024)
    cmp_p = cat("cmp_p").reshape(8, 4096, 2, 4, 64)
    sel_p = cat("sel_p").reshape(8, 4096, 2, 4, 64)
    win_p = cat("win_p").reshape(8, 512, 2, 4, 64)
    pool_p = cat("pool_p").reshape(1, 8, 15, 1024)
    cmp_s = cat("cmp_s").reshape(32, 1, 2, 4, 64)
    sel_s = cat("sel_s").reshape(32, 1, 2, 4, 64)
    win_s = cat("win_s").reshape(32, 512, 2, 4, 64)
    pool_s = cat("pool_s").reshape(1, 32, 15, 1024)
    return (y_prompt, y_sample, cmp_p, sel_p, win_p, pool_p, cmp_s, sel_s, win_s, pool_s)
```
